# Optimizing a Trainium2 kernel written in Bass

```python
import jax
import jax.numpy as jnp
from jax import lax
import numpy as np

D_MODEL = 1024
BATCH = 2
SEQ = 8192
DEPTH = 1

NSA_HEADS = 8
NSA_KV_GROUPS = 2
NSA_HEAD_DIM = 64
CMP_LEN = 32
CMP_STRIDE = 16
CMP_HIDDEN = 128
SLC_LEN = 64
N_SELECT = 16
WINDOW = 512
Q_BLOCK = 128
FORCE_SCORE = 1e4
HGRN_HEADS = 4
HGRN_KEY_DIM = 128
HGRN_VAL_DIM = 128
HGRN_CHUNK = 64
D_FF = -(-8 * D_MODEL // (3 * 256)) * 256
RMS_EPS = 1e-6

NSA_WIDTH = NSA_HEADS * NSA_HEAD_DIM
NSA_KV_WIDTH = NSA_KV_GROUPS * NSA_HEAD_DIM
HGRN_KEY_WIDTH = HGRN_HEADS * HGRN_KEY_DIM
HGRN_VAL_WIDTH = HGRN_HEADS * HGRN_VAL_DIM
MIX_WIDTH = NSA_WIDTH + HGRN_VAL_WIDTH
IN_SIZES = (NSA_WIDTH,) + (NSA_KV_WIDTH,) * 6 + (3 * NSA_HEADS, HGRN_KEY_WIDTH, HGRN_KEY_WIDTH, HGRN_VAL_WIDTH, HGRN_VAL_WIDTH)
IN_WIDTH = sum(IN_SIZES)

kernel_name = 'nsa_hgrn2_hybrid_layer'


def rms_norm(x, g):
    xf = x.astype(jnp.float32)
    y = xf * lax.rsqrt(jnp.mean(xf * xf, axis=-1, keepdims=True) + RMS_EPS)
    return (y * g.astype(jnp.float32)).astype(x.dtype)


def masked_softmax(s, mask):
    s = jnp.where(mask, s.astype(jnp.float32), -1e30)
    p = jax.nn.softmax(s, axis=-1)
    return jnp.where(mask, p, 0.0)


def compress(seq, pos, w1, w2):
    B, T, G, dh = seq.shape
    ratio = CMP_LEN // CMP_STRIDE
    nseg = T // CMP_STRIDE
    ncmp = nseg - ratio + 1
    seg = seq.reshape(B, nseg, CMP_STRIDE, G, dh)
    blocks = jnp.concatenate([seg[:, r:r + ncmp] for r in range(ratio)], axis=2)
    blocks = blocks + pos[None, None, :, None, :]
    flat = blocks.transpose(0, 1, 3, 2, 4).reshape(B, ncmp, G, CMP_LEN * dh)
    return jax.nn.gelu(flat @ w1) @ w2


def nsa_mixer(q, kc, vc, ks, vs, kw, vw, gates):
    B, T, H, dh = q.shape
    G = NSA_KV_GROUPS
    R = H // G
    ncmp = kc.shape[1]
    nslc = T // SLC_LEN
    n_sel = min(N_SELECT, nslc)
    scale = dh ** -0.5
    cmp_start = jnp.arange(ncmp) * CMP_STRIDE
    cmp_end = cmp_start + CMP_LEN - 1
    slc_start = jnp.arange(nslc) * SLC_LEN
    overlap = ((cmp_start[:, None] < slc_start[None, :] + SLC_LEN)
               & (cmp_start[:, None] + CMP_LEN > slc_start[None, :])).astype(jnp.float32)
    ks_blk = ks.reshape(B, nslc, SLC_LEN, G, dh).transpose(0, 3, 1, 2, 4)
    vs_blk = vs.reshape(B, nslc, SLC_LEN, G, dh).transpose(0, 3, 1, 2, 4)
    pad = ((0, 0), (WINDOW, 0), (0, 0), (0, 0))
    kw_pad = jnp.pad(kw, pad)
    vw_pad = jnp.pad(vw, pad)
    bi = jnp.arange(B)[:, None, None, None]
    gi = jnp.arange(G)[None, :, None, None]
    j = jnp.arange(nslc)

    def block(qb):
        s = qb * Q_BLOCK
        t = s + jnp.arange(Q_BLOCK)
        qblk = lax.dynamic_slice_in_dim(q, s, Q_BLOCK, 1).reshape(B, Q_BLOCK, G, R, dh)
        sc = jnp.einsum('bqgrd,bngd->bgrqn', qblk, kc) * scale
        p_cmp = masked_softmax(sc, cmp_end[None, :] <= t[:, None])
        o_cmp = jnp.einsum('bgrqn,bngd->bqgrd', p_cmp, vc.astype(jnp.float32))
        imp = jnp.einsum('bgrqn,nj->bgqj', p_cmp, overlap)
        jcur = t // SLC_LEN
        forced = (j[None, :] == 0) | (j[None, :] == jcur[:, None]) | (j[None, :] == jcur[:, None] - 1)
        future = j[None, :] > jcur[:, None]
        imp = jnp.where(forced, FORCE_SCORE, jnp.where(future, -1.0, imp))
        _, idx = lax.top_k(imp, n_sel)
        k_sel = ks_blk[bi, gi, idx].reshape(B, G, Q_BLOCK, n_sel * SLC_LEN, dh)
        v_sel = vs_blk[bi, gi, idx].reshape(B, G, Q_BLOCK, n_sel * SLC_LEN, dh)
        pos_sel = (idx[..., None] * SLC_LEN + jnp.arange(SLC_LEN)).reshape(B, G, Q_BLOCK, n_sel * SLC_LEN)
        ss = jnp.einsum('bqgrd,bgqkd->bgrqk', qblk, k_sel) * scale
        p_sel = masked_softmax(ss, (pos_sel <= t[None, None, :, None])[:, :, None])
        o_sel = jnp.einsum('bgrqk,bgqkd->bqgrd', p_sel, v_sel.astype(jnp.float32))
        k_win = lax.dynamic_slice_in_dim(kw_pad, s, WINDOW + Q_BLOCK, 1)
        v_win = lax.dynamic_slice_in_dim(vw_pad, s, WINDOW + Q_BLOCK, 1)
        pos_win = s - WINDOW + jnp.arange(WINDOW + Q_BLOCK)
        diff = t[:, None] - pos_win[None, :]
        win_mask = (diff >= 0) & (diff < WINDOW) & (pos_win[None, :] >= 0)
        sw = jnp.einsum('bqgrd,bkgd->bgrqk', qblk, k_win) * scale
        p_win = masked_softmax(sw, win_mask)
        o_win = jnp.einsum('bgrqk,bkgd->bqgrd', p_win, v_win.astype(jnp.float32))
        g = lax.dynamic_slice_in_dim(gates, s, Q_BLOCK, 1).reshape(B, Q_BLOCK, G, R, 3)
        o = g[..., 0:1] * o_cmp + g[..., 1:2] * o_sel + g[..., 2:3] * o_win
        return o.reshape(B, Q_BLOCK, H * dh)

    out = lax.map(block, jnp.arange(T // Q_BLOCK))
    return out.transpose(1, 0, 2, 3).reshape(B, T, H * dh)


def hgrn2_mixer(hq, hf, hi, hg, lb, o_gain):
    B, T, _ = hq.shape
    H, dk, dv, C = HGRN_HEADS, HGRN_KEY_DIM, HGRN_VAL_DIM, HGRN_CHUNK
    f32 = jnp.float32
    q = jax.nn.silu(hq.astype(f32)).reshape(B, T, H, dk)
    f = lb + (1.0 - lb) * jax.nn.sigmoid(hf.astype(f32))
    logf = jnp.log(f).reshape(B, T, H, dk)
    k = (1.0 - f).reshape(B, T, H, dk)
    v = hi.astype(f32).reshape(B, T, H, dv)
    nc = T // C

    def to_chunks(a):
        return a.reshape(B, nc, C, H, a.shape[-1]).transpose(1, 0, 3, 2, 4)

    causal = jnp.tril(jnp.ones((C, C), dtype=bool))

    def step(S, xs):
        qc, kc, vc, gc = xs
        b = jnp.cumsum(gc, axis=-2)
        b_last = b[:, :, -1:, :]
        o_inter = jnp.einsum('bhcd,bhde->bhce', qc * jnp.exp(b), S)
        decay = jnp.exp(jnp.where(causal[:, :, None], b[:, :, :, None, :] - b[:, :, None, :, :], -jnp.inf))
        attn = jnp.einsum('bhtd,bhsd,bhtsd->bhts', qc, kc, decay)
        o = o_inter + jnp.einsum('bhts,bhse->bhte', attn, vc)
        S = jnp.exp(b_last[:, :, 0, :])[..., None] * S + jnp.einsum('bhsd,bhse->bhde', kc * jnp.exp(b_last - b), vc)
        return S, o

    S0 = jnp.zeros((B, H, dk, dv), f32)
    _, o = lax.scan(step, S0, (to_chunks(q), to_chunks(k), to_chunks(v), to_chunks(logf)))
    o = o.transpose(1, 0, 3, 2, 4).reshape(B, T, H, dv)
    o = rms_norm(o, o_gain) * jax.nn.silu(hg.astype(f32)).reshape(B, T, H, dv)
    return o.reshape(B, T, H * dv)


def setup_inputs(seed: int = 0) -> dict:
    key = jax.random.key(seed)
    k = jax.random.split(key, 17)
    f32 = jnp.float32
    dh = NSA_HEAD_DIM

    def nrm(kk, shape, scale):
        return jax.random.normal(kk, shape, f32) * scale

    return {
        'x': nrm(k[0], (BATCH, SEQ, D_MODEL), 1.0),
        'norm_mix': 1.0 + nrm(k[1], (DEPTH, D_MODEL), 0.02),
        'w_in': nrm(k[2], (DEPTH, D_MODEL, IN_WIDTH), D_MODEL ** -0.5),
        'q_norm': 1.0 + nrm(k[3], (DEPTH, dh), 0.02),
        'k_norm': 1.0 + nrm(k[4], (DEPTH, 3, dh), 0.02),
        'cmp_pos_k': nrm(k[5], (DEPTH, CMP_LEN, dh), 0.5),
        'cmp_pos_v': nrm(k[6], (DEPTH, CMP_LEN, dh), 0.5),
        'cmp_k_w1': nrm(k[7], (DEPTH, CMP_LEN * dh, CMP_HIDDEN), (CMP_LEN * dh) ** -0.5),
        'cmp_k_w2': nrm(k[8], (DEPTH, CMP_HIDDEN, dh), CMP_HIDDEN ** -0.5),
        'cmp_v_w1': nrm(k[9], (DEPTH, CMP_LEN * dh, CMP_HIDDEN), (CMP_LEN * dh) ** -0.5),
        'cmp_v_w2': nrm(k[10], (DEPTH, CMP_HIDDEN, dh), CMP_HIDDEN ** -0.5),
        'hgrn_lb_logits': nrm(k[11], (DEPTH + 1, HGRN_KEY_WIDTH), 0.5),
        'hgrn_o_norm': 1.0 + nrm(k[12], (DEPTH, HGRN_VAL_DIM), 0.02),
        'w_out': nrm(k[13], (DEPTH, MIX_WIDTH, D_MODEL), MIX_WIDTH ** -0.5),
        'norm_ffn': 1.0 + nrm(k[14], (DEPTH, D_MODEL), 0.02),
        'w_gate_up': nrm(k[15], (DEPTH, D_MODEL, 2 * D_FF), D_MODEL ** -0.5),
        'w_down': nrm(k[16], (DEPTH, D_FF, D_MODEL), D_FF ** -0.5),
    }


def reference(x, norm_mix, w_in, q_norm, k_norm, cmp_pos_k, cmp_pos_v, cmp_k_w1, cmp_k_w2, cmp_v_w1, cmp_v_w2,
              hgrn_lb_logits, hgrn_o_norm, w_out, norm_ffn, w_gate_up, w_down):
    B, T, _ = x.shape
    H, G, dh = NSA_HEADS, NSA_KV_GROUPS, NSA_HEAD_DIM
    split_points = [int(v) for v in np.cumsum(IN_SIZES)[:-1]]
    lower_bounds = jnp.cumsum(jax.nn.softmax(hgrn_lb_logits.astype(jnp.float32), axis=0), axis=0)
    for l in range(DEPTH):
        h = rms_norm(x, norm_mix[l])
        proj = h @ w_in[l]
        (q, kc_raw, vc_raw, ks, vs, kw, vw, gate_logits, hq, hf, hi, hg) = jnp.split(proj, split_points, axis=-1)
        q = rms_norm(q.reshape(B, T, H, dh), q_norm[l])
        kc = rms_norm(compress(kc_raw.reshape(B, T, G, dh), cmp_pos_k[l], cmp_k_w1[l], cmp_k_w2[l]), k_norm[l, 0])
        vc = compress(vc_raw.reshape(B, T, G, dh), cmp_pos_v[l], cmp_v_w1[l], cmp_v_w2[l])
        ks = rms_norm(ks.reshape(B, T, G, dh), k_norm[l, 1])
        kw = rms_norm(kw.reshape(B, T, G, dh), k_norm[l, 2])
        gates = jax.nn.sigmoid(gate_logits.astype(jnp.float32)).reshape(B, T, H, 3)
        o_nsa = nsa_mixer(q, kc, vc, ks, vs.reshape(B, T, G, dh), kw, vw.reshape(B, T, G, dh), gates)
        o_hgrn = hgrn2_mixer(hq, hf, hi, hg, lower_bounds[l], hgrn_o_norm[l])
        mix = jnp.concatenate([o_nsa.astype(x.dtype), o_hgrn.astype(x.dtype)], axis=-1)
        x = x + mix @ w_out[l]
        h = rms_norm(x, norm_ffn[l])
        gate, up = jnp.split(h @ w_gate_up[l], 2, axis=-1)
        x = x + (jax.nn.silu(gate) * up) @ w_down[l]
    return x
```

```python
import numpy as np
from contextlib import ExitStack

import concourse.bass as bass
import concourse.mybir as mybir
from concourse.bass import ds
from concourse.bass_utils import run_bass_kernel_spmd

F32 = mybir.dt.float32
BF16 = mybir.dt.bfloat16
AF = mybir.ActivationFunctionType
ALU = mybir.AluOpType
AX = mybir.AxisListType

T = 8192
D = 1024
NT = T // 128
DFF = 2816
NFF = DFF // 128
EPS = 1e-6
NEG = -1.0e30
SAME_ENGINE_SYNC = True
EPSI = {float(D * EPS): 0, float(64 * EPS): 1, float(128 * EPS): 2}
N_TILES = NT
FE_STEPS = 56
STOP_AT = 10 ** 9
HOIST_MAX = 30
PIPELINE = True
DEBUG = False
RUN_CC = True
RUN_P2 = True


class Buf:
    __slots__ = ("name", "w", "r", "excl")

    def __init__(self, name):
        self.name = name
        self.w = []
        self.r = {}
        self.excl = name.startswith("ps")


class Sched:
    def __init__(self, nc, es):
        self.nc = nc
        self.es = es
        self.eng = {"pe": nc.tensor, "act": nc.scalar, "dve": nc.vector, "pool": nc.gpsimd, "sp": nc.sync}
        self.sem = {k: es.enter_context(nc.semaphore("sem_" + k)) for k in self.eng}
        self.cnt = {k: 0 for k in self.eng}
        self.known = {k: {} for k in self.eng}
        self.chsem = {}
        self.chcnt = {}
        self.nwait = 0

    def _sem_of(self, ev):
        return self.sem[ev[1]] if ev[0] == "e" else self.chsem[ev[1]]

    def _waits(self, e, reads, writes):
        need = {}
        for b in reads:
            for ev in b.w:
                k = (ev[0], ev[1])
                need[k] = max(need.get(k, 0), ev[2])
            if b.excl:
                for k, v in b.r.items():
                    if not (k[0] == "e" and k[1] == e):
                        need[k] = max(need.get(k, 0), v)
        for b in writes:
            for ev in b.w:
                k = (ev[0], ev[1])
                need[k] = max(need.get(k, 0), ev[2])
            for k, v in b.r.items():
                need[k] = max(need.get(k, 0), v)
        for k, v in need.items():
            if k[0] == "e" and k[1] == e and (e == "pe" or not SAME_ENGINE_SYNC):
                continue
            if self.known[e].get(k, 0) >= v:
                continue
            sem = self.sem[k[1]] if k[0] == "e" else self.chsem[k[1]]
            self.eng[e].wait_ge(sem, v)
            self.known[e][k] = v
            self.nwait += 1

    def _record(self, ev, reads, writes):
        k = (ev[0], ev[1])
        for b in reads:
            b.r[k] = max(b.r.get(k, 0), ev[2])
        for b in writes:
            b.w = [ev]
            b.r = {}

    def op(self, e, reads, writes, fn):
        self._waits(e, reads, writes)
        inst = fn(self.eng[e])
        self.cnt[e] += 1
        inst.then_inc(self.sem[e], 1)
        self._record(("e", e, self.cnt[e]), reads, writes)

    def dma(self, q, ch, out, in_, reads, writes, **kw):
        if ch not in self.chsem:
            self.chsem[ch] = self.es.enter_context(self.nc.semaphore("ch_" + ch))
            self.chcnt[ch] = 0
        self._waits(q, reads, writes)
        inst = self.eng[q].dma_start(out=out, in_=in_, **kw)
        self.chcnt[ch] += 16
        inst.then_inc(self.chsem[ch], 16)
        self._record(("d", ch, self.chcnt[ch]), reads, writes)

    def collective(self, ch, reads, writes, fn):
        if ch not in self.chsem:
            self.chsem[ch] = self.es.enter_context(self.nc.semaphore("ch_" + ch))
            self.chcnt[ch] = 0
        self._waits("pool", reads, writes)
        inst = fn(self.eng["pool"])
        self.chcnt[ch] += 1
        inst.then_inc(self.chsem[ch], 1)
        self._record(("d", ch, self.chcnt[ch]), reads, writes)

    def barrier(self, engines=None):
        engines = engines or list(self.eng)
        for e in engines:
            for e2 in self.eng:
                if e2 == e or self.cnt[e2] == 0:
                    continue
                k = ("e", e2)
                if self.known[e].get(k, 0) < self.cnt[e2]:
                    self.eng[e].wait_ge(self.sem[e2], self.cnt[e2])
                    self.known[e][k] = self.cnt[e2]
            for ch, v in self.chcnt.items():
                k = ("d", ch)
                if v and self.known[e].get(k, 0) < v:
                    self.eng[e].wait_ge(self.chsem[ch], v)
                    self.known[e][k] = v


def interleave(a, b_):
    da = db = False
    while not (da and db):
        if not da:
            try:
                next(a)
                yield
            except StopIteration:
                da = True
        if not db:
            try:
                next(b_)
                yield
            except StopIteration:
                db = True


def bc(ap, axis, shape):
    return ap.unsqueeze(axis).to_broadcast(list(shape))


def build_nc():
    nc = bass.Bass("TRN2", target_bir_lowering=False)

    def din(name, shape, dt=F32):
        return nc.dram_tensor(name, list(shape), dt, kind="ExternalInput")

    x_b = din("x_b", [T, D])
    x2 = din("x2", [2048, D])
    w_tm = din("w_tm", [D, 902])
    w_fm = din("w_fm", [D, 384])
    w1k = din("w1k", [2048, 128])
    w1v = din("w1v", [2048, 128])
    w2kv = din("w2kv", [128, 128])
    posT = din("posT", [128, 32])
    c_ident = din("c_ident", [128, 128])
    c_tri = din("c_tri", [128, 128])
    c_tri2t = din("c_tri2t", [128, 128])
    c_ov = din("c_ov", [512, 128])
    c_wa = din("c_wa", [128, 256])
    c_wb = din("c_wb", [128, 256])
    c_e = din("c_e", [64, T])
    gq6_d = din("gq6", [128, 384])
    gkc_d = din("gkc", [128, 64])
    gon_d = din("gon", [128, 128])
    nmcol_d = din("nmcol", [128, 8])
    lbrow_d = din("lbrow", [128, 256])
    lbc_d = din("lbc", [128, 2])
    w_out_p = din("w_out_p", [D, D])
    w_gu = din("w_gu", [D, 2 * DFF])
    w_dn = din("w_dn", [DFF, D])
    nfcol_d = din("nfcol", [128, 8])
    y = nc.dram_tensor("y", [2048, D], F32, kind="ExternalOutput")

    dbg = nc.dram_tensor("dbg", [1024, 2048], BF16, kind="ExternalOutput") if DEBUG else None
    mixsrc = nc.dram_tensor("mixsrc", [1024, 2048], BF16)
    mixall = nc.dram_tensor("mixall", [4096, 2048], BF16)
    mixsrc_b = Buf("mixsrc")
    mixall_b = Buf("mixall")

    with ExitStack() as es_all:
        S = Sched(nc, es_all)
        ps = [es_all.enter_context(nc.psum_tensor(f"ps{i}", [128, 512], F32)) for i in range(7)]
        psb = es_all.enter_context(nc.psum_tensor("psb", [128, 1024], BF16))

        with ExitStack() as es:
            def sb(name, shape, dt=F32):
                return es.enter_context(nc.sbuf_tensor(name, list(shape), dt))

            wtm = sb("wtm", [128, 8, 902], BF16)
            wfm = sb("wfm", [128, 8, 384], BF16)
            W1k_ = sb("W1k_", [64, 32, 128], BF16)
            W1v_ = sb("W1v_", [64, 32, 128], BF16)
            W1s = [W1k_, W1v_]
            w2 = sb("w2", [128, 128], BF16)
            posTk = sb("posTk", [64, 32], BF16)
            posTv = sb("posTv", [64, 32], BF16)
            posTs = [posTk, posTv]
            ident = sb("ident", [128, 128], BF16)
            tri = sb("tri", [128, 128], F32)
            tri2t = sb("tri2t", [128, 128], F32)
            ovsb = sb("ovsb", [128, 4, 128], BF16)
            ones_c = sb("ones_c", [128, 1], BF16)
            gq6 = sb("gq6s", [128, 384])
            gkc = sb("gkcs", [128, 64])
            gon = sb("gons", [128, 128])
            wa = sb("was", [128, 256])
            wb = sb("wbs", [128, 256])
            nmcol = sb("nmcols", [128, 8])
            lbrow = sb("lbrows", [128, 256])
            lbB = sb("lbB", [128, 128])
            omlB = sb("omlB", [128, 128])
            lbc = sb("lbcs", [128, 2])
            lbcol = sb("lbcol", [128, 1])
            omlc = sb("omlc", [128, 1])
            nomlc = sb("nomlc", [128, 1])
            cbias = sb("cbias", [128, 2])

            KE = sb("KE", [128, T], BF16)
            kwT = sb("kwT", [64, T], BF16)
            kcrT = sb("kcrT", [64, T], BF16)
            vcrT = sb("vcrT", [64, T], BF16)
            kvr = [kcrT, vcrT]
            vsA = sb("vsA", [128, NT, 65], BF16)
            vwA = sb("vwA", [128, NT, 65], BF16)
            kcT = sb("kcT", [64, 512], BF16)
            vcT = sb("vcT", [64, 512], BF16)
            vcA = sb("vcA", [128, 4, 65], BF16)
            Sst = sb("Sst", [128, 128])
            Sbf = [sb(f"Sbf{i}", [128, 128], BF16) for i in range(2)]
            QpTz = [sb(f"QpTz{i}", [128, 128], BF16) for i in range(2)]
            Kppz = [sb(f"Kppz{i}", [128, 128], BF16) for i in range(2)]

            NXS = 3
            xt = [sb(f"xt{i}", [128, D]) for i in range(NXS)]
            sqj = sb("sqj", [128, D])
            ssx = sb("ssx", [128, 1])
            rx = sb("rx", [128, 1])
            xn = sb("xn", [128, D], BF16)
            xnT = [sb(f"xnT{i}", [128, 8, 128], BF16) for i in range(2)]
            sq6 = sb("sq6", [128, 384])
            ss6 = sb("ss6", [128, 6])
            r6 = sb("r6", [128, 6])
            t6 = sb("t6", [128, 384])
            qk6 = sb("qk6", [128, 384], BF16)
            gts = [sb(f"gts{i}", [128, 6]) for i in range(2)]
            qT4 = sb("qT4", [64, 4, 128], BF16)
            QB = [[sb(f"QB{p_}{i}", [128, 2, 128], BF16) for i in range(2)] for p_ in range(2)]
            ocmp = [sb(f"ocmp{i}", [128, 2, 64]) for i in range(2)]
            cfc = sb("cfc", [128, 2])
            hy = sb("hy", [128, 16])
            hy2 = sb("hy2", [128, 16])
            hsg = sb("hsg", [128, 16])
            hid = sb("hid", [128, 16], BF16)
            ksq = sb("ksq", [8, 64])
            kss = sb("kss", [8, 1])
            kr = sb("kr", [8, 1])
            kt1 = sb("kt1", [8, 64])
            kcn = sb("kcn", [8, 64], BF16)
            Pc = [sb(f"Pc{i}", [128, 512], BF16) for i in range(2)]
            Pb = [sb(f"Pb{i}", [128, 512], BF16) for i in range(3)]
            dn4 = sb("dn4", [128, 4])
            rd4 = sb("rd4", [128, 4])
            imp = sb("imp", [128, 128])
            imp2 = sb("imp2", [128, 128])
            m8a = sb("m8a", [128, 8])
            m8b = sb("m8b", [128, 8])
            thr = sb("thr", [128, 1])
            selb = sb("selb", [128, 128], BF16)
            selsw = sb("selsw", [128, 128], BF16)
            dn6 = sb("dn6", [128, 6])
            rd6 = sb("rd6", [128, 6])
            coef = sb("coef", [128, 6])
            oacc = sb("oacc", [128, 64])
            mix = [sb(f"mix{i}", [128, 256], BF16) for i in range(2)]
            mixT = sb("mixT", [128, 2, 128], BF16)
            sgT = sb("sgT", [128, 128])
            qTf = sb("qTf", [128, 128])
            kTf = sb("kTf", [128, 128])
            sg = sb("sg", [128, 128])
            ff = sb("ff", [128, 128])
            logf = sb("logf", [128, 128])
            ktm = sb("ktm", [128, 128])
            gsil = sb("gsil", [128, 128])
            vbf = sb("vbf", [128, 128], BF16)
            ebT = sb("ebT", [128, 128])
            enbT = sb("enbT", [128, 128])
            eblmb = sb("eblmb", [128, 128])
            QpT = sb("QpT", [128, 128], BF16)
            KpT = sb("KpT", [128, 128], BF16)
            Kpp = sb("Kpp", [128, 128], BF16)
            attnT = sb("attnT", [128, 128], BF16)
            sso = sb("sso", [128, 1])
            ro = sb("ro", [128, 1])
            o1 = sb("o1", [128, 128])
            o2 = sb("o2", [128, 128])

            B = {}

            def b(name):
                if name not in B:
                    B[name] = Buf(name)
                return B[name]

            epsc = sb("epsc", [128, 3])
            for v_, i_ in EPSI.items():
                S.op("pool", [], [b("epsc")], lambda e, v_=v_, i_=i_: e.memset(epsc[:, i_:i_ + 1], v_))
            S.barrier(["act"])

            cst = b("const")
            S.dma("pool", "cw5", ident[:], c_ident.ap(), [], [b("ident")])
            S.dma("pool", "cw0", wtm[:], w_tm.ap().rearrange("(k p) n -> p k n", p=128), [], [b("wtm")])
            S.dma("pool", "cw1", wfm[:], w_fm.ap().rearrange("(k p) n -> p k n", p=128), [], [b("wfm")])
            S.dma("pool", "cw2", W1k_[:], w1k.ap().rearrange("(p d) h -> d p h", d=64), [], [b("W1a")])
            S.dma("pool", "cw2", W1v_[:], w1v.ap().rearrange("(p d) h -> d p h", d=64), [], [b("W1b")])
            S.dma("pool", "cw3", w2[:], w2kv.ap(), [], [b("w2")])
            S.dma("pool", "cw3", posTk[:], posT.ap()[0:64, :], [], [b("posT")])
            S.dma("pool", "cw3", posTv[:], posT.ap()[64:128, :], [], [b("posT2")])
            S.dma("pool", "cw3", ovsb[:], c_ov.ap().rearrange("(c p) j -> p c j", p=128), [], [b("ov")])
            S.dma("pool", "cw4", KE[64:128, :], c_e.ap(), [], [b("KEe")])
            for bb_ in ("w2", "posT", "posT2", "ov"):
                b(bb_).w = [("d", "cw3", S.chcnt["cw3"])]
            b("W1a").w = [("d", "cw2", S.chcnt["cw2"])]
            for i, (dst, src) in enumerate([(tri, c_tri), (tri2t, c_tri2t), (gq6, gq6_d), (gkc, gkc_d), (gon, gon_d),
                                            (wa, c_wa), (wb, c_wb), (nmcol, nmcol_d), (lbrow, lbrow_d), (lbc, lbc_d)]):
                S.dma("sp", "cs0", dst[:], src.ap(), [], [cst])
            cst.w = [("d", "cs0", S.chcnt["cs0"])]

            S.op("pool", [], [b("ones")], lambda e: e.memset(ones_c[:], 1.0))
            S.op("pool", [], [b("vsAones")], lambda e: e.memset(vsA[:, :, 64:65], 1.0))
            S.op("pool", [], [b("vwAones")], lambda e: e.memset(vwA[:, :, 64:65], 1.0))
            S.op("pool", [], [b("vcA")], lambda e: e.memset(vcA[:, :, 0:64], 0.0))
            S.op("pool", [], [b("vcAones")], lambda e: e.memset(vcA[:, :, 64:65], 1.0))
            S.op("pool", [], [b("kcT")], lambda e: e.memset(kcT[:], 0.0))
            S.op("pool", [], [b("vcT")], lambda e: e.memset(vcT[:], 0.0))
            S.op("pool", [], [b("Sst")], lambda e: e.memset(Sst[:], 0.0))
            for i in range(2):
                S.op("pool", [], [b(f"Sbf{i}")], lambda e, i=i: e.memset(Sbf[i][:], 0.0))
                S.op("pool", [], [b(f"QpTz{i}")], lambda e, i=i: e.memset(QpTz[i][:], 0.0))
                S.op("pool", [], [b(f"Kppz{i}")], lambda e, i=i: e.memset(Kppz[i][:], 0.0))
            S.op("dve", [cst], [b("lbB")], lambda e: e.tensor_sub(lbB[:], lbrow[:, 0:128], lbrow[:, 128:256]))
            S.op("act", [b("lbB")], [b("lbB")], lambda e: e.activation(lbB[:], lbB[:], AF.Sigmoid))
            S.op("dve", [b("lbB")], [b("omlB")],
                 lambda e: e.tensor_scalar(omlB[:], lbB[:], -1.0, 1.0, ALU.mult, ALU.add))
            S.op("dve", [cst], [b("lbcol")], lambda e: e.tensor_sub(lbcol[:], lbc[:, 0:1], lbc[:, 1:2]))
            S.op("act", [b("lbcol")], [b("lbcol")], lambda e: e.activation(lbcol[:], lbcol[:], AF.Sigmoid))
            S.op("dve", [b("lbcol")], [b("omlc")],
                 lambda e: e.tensor_scalar(omlc[:], lbcol[:], -1.0, 1.0, ALU.mult, ALU.add))
            S.op("dve", [b("lbcol")], [b("nomlc")],
                 lambda e: e.tensor_scalar(nomlc[:], lbcol[:], 1.0, -1.0, ALU.mult, ALU.add))
            pX0 = ps[3][:, 0:16]
            for kv in range(2):
                for p in range(32):
                    S.op("pe", [b("W1a"), b("W1b"), b("posT"), b("posT2")], [b("ps3")],
                         lambda e, p=p, kv=kv: e.matmul(pX0[:, kv:kv + 1], lhsT=W1s[kv][:, p, :],
                                                        rhs=posTs[kv][:, p:p + 1],
                                                        start=(p == 0), stop=(p == 31)))
            S.op("dve", [b("ps3")], [b("cbias")], lambda e: e.tensor_copy(cbias[:], pX0[:, 0:2]))

            def load_x(qb):
                sl = qb % NXS
                S.dma("sp", f"x{sl}", xt[sl][:], x_b.ap()[qb * 128:(qb + 1) * 128, :], [], [b(f"xt{sl}")])

            load_x(0)
            if N_TILES > 1:
                load_x(1)

            psA, psB_, psC = ps[0], ps[1], ps[2]
            F3 = ps[3]
            F3b = psb
            psO = ps[6]
            psSWs = [ps[4][:, 0:256], ps[5][:, 0:256]]
            psSW2 = [ps[4], ps[5]]
            swtok = ["ps4", "ps5"]
            pX = F3[:, 0:16]
            pY = F3[0:8, 16:80]
            pZ = F3[0:64, 80:88]
            pGt = ps[1][:, 384:390]
            pKT = psb[0:64, 768:776]
            pVT = psb[:, 776:840]
            pT1 = psb[:, 840:968]
            pSC = ps[2]

            def impap(r):
                return ps[1][:, r * 128:(r + 1) * 128]

            def imptok(r):
                return b("ps1")

            def ocap(h):
                return F3[:, 128 + h * 64:192 + h * 64]

            def xstage(q):
                yield from xstage_a(q)
                yield from xstage_b(q)

            def xstage_a(q):
                sl = q % NXS
                xs = xt[sl]
                xb = b(f"xt{sl}")
                S.op("pool", [], [b("ssx")], lambda e: e.memset(ssx[:], 0.0))
                S.op("act", [xb, b("ssx")], [b("sqj"), b("ssx")],
                     lambda e: e.activation(sqj[:], xs[:], AF.Square, accum_out=ssx[:]))
                S.op("act", [b("ssx")], [b("rx")],
                     lambda e: e.activation(rx[:], ssx[:], AF.Ln, bias=epsc[:, 0:1]))
                S.op("act", [b("rx")], [b("rx")], lambda e: e.activation(rx[:], rx[:], AF.Exp, scale=-0.5))
                yield
                S.op("dve", [xb, b("rx")], [b("xn")],
                     lambda e: e.tensor_scalar(xn[:], xs[:], rx[:, 0:1], 32.0, ALU.mult, ALU.mult))
                yield

            def xstage_b(q):
                xT = xnT[q % 2]
                xTb = b(f"xnT{q % 2}")
                for k in range(8):
                    S.op("pe", [b("xn"), b("ident")], [b("psb")],
                         lambda e, k=k: e.transpose(F3b[:, k * 128:(k + 1) * 128], xn[:, k * 128:(k + 1) * 128], ident[:]))
                yield
                S.op("dve", [b("psb"), cst], [xTb],
                     lambda e: e.tensor_tensor(xT[:], F3b[:, 0:1024].rearrange("p (k t) -> p k t", k=8),
                                               bc(nmcol[:], 2, [128, 8, 128]), ALU.mult))
                yield

            def hoisted(q):
                return 1 <= q <= HOIST_MAX and q < N_TILES

            def frontend(qb):
                par = qb % 2
                t0 = qb * 128
                xT = xnT[qb % 2]
                xTb = b(f"xnT{qb % 2}")
                if qb + 2 < N_TILES:
                    load_x(qb + 2)
                if not hoisted(qb):
                    if qb == 0:
                        yield from xstage_a(qb)
                    yield from xstage_b(qb)
                for k in range(8):
                    S.op("pe", [xTb, b("wtm")], [b("ps0")],
                         lambda e, k=k: e.matmul(psA[:, 0:512], lhsT=xT[:, k, :], rhs=wtm[:, k, 0:512],
                                                 start=(k == 0), stop=(k == 7)))
                    if k % 2 == 1:
                        yield
                for k in range(8):
                    S.op("pe", [xTb, b("wtm")], [b("ps1")],
                         lambda e, k=k: e.matmul(psB_[:, 0:390], lhsT=xT[:, k, :], rhs=wtm[:, k, 512:902],
                                                 start=(k == 0), stop=(k == 7)))
                    if k % 2 == 1:
                        yield
                for c, (m0, m1) in enumerate([(0, 64), (64, 128), (128, 256), (256, 384)]):
                    for k in range(8):
                        S.op("pe", [xTb, b("wfm")], [b("ps2")],
                             lambda e, k=k, c=c, m0=m0, m1=m1: e.matmul(psC[0:m1 - m0, c * 128:(c + 1) * 128],
                                                                        lhsT=wfm[:, k, m0:m1], rhs=xT[:, k, :],
                                                                        start=(k == 0), stop=(k == 7)))
                    yield
                S.op("act", [b("ps0")], [b("sq6")], lambda e: e.activation(sq6[:], psA[:, 0:384], AF.Square))
                S.op("dve", [b("sq6")], [b("ss6")],
                     lambda e: e.tensor_reduce(ss6[:], sq6[:].rearrange("p (h d) -> p h d", h=6), AX.X, ALU.add))
                S.op("act", [b("ss6")], [b("r6")], lambda e: e.activation(r6[:], ss6[:], AF.Ln, bias=epsc[:, 1:2]))
                S.op("act", [b("r6")], [b("r6")], lambda e: e.activation(r6[:], r6[:], AF.Exp, scale=-0.5))
                yield
                S.op("dve", [b("r6")], [b("r6")], lambda e: e.tensor_scalar_mul(r6[:, 4:6], r6[:, 4:6], 8.0))
                S.op("dve", [b("ps0"), b("r6")], [b("t6")],
                     lambda e: e.tensor_tensor(t6[:].rearrange("p (h d) -> p h d", h=6),
                                               psA[:, 0:384].rearrange("p (h d) -> p h d", h=6),
                                               bc(r6[:], 2, [128, 6, 64]), ALU.mult))
                S.op("dve", [b("t6"), cst], [b("qk6")], lambda e: e.tensor_tensor(qk6[:], t6[:], gq6[:], ALU.mult))
                yield
                S.op("act", [b("ps0")], [b(f"vsA{qb}")], lambda e: e.copy(vsA[:, qb, 0:64], psA[:, 384:448]))
                S.op("act", [b("ps0")], [b(f"vwA{qb}")], lambda e: e.copy(vwA[:, qb, 0:64], psA[:, 448:512]))
                S.op("act", [b("ps2")], [b("kvcT")], lambda e: e.copy(kcrT[:, t0:t0 + 128], psC[0:64, 0:128]))
                S.op("act", [b("ps2")], [b("kvcT")], lambda e: e.copy(vcrT[:, t0:t0 + 128], psC[0:64, 128:256]))
                yield
                for h in range(6):
                    S.op("pe", [b("qk6"), b("ident")], [b("psb")],
                         lambda e, h=h: e.transpose(F3b[0:64, h * 128:(h + 1) * 128], qk6[:, h * 64:(h + 1) * 64], ident[:]))
                yield
                S.op("act", [b("psb")], [b("qT4")],
                     lambda e: e.copy(qT4[:], F3b[0:64, 0:512].rearrange("p (h t) -> p h t", h=4)))
                for i in range(2):
                    if i == 1 and qb < 32:
                        continue
                    S.op("dve", [b("psb")], [b(f"QBq{par}{i}")],
                         lambda e, i=i: e.tensor_copy(QB[par][i][0:64], F3b[0:64, 0:256].rearrange("p (h t) -> p h t", h=2)))
                S.op("act", [b("psb")], [b(f"KEk{qb}")], lambda e: e.copy(KE[0:64, t0:t0 + 128], F3b[0:64, 512:640]))
                S.op("act", [b("psb")], [b(f"kwT{qb}")], lambda e: e.copy(kwT[:, t0:t0 + 128], F3b[0:64, 640:768]))
                yield
                i0 = max(0, 8 * qb - 1)
                n = 8 * qb + 7 - i0
                for kv in range(2):
                    for p in range(32):
                        st = 16 * i0 + p
                        S.op("pe", [b("W1a"), b("W1b"), b("kvcT")], [b("ps3")],
                             lambda e, p=p, kv=kv, st=st: e.matmul(
                                 pX[:, kv * 8:kv * 8 + n], lhsT=W1s[kv][:, p, :],
                                 rhs=kvr[kv][:, st:st + 16 * (n - 1) + 1:16], start=(p == 0), stop=(p == 31)))
                        if p % 8 == 7:
                            yield
                v3 = lambda t_: t_.rearrange("p (k n) -> p k n", k=2)[:, :, 0:n]
                S.op("dve", [b("ps3"), b("cbias")], [b("hy")],
                     lambda e: e.tensor_tensor(v3(hy[:]), v3(pX), bc(cbias[:], 2, [128, 2, n]), ALU.add))
                S.op("dve", [b("hy")], [b("hy2")], lambda e: e.tensor_tensor(v3(hy2[:]), v3(hy[:]), v3(hy[:]), ALU.mult))
                S.op("dve", [b("hy2")], [b("hy2")],
                     lambda e: e.tensor_scalar(v3(hy2[:]), v3(hy2[:]), 0.044715, 1.0, ALU.mult, ALU.add))
                S.op("dve", [b("hy2"), b("hy")], [b("hy2")],
                     lambda e: e.tensor_tensor(v3(hy2[:]), v3(hy2[:]), v3(hy[:]), ALU.mult))
                yield
                gt = gts[par]

                def sig_fin(t, tok):
                    S.op("dve", [tok], [tok], lambda e: e.tensor_scalar(t, t, 1.0, None, ALU.add))
                    S.op("dve", [tok], [tok], lambda e: e.reciprocal(t, t))

                S.op("act", [b("hy2")], [b("hsg")],
                     lambda e: e.activation(v3(hsg[:]), v3(hy2[:]), AF.Exp, scale=-1.5957691216057308))
                S.op("act", [b("ps1")], [b(f"gts{par}")], lambda e: e.activation(gt[:], pGt, AF.Exp, scale=-1.0))
                S.op("act", [b("ps2")], [b("sgT")], lambda e: e.activation(sgT[:], psC[:, 384:512], AF.Exp, scale=-1.0))
                S.op("act", [b("ps1")], [b("sg")], lambda e: e.activation(sg[:], psB_[:, 0:128], AF.Exp, scale=-1.0))
                S.op("act", [b("ps2")], [b("qTf")], lambda e: e.activation(qTf[:], psC[:, 256:384], AF.Exp, scale=-1.0))
                S.op("act", [b("ps1")], [b("gsil")], lambda e: e.activation(gsil[:], psB_[:, 256:384], AF.Exp, scale=-1.0))
                S.op("act", [b("ps1")], [b("vbf")], lambda e: e.copy(vbf[:], psB_[:, 128:256]))
                yield
                def chain_a():
                    sig_fin(v3(hsg[:]), b("hsg"))
                    sig_fin(gt[:], b(f"gts{par}"))
                    S.op("dve", [b("hsg"), b("hy")], [b("hid")],
                         lambda e: e.tensor_tensor(v3(hid[:]), v3(hy[:]), v3(hsg[:]), ALU.mult))
                    yield
                    S.op("pe", [b("hid"), b("w2")], [b("ps3")],
                         lambda e: e.matmul(pY[0:n, :], lhsT=hid[:, 0:n], rhs=w2[:, 0:64], start=True, stop=True))
                    S.op("pe", [b("hid"), b("w2")], [b("ps3")],
                         lambda e: e.matmul(pZ[:, 0:n], lhsT=w2[:, 64:128], rhs=hid[:, 8:8 + n], start=True, stop=True))
                    S.op("act", [b("ps3")], [b("ksq")], lambda e: e.activation(ksq[0:n], pY[0:n, :], AF.Square))
                    S.op("dve", [b("ksq")], [b("kss")], lambda e: e.tensor_reduce(kss[0:n], ksq[0:n], AX.X, ALU.add))
                    S.op("act", [b("kss")], [b("kr")], lambda e: e.activation(kr[0:n], kss[0:n], AF.Ln, bias=epsc[0:n, 1:2]))
                    S.op("act", [b("kr")], [b("kr")], lambda e: e.activation(kr[0:n], kr[0:n], AF.Exp, scale=-0.5))
                    yield
                    S.op("dve", [b("ps3"), b("kr")], [b("kt1")],
                         lambda e: e.tensor_scalar(kt1[0:n], pY[0:n, :], kr[0:n, 0:1], 8.0, ALU.mult, ALU.mult))
                    S.op("dve", [b("kt1"), cst], [b("kcn")], lambda e: e.tensor_tensor(kcn[0:n], kt1[0:n], gkc[0:n], ALU.mult))
                    S.op("act", [b("ps3")], [b("vcT")], lambda e: e.copy(vcT[:, i0:i0 + n], pZ[:, 0:n]))
                    S.op("pe", [b("kcn"), b("ident")], [b("psb")],
                         lambda e: e.transpose(pKT[:, 0:n], kcn[0:n, :], ident[0:n, 0:n]))
                    S.op("act", [b("psb")], [b("kcT")], lambda e: e.copy(kcT[:, i0:i0 + n], pKT[:, 0:n]))
                    yield
                    for ct in range(i0 // 128, (i0 + n - 1) // 128 + 1):
                        S.op("pe", [b("vcT"), b("ident")], [b("psb")],
                             lambda e, ct=ct: e.transpose(pVT, vcT[:, ct * 128:(ct + 1) * 128], ident[0:64, 0:64]))
                        S.op("act", [b("psb")], [b("vcA")], lambda e, ct=ct: e.copy(vcA[:, ct, 0:64], pVT))
                    yield

                    nct = (8 * qb + 6) // 128 + 1
                    for ct in range(nct):
                        pc = Pc[ct % 2]
                        pcb = b(f"Pc{ct % 2}")
                        S.op("pe", [b("kcT"), b("qT4")], [b("ps2")],
                             lambda e, ct=ct: e.matmul(pSC[:, 0:512], lhsT=kcT[:, ct * 128:(ct + 1) * 128],
                                                       rhs=qT4[:].rearrange("p h t -> p (h t)"), start=True, stop=True))
                        S.op("act", [b("ps2")], [pcb], lambda e, pc=pc: e.activation(pc[:], pSC[:, 0:512], AF.Exp))
                        pcv = pc[:].rearrange("p (h q) -> p h q", h=4)
                        if 16 * (ct * 128 + 127) + 31 > t0:
                            S.op("pool", [pcb], [pcb],
                                 lambda e, pcv=pcv, ct=ct: e.affine_select(out=pcv, in_=pcv, pattern=[[0, 4], [1, 128]],
                                                                           compare_op=ALU.is_ge, fill=0.0,
                                                                           base=t0 - 2048 * ct - 31, channel_multiplier=-16))
                        yield
                        for r in range(4):
                            lw = pc[:, r * 128:(r + 1) * 128]
                            S.op("pe", [pcb, b("ov")], [imptok(r)],
                                 lambda e, r=r, lw=lw, ct=ct: e.matmul(impap(r), lhsT=lw, rhs=ovsb[:, ct, :],
                                                                       start=(ct == 0 and r == 0), stop=(ct == nct - 1),
                                                                       skip_group_check=True))
                            if r < 2:
                                S.op("pe", [pcb, b("vcA")], [b("ps3")],
                                     lambda e, r=r, lw=lw, ct=ct: e.matmul(ocap(r), lhsT=lw,
                                                                           rhs=vcA[:, ct, 0:64], start=(ct == 0 and r == 0),
                                                                           stop=(ct == nct - 1), skip_group_check=True))
                        yield
                    S.op("dve", [b("ps1")], [b("dn4")],
                         lambda e: e.tensor_scalar_max(dn4[:], ps[1][:, 0:512].rearrange("p (r j) -> p r j", r=4)[:, :, 0], 1e-30))
                    S.op("dve", [b("dn4")], [b("rd4")], lambda e: e.reciprocal(rd4[:], dn4[:]))
                    yield
                    S.op("dve", [b("rd4"), b(f"gts{par}")], [b("cfc")],
                         lambda e: e.tensor_tensor(cfc[:], rd4[:, 0:2], gt[:, 0:6:3], ALU.mult))
                    for h in range(2):
                        S.op("dve", [b("ps3"), b("cfc")], [b(f"ocmp{par}")],
                             lambda e, h=h: e.tensor_scalar(ocmp[par][:, h, :], ocap(h),
                                                            cfc[:, h:h + 1], None, ALU.mult))
                    yield
                    S.op("dve", [b("ps1"), b("rd4")], [b("imp")],
                         lambda e: e.tensor_scalar(imp[:], impap(0), rd4[:, 0:1], None, ALU.mult))
                    for r in range(1, 4):
                        S.op("dve", [imptok(r), b("rd4"), b("imp")], [b("imp")],
                             lambda e, r=r: e.scalar_tensor_tensor(imp[:], impap(r), rd4[:, r:r + 1], imp[:],
                                                                   ALU.mult, ALU.add))
                    yield
                    w0 = 128 - 2 * qb
                    S.op("dve", [b("imp"), cst], [b("imp")], lambda e: e.tensor_tensor(imp[:], imp[:], wa[:, w0:w0 + 128], ALU.mult))
                    S.op("dve", [b("imp"), cst], [b("imp")], lambda e: e.tensor_tensor(imp[:], imp[:], wb[:, w0:w0 + 128], ALU.add))
                    S.op("dve", [b("imp")], [b("imp")], lambda e: e.memset(imp[:, 0:1], 1.0e4))
                    yield
                    S.op("dve", [b("imp")], [b("m8a")], lambda e: e.max(m8a[:], imp[:]))
                    S.op("dve", [b("imp"), b("m8a")], [b("imp2")],
                         lambda e: e.match_replace(imp2[:], m8a[:], imp[:], NEG))
                    S.op("dve", [b("imp2")], [b("m8b")], lambda e: e.max(m8b[:], imp2[:]))
                    S.op("dve", [b("m8b")], [b("thr")], lambda e: e.tensor_reduce(thr[:], m8b[:], AX.X, ALU.min))
                    yield
                    S.op("dve", [b("imp"), b("thr")], [b("selsw")],
                         lambda e: e.tensor_scalar(selsw[:, 64:128], imp[:, 0:64], thr[:, 0:1], 1.0, ALU.is_ge, ALU.subtract))
                    S.op("dve", [b("imp"), b("thr")], [b("selsw")],
                         lambda e: e.tensor_scalar(selsw[:, 0:64], imp[:, 64:128], thr[:, 0:1], 1.0, ALU.is_ge, ALU.subtract))
                    S.op("pe", [b("selsw"), b("ident")], [b("psb")], lambda e: e.transpose(pT1, selsw[:], ident[:]))
                    S.op("act", [b("psb")], [b(f"QBb{par}0")],
                         lambda e: e.copy(QB[par][0][64:128], bc(pT1[64:128, :], 1, [64, 2, 128])))
                    yield
                    if qb >= 32:
                        S.op("dve", [b("imp"), b("thr")], [b("selb")],
                             lambda e: e.tensor_scalar(selb[:], imp[:], thr[:, 0:1], 1.0, ALU.is_ge, ALU.subtract))
                        S.op("pe", [b("selb"), b("ident")], [b("psb")], lambda e: e.transpose(pT1, selb[:], ident[:]))
                        S.op("act", [b("psb")], [b(f"QBb{par}1")],
                             lambda e: e.copy(QB[par][1][64:128], bc(pT1[64:128, :], 1, [64, 2, 128])))
                        yield


                def chain_b():
                    sig_fin(sgT[:], b("sgT"))
                    sig_fin(sg[:], b("sg"))
                    yield
                    sig_fin(qTf[:], b("qTf"))
                    S.op("dve", [b("qTf"), b("ps2")], [b("qTf")],
                         lambda e: e.tensor_tensor(qTf[:], psC[:, 256:384], qTf[:], ALU.mult))
                    sig_fin(gsil[:], b("gsil"))
                    S.op("dve", [b("gsil"), b("ps1")], [b("gsil")],
                         lambda e: e.tensor_tensor(gsil[:], psB_[:, 256:384], gsil[:], ALU.mult))
                    yield
                    S.op("dve", [b("sgT"), b("nomlc"), b("omlc")], [b("kTf")],
                         lambda e: e.tensor_scalar(kTf[:], sgT[:], nomlc[:, 0:1], omlc[:, 0:1], ALU.mult, ALU.add))
                    S.op("dve", [b("sg"), b("omlB")], [b("ff")], lambda e: e.tensor_tensor(ff[:], sg[:], omlB[:], ALU.mult))
                    S.op("dve", [b("ff"), b("lbB")], [b("ff")], lambda e: e.tensor_tensor(ff[:], ff[:], lbB[:], ALU.add))
                    yield
                    S.op("act", [b("ff")], [b("logf")], lambda e: e.activation(logf[:], ff[:], AF.Ln))
                    S.op("dve", [b("ff")], [b("ktm")],
                         lambda e: e.tensor_scalar(ktm[:], ff[:], -1.0, 1.0, ALU.mult, ALU.add))
                    yield
                    pH = ps[0]
                    S.op("pe", [b("logf"), cst], [b("ps0")],
                         lambda e: e.matmul(pH[:, 0:128], lhsT=logf[:], rhs=tri[:], start=True, stop=True))
                    S.op("pe", [b("logf"), cst], [b("ps0")],
                         lambda e: e.matmul(pH[:, 128:256], lhsT=tri2t[:], rhs=logf[:], start=True, stop=True))
                    yield
                    S.op("act", [b("ps0")], [b("ebT")], lambda e: e.activation(ebT[:], pH[:, 0:128], AF.Exp))
                    S.op("act", [b("ps0")], [b("enbT")], lambda e: e.activation(enbT[:], pH[:, 0:128], AF.Exp, scale=-1.0))
                    S.op("act", [b("ps0")], [b("eblmb")], lambda e: e.activation(eblmb[:], pH[:, 128:256], AF.Exp))
                    yield
                    S.op("dve", [b("qTf"), b("ebT")], [b("QpT")], lambda e: e.tensor_tensor(QpT[:], qTf[:], ebT[:], ALU.mult))
                    S.op("dve", [b("kTf"), b("enbT")], [b("KpT")], lambda e: e.tensor_tensor(KpT[:], kTf[:], enbT[:], ALU.mult))
                    for c in range(2):
                        r0 = c * 64
                        S.op("dve", [b("ktm"), b("eblmb")], [b(f"Kppz{c}")],
                             lambda e, c=c, r0=r0: e.tensor_tensor(Kppz[c][r0:r0 + 64, :], ktm[r0:r0 + 64, :],
                                                                   eblmb[r0:r0 + 64, :], ALU.mult))
                        S.op("pool", [b("QpT")], [b(f"QpTz{c}")],
                             lambda e, c=c, r0=r0: e.tensor_copy(QpTz[c][:, r0:r0 + 64], QpT[:, r0:r0 + 64]))
                    yield
                    S.op("pe", [b("KpT"), b("QpT")], [b("ps0")],
                         lambda e: e.matmul(pH[:, 256:384], lhsT=KpT[:], rhs=QpT[:], start=True, stop=True))
                    S.op("dve", [b("ps0"), cst], [b("attnT")],
                         lambda e: e.tensor_tensor(attnT[:], pH[:, 256:384], tri[:], ALU.mult))
                    pHo = pH[:, 384:512]
                    pSu = pH[:, 0:128]
                    S.op("pe", [b("Kppz0"), b("vbf")], [b("ps0")],
                         lambda e: e.matmul(pSu, lhsT=Kppz[0][:], rhs=vbf[:], start=True, stop=True))
                    yield
                    S.op("dve", [b("ps0"), b("ebT"), b("Sst")], [b("Sst")],
                         lambda e: e.scalar_tensor_tensor(Sst[:], Sst[:], ebT[:, 63:64], pSu, ALU.mult, ALU.add))
                    S.op("act", [b("Sst")], [b("Sbf1")], lambda e: e.copy(Sbf[1][:], Sst[:]))
                    yield
                    S.op("pe", [b("attnT"), b("vbf")], [b("ps0")],
                         lambda e: e.matmul(pHo, lhsT=attnT[:], rhs=vbf[:], start=True, stop=False))
                    S.op("pe", [b("QpTz0"), b("Sbf0")], [b("ps0")],
                         lambda e: e.matmul(pHo, lhsT=QpTz[0][:], rhs=Sbf[0][:], start=False, stop=False))
                    S.op("pe", [b("QpTz1"), b("Sbf1")], [b("ps0")],
                         lambda e: e.matmul(pHo, lhsT=QpTz[1][:], rhs=Sbf[1][:], start=False, stop=True))
                    yield
                    S.op("pool", [], [b("sso")], lambda e: e.memset(sso[:], 0.0))
                    S.op("act", [b("ps0"), b("sso")], [b("o2"), b("sso")],
                         lambda e: e.activation(o2[:], pHo, AF.Square, accum_out=sso[:]))
                    S.op("act", [b("sso")], [b("ro")], lambda e: e.activation(ro[:], sso[:], AF.Ln, bias=epsc[:, 2:3]))
                    S.op("act", [b("ro")], [b("ro")], lambda e: e.activation(ro[:], ro[:], AF.Exp, scale=-0.5))
                    S.op("dve", [b("ps0"), b("ro")], [b("o1")],
                         lambda e: e.tensor_scalar(o1[:], pHo, ro[:, 0:1], float(np.sqrt(128.0)), ALU.mult, ALU.mult))
                    yield
                    S.op("pe", [b("Kppz1"), b("vbf")], [b("ps0")],
                         lambda e: e.matmul(pSu, lhsT=Kppz[1][:], rhs=vbf[:], start=True, stop=True))
                    S.op("dve", [b("ps0"), b("ebT"), b("Sst")], [b("Sst")],
                         lambda e: e.scalar_tensor_tensor(Sst[:], Sst[:], ebT[:, 127:128], pSu, ALU.mult, ALU.add))
                    S.op("act", [b("Sst")], [b("Sbf0")], lambda e: e.copy(Sbf[0][:], Sst[:]))
                    yield
                    S.op("dve", [b("o1"), cst], [b("o1")], lambda e: e.tensor_tensor(o1[:], o1[:], gon[:], ALU.mult))
                    S.op("dve", [b("o1"), b("gsil")], [b(f"mixh{par}")],
                         lambda e: e.tensor_tensor(mix[par][:, 128:256], o1[:], gsil[:], ALU.mult))
                    yield


                def chain_b2():
                    yield from chain_b()
                    if hoisted(qb + 1):
                        yield from xstage(qb + 1)
                    elif qb + 1 < N_TILES:
                        yield from xstage_a(qb + 1)

                yield from interleave(chain_a(), chain_b2())

            pbctr = [0]

            def backend(qb, gen):
                par = qb % 2
                t0 = qb * 128
                jobs = [("slc", kt) for kt in range(qb + 1)] + [("win", kt) for kt in range(max(0, qb - 4), qb + 1)]
                pairs = [jobs[i_:i_ + 2] for i_ in range(0, len(jobs), 2)]
                fe_n = FE_STEPS + 2 * ((8 * (qb + 1) + 6) // 128) + (1 if qb + 1 >= 32 else 0)
                nsteps = -(-fe_n // max(1, len(pairs)))

                def advance(k):
                    if gen is None:
                        return
                    for _ in range(k):
                        try:
                            next(gen)
                        except StopIteration:
                            return

                def score(pair, slot):
                    for j_, (kind, kt) in enumerate(pair):
                        pout = psSW2[slot][:, j_ * 256:(j_ + 1) * 256]
                        if kind == "slc":
                            hf_ = kt // 32
                            S.op("pe", [b(f"KEk{kt}"), b("KEe"), b(f"QBq{par}{hf_}"), b(f"QBb{par}{hf_}")], [b(swtok[slot])],
                                 lambda e, pout=pout, kt=kt, hf_=hf_: e.matmul(
                                     pout, lhsT=KE[:, kt * 128:(kt + 1) * 128],
                                     rhs=QB[par][hf_][:].rearrange("p h t -> p (h t)"), start=True, stop=True))
                        else:
                            S.op("pe", [b(f"kwT{kt}"), b(f"QBq{par}0")], [b(swtok[slot])],
                                 lambda e, pout=pout, kt=kt: e.matmul(
                                     pout, lhsT=kwT[:, kt * 128:(kt + 1) * 128],
                                     rhs=QB[par][0][0:64].rearrange("p h t -> p (h t)"), start=True, stop=True))

                score(pairs[0], 0)
                first_pv = [True]
                for pi, pair in enumerate(pairs):
                    if pi + 1 < len(pairs):
                        score(pairs[pi + 1], (pi + 1) % 2)
                    pslot = pbctr[0] % 3
                    pbctr[0] += 1
                    slot_ps = pi % 2
                    pbt = b(f"Pb{pslot}")
                    pbuf = Pb[pslot]
                    w_ = 256 * len(pair)
                    S.op("act", [b(swtok[slot_ps])], [pbt],
                         lambda e: e.activation(pbuf[:, 0:w_], psSW2[slot_ps][:, 0:w_], AF.Exp))
                    for j_, (kind, kt) in enumerate(pair):
                        pv = pbuf[:, j_ * 256:(j_ + 1) * 256].rearrange("p (h q) -> p h q", h=2)
                        if kt == qb:
                            S.op("pool", [pbt], [pbt],
                                 lambda e, pv=pv: e.affine_select(out=pv, in_=pv, pattern=[[0, 2], [1, 128]],
                                                                  compare_op=ALU.is_ge, fill=0.0, base=0, channel_multiplier=-1))
                        elif kind == "win" and kt == qb - 4:
                            S.op("pool", [pbt], [pbt],
                                 lambda e, pv=pv: e.affine_select(out=pv, in_=pv, pattern=[[0, 2], [-1, 128]],
                                                                  compare_op=ALU.is_ge, fill=0.0, base=-1, channel_multiplier=1))
                    if PIPELINE:
                        advance(nsteps)
                    for j_, (kind, kt) in enumerate(pair):
                        branch = 0 if kind == "slc" else 1
                        vA, vname = (vsA, "vsA") if kind == "slc" else (vwA, "vwA")
                        for h in range(2):
                            c0 = (branch * 2 + h) * 65
                            st_ = first_pv[0]
                            first_pv[0] = False
                            S.op("pe", [pbt, b(f"{vname}{kt}"), b(vname + "ones")], [b("ps6")],
                                 lambda e, h=h, c0=c0, j_=j_, kt=kt, vA=vA, st_=st_: e.matmul(
                                     psO[:, c0:c0 + 65], lhsT=pbuf[:, j_ * 256 + h * 128:j_ * 256 + (h + 1) * 128],
                                     rhs=vA[:, kt, :], start=st_, stop=(kt == qb), skip_group_check=True))

                gt = gts[par]
                pOv = psO[:, 0:260].rearrange("p (x c) -> p x c", c=65)
                S.op("dve", [b("ps6")], [b("dn6")], lambda e: e.tensor_scalar_max(dn6[:, 0:4], pOv[:, :, 64], 1e-30))
                S.op("dve", [b("dn6")], [b("rd6")], lambda e: e.reciprocal(rd6[:, 0:4], dn6[:, 0:4]))
                S.op("dve", [b("rd6"), b(f"gts{par}")], [b("coef")],
                     lambda e: e.tensor_tensor(coef[:, 0:4].rearrange("p (b h) -> p b h", b=2),
                                               rd6[:, 0:4].rearrange("p (b h) -> p b h", b=2),
                                               gt[:].rearrange("p (h b) -> p b h", h=2)[:, 1:3, :], ALU.mult))
                for h in range(2):
                    S.op("dve", [b("ps6"), b("coef"), b(f"ocmp{par}")], [b("oacc")],
                         lambda e, h=h: e.scalar_tensor_tensor(oacc[:], psO[:, h * 65:h * 65 + 64],
                                                               coef[:, h:h + 1], ocmp[par][:, h, :], ALU.mult, ALU.add))
                    S.op("dve", [b("ps6"), b("coef"), b("oacc")], [b(f"mixn{par}")],
                         lambda e, h=h: e.scalar_tensor_tensor(mix[par][:, h * 64:(h + 1) * 64],
                                                               psO[:, (2 + h) * 65:(2 + h) * 65 + 64],
                                                               coef[:, 2 + h:3 + h], oacc[:], ALU.mult, ALU.add))
                pM = psb[:, 0:256]
                for k in range(2):
                    S.op("pe", [b(f"mixn{par}"), b(f"mixh{par}"), b("ident")], [b("psb")],
                         lambda e, k=k: e.transpose(pM[:, k * 128:(k + 1) * 128], mix[par][:, k * 128:(k + 1) * 128], ident[:]))
                S.op("act", [b("psb")], [b("mixT")],
                     lambda e: e.copy(mixT[:], pM.rearrange("p (k t) -> p k t", k=2)))
                cj, cc0 = qb // 16, (qb % 16) * 128
                S.dma("sp", "mx", mixsrc.ap()[cj * 256:(cj + 1) * 256, cc0:cc0 + 128].rearrange("(k p) t -> p k t", p=128),
                      mixT[:], [b("mixT")], [mixsrc_b])
                advance(10 ** 6)

            g0 = frontend(0)
            nst_ = 0
            for _ in g0:
                nst_ += 1
                if nst_ >= STOP_AT:
                    break
            print("front-end steps", nst_)
            if STOP_AT >= 10 ** 8:
                for qb in range(N_TILES):
                    backend(qb, frontend(qb + 1) if qb + 1 < N_TILES else None)

            if DEBUG:
                S.dma("sp", "dbg", dbg.ap()[:, :], mixsrc.ap()[:, :], [mixsrc_b], [Buf("dbg")])
            S.barrier()

        if RUN_CC:
            for cj in range(4):
                S.collective("cc", [mixsrc_b], [mixall_b],
                             lambda e, cj=cj: e.collective_compute(
                                 "AllGather", ALU.bypass, replica_groups=[[0, 1, 2, 3], [4, 5, 6, 7]],
                                 ins=[mixsrc.ap()[cj * 256:(cj + 1) * 256, :].opt()],
                                 outs=[mixall.ap()[cj * 1024:(cj + 1) * 1024, :].opt()]))

        if not RUN_P2:
            S.dma("sp", "y0", y.ap()[0:128, 0:8], x2.ap()[0:128, 0:8], [], [Buf("yd")])
            S.barrier(["sp"])
            return nc
        with ExitStack() as es:
            def sb(name, shape, dt=F32):
                return es.enter_context(nc.sbuf_tensor(name, list(shape), dt))

            B = {}

            def b(name):
                if name not in B:
                    B[name] = Buf(name)
                return B[name]

            epsc = sb("epsc2", [128, 3])
            for v_, i_ in EPSI.items():
                S.op("pool", [], [b("epsc")], lambda e, v_=v_, i_=i_: e.memset(epsc[:, i_:i_ + 1], v_))
            S.barrier(["act"])
            wo = sb("wo", [128, 8, D], BF16)
            wd = sb("wd", [128, NFF, D], BF16)
            ident2 = sb("ident2", [128, 128], BF16)
            nfcol = sb("nfcol_s", [128, 8])
            WCH = 4
            NSC = -(-NFF // WCH)
            wg = [sb(f"wg{i}", [128, 8, 2, WCH * 128], BF16) for i in range(3)]
            mT = sb("mT", [128, 8, 512], BF16)
            x1 = sb("x1", [128, 4, D])
            hn2 = [sb(f"hn{i}", [128, D], BF16) for i in range(2)]
            hT = sb("hT", [128, 8, 512], BF16)
            actT = sb("actT", [128, NFF, 512], BF16)
            sgl = [sb(f"sgl{i}", [128, 512]) for i in range(2)]
            yo = [sb(f"yo{i}", [128, D]) for i in range(2)]
            sqj2 = sb("sqj2", [128, D])
            ss2 = sb("ss2", [128, 1])
            r2 = sb("r2", [128, 1])

            S.dma("pool", "p2w0", wo[:], w_out_p.ap().rearrange("(k p) n -> p k n", p=128), [], [b("wo")])
            S.dma("pool", "p2w1", ident2[:], c_ident.ap(), [], [b("ident2")])
            S.dma("sp", "p2w2", nfcol[:], nfcol_d.ap(), [], [b("nfcol")])
            for j0 in range(0, NFF, 2):
                S.dma("pool", "p2w3", wd[:, j0:j0 + 2, :],
                      w_dn.ap().rearrange("(j p) n -> p j n", p=128)[:, j0:j0 + 2, :], [], [b("wd")])

            pid = nc.sync.partition_id()
            rank = pid % 4
            wgu_v = w_gu.ap().rearrange("(k p) n -> p k n", p=128)
            gctr = [0]

            def load_wg():
                n_ = gctr[0]
                if n_ >= 4 * NSC:
                    return
                gctr[0] += 1
                i, sc = n_ % 3, n_ % NSC
                j0 = sc * WCH * 128
                w_ = min(WCH * 128, DFF - j0)
                S.dma("pool", f"wg{i}", wg[i][:, :, 0, 0:w_], wgu_v[:, :, j0:j0 + w_], [], [b(f"wg{i}")])
                S.dma("pool", f"wg{i}", wg[i][:, :, 1, 0:w_], wgu_v[:, :, DFF + j0:DFF + j0 + w_], [], [b(f"wg{i}")])

            load_wg()
            load_wg()
            wuse = [0]

            pOP = [ps[0], ps[1]]
            pG = [ps[2], ps[3]]
            pU = [ps[4], ps[5]]
            octr = [0]
            yctr = [0]
            for g in range(4):
                S.dma("sp", "mT", mT[:],
                      mixall.ap().rearrange("(jk p) t -> p jk t", p=128)[:, ds(rank * 8, 8), g * 512:(g + 1) * 512],
                      [mixall_b], [b("mT")])
                S.dma("sp", "x1", x1[:], x2.ap()[g * 512:(g + 1) * 512, :].rearrange("(a p) d -> p a d", p=128),
                      [], [b("x1")])
                def stage_a(a):
                    for hf_ in range(2):
                        po = pOP[octr[0] % 2]
                        pob = b(f"pOP{octr[0] % 2}")
                        octr[0] += 1
                        for k in range(8):
                            S.op("pe", [b("mT"), b("wo")], [pob],
                                 lambda e, k=k, po=po, hf_=hf_: e.matmul(po[:, 0:512], lhsT=mT[:, k, a * 128:(a + 1) * 128],
                                                                         rhs=wo[:, k, hf_ * 512:(hf_ + 1) * 512],
                                                                         start=(k == 0), stop=(k == 7)))
                        S.op("dve", [pob, b("x1")], [b("x1")],
                             lambda e, po=po, hf_=hf_: e.tensor_tensor(x1[:, a, hf_ * 512:(hf_ + 1) * 512], po[:, 0:512],
                                                                       x1[:, a, hf_ * 512:(hf_ + 1) * 512], ALU.add))
                    hn_ = hn2[a % 2]
                    S.op("pool", [], [b("ss2")], lambda e: e.memset(ss2[:], 0.0))
                    S.op("act", [b("x1"), b("ss2")], [b("sqj2"), b("ss2")],
                         lambda e: e.activation(sqj2[:], x1[:, a, :], AF.Square, accum_out=ss2[:]))
                    S.op("act", [b("ss2")], [b("r2")],
                         lambda e: e.activation(r2[:], ss2[:], AF.Ln, bias=epsc[:, 0:1]))
                    S.op("act", [b("r2")], [b("r2")],
                         lambda e: e.activation(r2[:], r2[:], AF.Exp, scale=-0.5))
                    S.op("dve", [b("x1"), b("r2")], [b(f"hn{a % 2}")],
                         lambda e: e.tensor_scalar(hn_[:], x1[:, a, :], r2[:, 0:1], 32.0, ALU.mult, ALU.mult))

                def stage_b(a):
                    hn_ = hn2[a % 2]
                    for k in range(8):
                        S.op("pe", [b(f"hn{a % 2}"), b("ident2")], [b("psb")],
                             lambda e, k=k: e.transpose(psb[:, k * 128:(k + 1) * 128], hn_[:, k * 128:(k + 1) * 128], ident2[:]))
                    S.op("dve", [b("psb"), b("nfcol")], [b("hT")],
                         lambda e: e.tensor_tensor(hT[:, :, a * 128:(a + 1) * 128],
                                                   psb[:, 0:1024].rearrange("p (k t) -> p k t", k=8),
                                                   bc(nfcol[:], 2, [128, 8, 128]), ALU.mult))

                stage_a(0)
                for a in range(1, 4):
                    stage_a(a)
                    stage_b(a - 1)
                stage_b(3)
                for j in range(NFF):
                    if j % WCH == 0:
                        i = wuse[0] % 3
                        wuse[0] += 1
                        load_wg()
                    jj = j % WCH
                    pg, pu = pG[j % 2], pU[j % 2]
                    pgb, pub = b(f"pG{j % 2}"), b(f"pU{j % 2}")
                    for k in range(8):
                        S.op("pe", [b("hT"), b(f"wg{i}")], [pgb],
                             lambda e, k=k: e.matmul(pg[:, 0:512], lhsT=wg[i][:, k, 0, jj * 128:(jj + 1) * 128], rhs=hT[:, k, :],
                                                     start=(k == 0), stop=(k == 7)))
                    for k in range(8):
                        S.op("pe", [b("hT"), b(f"wg{i}")], [pub],
                             lambda e, k=k: e.matmul(pu[:, 0:512], lhsT=wg[i][:, k, 1, jj * 128:(jj + 1) * 128], rhs=hT[:, k, :],
                                                     start=(k == 0), stop=(k == 7)))
                    sgt = sgl[j % 2]
                    S.op("act", [pgb], [b(f"sgl{j % 2}")], lambda e: e.activation(sgt[:], pg[:, 0:512], AF.Silu))
                    S.op("dve", [b(f"sgl{j % 2}"), pub], [b("actT")],
                         lambda e: e.tensor_tensor(actT[:, j, :], sgt[:], pu[:, 0:512], ALU.mult))
                for a in range(4):
                    yt = yo[yctr[0] % 2]
                    ytb = b(f"yo{yctr[0] % 2}")
                    ych = f"y{yctr[0] % 2}"
                    yctr[0] += 1
                    for hf_ in range(2):
                        po = pOP[octr[0] % 2]
                        pob = b(f"pOP{octr[0] % 2}")
                        octr[0] += 1
                        for j in range(NFF):
                            S.op("pe", [b("actT"), b("wd")], [pob],
                                 lambda e, j=j, po=po: e.matmul(po[:, 0:512], lhsT=actT[:, j, a * 128:(a + 1) * 128],
                                                                rhs=wd[:, j, hf_ * 512:(hf_ + 1) * 512],
                                                                start=(j == 0), stop=(j == NFF - 1)))
                        S.op("dve", [pob, b("x1")], [ytb],
                             lambda e, po=po: e.tensor_tensor(yt[:, hf_ * 512:(hf_ + 1) * 512], po[:, 0:512],
                                                              x1[:, a, hf_ * 512:(hf_ + 1) * 512], ALU.add))
                    r0 = g * 512 + a * 128
                    S.dma("sp", ych, y.ap()[r0:r0 + 128, :], yt[:], [ytb], [b("ydram")])
            S.barrier(["sp"])
        print("instr counts", S.cnt, "waits", S.nwait, "sems", 5 + len(S.chsem))
    return nc


def _consts():
    ident = np.eye(128, dtype=np.float32)
    s = np.arange(128)
    same = (s[:, None] // 64) == (s[None, :] // 64)
    tri = (same & (s[:, None] <= s[None, :])).astype(np.float32)
    tri2t = (same & (s[:, None] > s[None, :])).astype(np.float32)
    i = np.arange(512)[:, None]
    j = np.arange(128)[None, :]
    ov = ((i * 16 < j * 64 + 64) & (i * 16 + 32 > j * 64)).astype(np.float32)
    ov[:, 0] = 1.0
    q = np.arange(128)[:, None]
    xcol = np.arange(256)[None, :]
    jrel = xcol - 128 - (q >= 64)
    wa = (jrel < -1).astype(np.float32)
    wb = np.where(jrel > 0, -1.0, np.where(jrel >= -1, 1.0e4, 0.0)).astype(np.float32)
    key = np.arange(T)[None, :]
    e = (((key // 64) % 64) == np.arange(64)[:, None]).astype(np.float32) * 30000.0
    return dict(c_ident=ident, c_tri=tri, c_tri2t=tri2t, c_ov=ov, c_wa=wa, c_wb=wb, c_e=e)


def make_in_maps(x, norm_mix, w_in, q_norm, k_norm, cmp_pos_k, cmp_pos_v, cmp_k_w1, cmp_k_w2, cmp_v_w1, cmp_v_w2,
                 hgrn_lb_logits, hgrn_o_norm, w_out, norm_ffn, w_gate_up, w_down):
    f = np.float32
    x = np.asarray(x, f)
    w_in0 = np.asarray(w_in, f)[0]
    consts = _consts()
    offs = np.cumsum([0, 512, 128, 128, 128, 128, 128, 128, 24, 512, 512, 512, 512])
    o_q, o_kc, o_vc, o_ks, o_vs, o_kw, o_vw, o_g, o_hq, o_hf, o_hi, o_hg = [int(v) for v in offs[:12]]
    rep = lambda v, n=128: np.ascontiguousarray(np.broadcast_to(np.asarray(v, f)[None, :], (n, len(v))))
    qn = np.asarray(q_norm, f)[0]
    kn = np.asarray(k_norm, f)[0]
    gq6 = np.concatenate([rep(qn)] * 4 + [rep(kn[1]), rep(kn[2])], axis=1)
    col8 = lambda v: np.ascontiguousarray(np.asarray(v, f).reshape(8, 128).T)
    lbl = np.asarray(hgrn_lb_logits, f)
    w_out0 = np.asarray(w_out, f)[0]
    in_maps = []
    for c in range(8):
        bi, s = c // 4, c % 4
        g = s // 2
        own = [2 * s, 2 * s + 1]
        oth = [h for h in range(4 * g, 4 * g + 4) if h not in own]
        hc = lambda o, h, w: list(range(o + h * w, o + (h + 1) * w))
        cols_tm = []
        for h in own + oth:
            cols_tm += hc(o_q, h, 64)
        cols_tm += hc(o_ks, g, 64) + hc(o_kw, g, 64) + hc(o_vs, g, 64) + hc(o_vw, g, 64)
        cols_tm += hc(o_hf, s, 128) + hc(o_hi, s, 128) + hc(o_hg, s, 128)
        for h in own:
            cols_tm += hc(o_g, h, 3)
        cols_fm = hc(o_kc, g, 64) + hc(o_vc, g, 64) + hc(o_hq, s, 128) + hc(o_hf, s, 128)
        hs = slice(s * 128, (s + 1) * 128)
        rows_out = []
        for s2 in range(4):
            rows_out += list(range(2 * s2 * 64, (2 * s2 + 2) * 64)) + list(range(512 + s2 * 128, 512 + (s2 + 1) * 128))
        m = dict(consts)
        m.update(
            x_b=np.ascontiguousarray(x[bi]),
            x2=np.ascontiguousarray(x[bi, s * 2048:(s + 1) * 2048]),
            w_tm=np.ascontiguousarray(w_in0[:, cols_tm]),
            w_fm=np.ascontiguousarray(w_in0[:, cols_fm]),
            w1k=np.asarray(cmp_k_w1, f)[0], w1v=np.asarray(cmp_v_w1, f)[0],
            w2kv=np.ascontiguousarray(np.concatenate([np.asarray(cmp_k_w2, f)[0], np.asarray(cmp_v_w2, f)[0]], axis=1)),
            posT=np.ascontiguousarray(np.concatenate([np.asarray(cmp_pos_k, f)[0].T, np.asarray(cmp_pos_v, f)[0].T], axis=0)),
            gq6=gq6, gkc=rep(kn[0]), gon=rep(np.asarray(hgrn_o_norm, f)[0]),
            nmcol=col8(np.asarray(norm_mix, f)[0]),
            lbrow=np.concatenate([rep(lbl[0, hs]), rep(lbl[1, hs])], axis=1),
            lbc=np.ascontiguousarray(np.stack([lbl[0, hs], lbl[1, hs]], axis=1)),
            w_out_p=np.ascontiguousarray(w_out0[rows_out]),
            w_gu=np.asarray(w_gate_up, f)[0], w_dn=np.asarray(w_down, f)[0],
            nfcol=col8(np.asarray(norm_ffn, f)[0]),
        )
        in_maps.append(m)
    return in_maps


def kernel(**inputs):
    in_maps = make_in_maps(**inputs)
    nc = build_nc()
    res = run_bass_kernel_spmd(nc, in_maps, core_ids=list(range(8)))
    out = np.empty((2, T, D), np.float32)
    for c in range(8):
        bi, s = c // 4, c % 4
        out[bi, s * 2048:(s + 1) * 2048] = res.results[c]["y"]
    return out
```

```python
import numpy as np
from contextlib import ExitStack

import concourse.bass as bass
import concourse.mybir as mybir
from concourse.bass import ds
from concourse.bass_utils import run_bass_kernel_spmd

F32 = mybir.dt.float32
BF16 = mybir.dt.bfloat16
AF = mybir.ActivationFunctionType
ALU = mybir.AluOpType
AX = mybir.AxisListType

T = 8192
D = 1024
NT = T // 128
DFF = 2816
NFF = DFF // 128
EPS = 1e-6
NEG = -1.0e30
SAME_ENGINE_SYNC = True
EPSI = {float(D * EPS): 0, float(64 * EPS): 1, float(128 * EPS): 2}
N_TILES = NT
FE_STEPS = 56
STOP_AT = 10 ** 9
HOIST_MAX = 63
PIPELINE = True
DEBUG = False
RUN_CC = True
RUN_P2 = True


class Buf:
    __slots__ = ("name", "w", "r", "excl")

    def __init__(self, name):
        self.name = name
        self.w = []
        self.r = {}
        self.excl = name.startswith("ps")


class Sched:
    def __init__(self, nc, es):
        self.nc = nc
        self.es = es
        self.eng = {"pe": nc.tensor, "act": nc.scalar, "dve": nc.vector, "pool": nc.gpsimd, "sp": nc.sync}
        self.sem = {k: es.enter_context(nc.semaphore("sem_" + k)) for k in self.eng}
        self.cnt = {k: 0 for k in self.eng}
        self.known = {k: {} for k in self.eng}
        self.chsem = {}
        self.chcnt = {}
        self.nwait = 0

    def _sem_of(self, ev):
        return self.sem[ev[1]] if ev[0] == "e" else self.chsem[ev[1]]

    def _waits(self, e, reads, writes):
        need = {}
        for b in reads:
            for ev in b.w:
                k = (ev[0], ev[1])
                need[k] = max(need.get(k, 0), ev[2])
            if b.excl:
                for k, v in b.r.items():
                    if not (k[0] == "e" and k[1] == e):
                        need[k] = max(need.get(k, 0), v)
        for b in writes:
            for ev in b.w:
                k = (ev[0], ev[1])
                need[k] = max(need.get(k, 0), ev[2])
            for k, v in b.r.items():
                need[k] = max(need.get(k, 0), v)
        for k, v in need.items():
            if k[0] == "e" and k[1] == e and (e == "pe" or not SAME_ENGINE_SYNC):
                continue
            if self.known[e].get(k, 0) >= v:
                continue
            sem = self.sem[k[1]] if k[0] == "e" else self.chsem[k[1]]
            self.eng[e].wait_ge(sem, v)
            self.known[e][k] = v
            self.nwait += 1

    def _record(self, ev, reads, writes):
        k = (ev[0], ev[1])
        for b in reads:
            b.r[k] = max(b.r.get(k, 0), ev[2])
        for b in writes:
            b.w = [ev]
            b.r = {}

    def op(self, e, reads, writes, fn):
        self._waits(e, reads, writes)
        inst = fn(self.eng[e])
        self.cnt[e] += 1
        inst.then_inc(self.sem[e], 1)
        self._record(("e", e, self.cnt[e]), reads, writes)

    def dma(self, q, ch, out, in_, reads, writes, **kw):
        if ch not in self.chsem:
            self.chsem[ch] = self.es.enter_context(self.nc.semaphore("ch_" + ch))
            self.chcnt[ch] = 0
        self._waits(q, reads, writes)
        inst = self.eng[q].dma_start(out=out, in_=in_, **kw)
        self.chcnt[ch] += 16
        inst.then_inc(self.chsem[ch], 16)
        self._record(("d", ch, self.chcnt[ch]), reads, writes)

    def collective(self, ch, reads, writes, fn):
        if ch not in self.chsem:
            self.chsem[ch] = self.es.enter_context(self.nc.semaphore("ch_" + ch))
            self.chcnt[ch] = 0
        self._waits("pool", reads, writes)
        inst = fn(self.eng["pool"])
        self.chcnt[ch] += 1
        inst.then_inc(self.chsem[ch], 1)
        self._record(("d", ch, self.chcnt[ch]), reads, writes)

    def barrier(self, engines=None):
        engines = engines or list(self.eng)
        for e in engines:
            for e2 in self.eng:
                if e2 == e or self.cnt[e2] == 0:
                    continue
                k = ("e", e2)
                if self.known[e].get(k, 0) < self.cnt[e2]:
                    self.eng[e].wait_ge(self.sem[e2], self.cnt[e2])
                    self.known[e][k] = self.cnt[e2]
            for ch, v in self.chcnt.items():
                k = ("d", ch)
                if v and self.known[e].get(k, 0) < v:
                    self.eng[e].wait_ge(self.chsem[ch], v)
                    self.known[e][k] = v


def interleave(a, b_):
    da = db = False
    while not (da and db):
        if not da:
            try:
                next(a)
                yield
            except StopIteration:
                da = True
        if not db:
            try:
                next(b_)
                yield
            except StopIteration:
                db = True


def bc(ap, axis, shape):
    return ap.unsqueeze(axis).to_broadcast(list(shape))


def build_nc():
    nc = bass.Bass("TRN2", target_bir_lowering=False)

    def din(name, shape, dt=F32):
        return nc.dram_tensor(name, list(shape), dt, kind="ExternalInput")

    x_b = din("x_b", [T, D])
    x2 = din("x2", [2048, D])
    w_tm = din("w_tm", [D, 902])
    w_fm = din("w_fm", [D, 384])
    w1k = din("w1k", [2048, 128])
    w1v = din("w1v", [2048, 128])
    w2kv = din("w2kv", [128, 128])
    posT = din("posT", [128, 32])
    c_ident = din("c_ident", [128, 128])
    c_tri = din("c_tri", [128, 128])
    c_tri2t = din("c_tri2t", [128, 128])
    c_ov = din("c_ov", [512, 128])
    c_wa = din("c_wa", [128, 256])
    c_wb = din("c_wb", [128, 256])
    c_e = din("c_e", [64, T])
    gq6_d = din("gq6", [128, 384])
    gkc_d = din("gkc", [128, 64])
    gon_d = din("gon", [128, 128])
    nmcol_d = din("nmcol", [128, 8])
    lbrow_d = din("lbrow", [128, 256])
    lbc_d = din("lbc", [128, 2])
    w_out_p = din("w_out_p", [D, D])
    w_gu = din("w_gu", [D, 2 * DFF])
    w_dn = din("w_dn", [DFF, D])
    nfcol_d = din("nfcol", [128, 8])
    y = nc.dram_tensor("y", [2048, D], F32, kind="ExternalOutput")

    dbg = nc.dram_tensor("dbg", [1024, 2048], BF16, kind="ExternalOutput") if DEBUG else None
    mixsrc = nc.dram_tensor("mixsrc", [1024, 2048], BF16)
    mixall = nc.dram_tensor("mixall", [4096, 2048], BF16)
    mixsrc_b = Buf("mixsrc")
    mixall_b = Buf("mixall")

    with ExitStack() as es_all:
        S = Sched(nc, es_all)
        ps = [es_all.enter_context(nc.psum_tensor(f"ps{i}", [128, 512], F32)) for i in range(7)]
        psb = es_all.enter_context(nc.psum_tensor("psb", [128, 1024], BF16))

        with ExitStack() as es:
            def sb(name, shape, dt=F32):
                return es.enter_context(nc.sbuf_tensor(name, list(shape), dt))

            wtm = sb("wtm", [128, 8, 902], BF16)
            wfm = sb("wfm", [128, 8, 384], BF16)
            W1k_ = sb("W1k_", [64, 32, 128], BF16)
            W1v_ = sb("W1v_", [64, 32, 128], BF16)
            W1s = [W1k_, W1v_]
            w2 = sb("w2", [128, 128], BF16)
            posTk = sb("posTk", [64, 32], BF16)
            posTv = sb("posTv", [64, 32], BF16)
            posTs = [posTk, posTv]
            ident = sb("ident", [128, 128], BF16)
            tri = sb("tri", [128, 128], F32)
            tri2t = sb("tri2t", [128, 128], F32)
            ovsb = sb("ovsb", [128, 4, 128], BF16)
            ones_c = sb("ones_c", [128, 1], BF16)
            gq6 = sb("gq6s", [128, 384])
            gkc = sb("gkcs", [128, 64])
            gon = sb("gons", [128, 128])
            wa = sb("was", [128, 256])
            wb = sb("wbs", [128, 256])
            nmcol = sb("nmcols", [128, 8])
            lbrow = sb("lbrows", [128, 256])
            lbB = sb("lbB", [128, 128])
            omlB = sb("omlB", [128, 128])
            lbc = sb("lbcs", [128, 2])
            lbcol = sb("lbcol", [128, 1])
            omlc = sb("omlc", [128, 1])
            nomlc = sb("nomlc", [128, 1])
            cbias = sb("cbias", [128, 2])

            KE = sb("KE", [128, T], BF16)
            kwT = sb("kwT", [64, T], BF16)
            kcrT = sb("kcrT", [64, T], BF16)
            vcrT = sb("vcrT", [64, T], BF16)
            kvr = [kcrT, vcrT]
            vsA = sb("vsA", [128, NT, 65], BF16)
            vwA = sb("vwA", [128, NT, 65], BF16)
            kcT = sb("kcT", [64, 512], BF16)
            vcT = sb("vcT", [64, 512], BF16)
            vcA = sb("vcA", [128, 4, 65], BF16)
            Sst = sb("Sst", [128, 128])
            Sbf = [sb(f"Sbf{i}", [128, 128], BF16) for i in range(2)]
            QpTz = [sb(f"QpTz{i}", [128, 128], BF16) for i in range(2)]
            Kppz = [sb(f"Kppz{i}", [128, 128], BF16) for i in range(2)]

            NXS = 3
            xt = [sb(f"xt{i}", [128, D]) for i in range(NXS)]
            sqj = sb("sqj", [128, D])
            ssx = sb("ssx", [128, 1])
            rx = sb("rx", [128, 1])
            xn = sb("xn", [128, D], BF16)
            xnT = [sb(f"xnT{i}", [128, 8, 128], BF16) for i in range(2)]
            sq6 = sb("sq6", [128, 384])
            ss6 = sb("ss6", [128, 6])
            r6 = sb("r6", [128, 6])
            t6 = sb("t6", [128, 384])
            qk6 = sb("qk6", [128, 384], BF16)
            gts = [sb(f"gts{i}", [128, 6]) for i in range(2)]
            qT4 = sb("qT4", [64, 4, 128], BF16)
            QB = [[sb(f"QB{p_}{i}", [128, 2, 128], BF16) for i in range(2)] for p_ in range(2)]
            ocmp = [sb(f"ocmp{i}", [128, 2, 64]) for i in range(2)]
            cfc = sb("cfc", [128, 2])
            hy = sb("hy", [128, 16])
            hy2 = sb("hy2", [128, 16])
            hsg = sb("hsg", [128, 16])
            hid = sb("hid", [128, 16], BF16)
            ksq = sb("ksq", [8, 64])
            kss = sb("kss", [8, 1])
            kr = sb("kr", [8, 1])
            kt1 = sb("kt1", [8, 64])
            kcn = sb("kcn", [8, 64], BF16)
            Pc = [sb(f"Pc{i}", [128, 512], BF16) for i in range(2)]
            Pb = [sb(f"Pb{i}", [128, 512], BF16) for i in range(3)]
            dn4 = sb("dn4", [128, 4])
            rd4 = sb("rd4", [128, 4])
            imp = sb("imp", [128, 128])
            imp2 = sb("imp2", [128, 128])
            m8a = sb("m8a", [128, 8])
            m8b = sb("m8b", [128, 8])
            thr = sb("thr", [128, 1])
            selb = sb("selb", [128, 128], BF16)
            selsw = sb("selsw", [128, 128], BF16)
            dn6 = sb("dn6", [128, 6])
            rd6 = sb("rd6", [128, 6])
            coef = sb("coef", [128, 6])
            oacc = sb("oacc", [128, 64])
            mix = [sb(f"mix{i}", [128, 256], BF16) for i in range(2)]
            mixT = sb("mixT", [128, 2, 128], BF16)
            sgT = sb("sgT", [128, 128])
            qTf = sb("qTf", [128, 128])
            kTf = sb("kTf", [128, 128])
            sg = sb("sg", [128, 128])
            ff = sb("ff", [128, 128])
            logf = sb("logf", [128, 128])
            ktm = sb("ktm", [128, 128])
            gsil = sb("gsil", [128, 128])
            vbf = sb("vbf", [128, 128], BF16)
            ebT = sb("ebT", [128, 128])
            enbT = sb("enbT", [128, 128])
            eblmb = sb("eblmb", [128, 128])
            QpT = sb("QpT", [128, 128], BF16)
            KpT = sb("KpT", [128, 128], BF16)
            Kpp = sb("Kpp", [128, 128], BF16)
            attnT = sb("attnT", [128, 128], BF16)
            sso = sb("sso", [128, 1])
            ro = sb("ro", [128, 1])
            o1 = sb("o1", [128, 128])
            o2 = sb("o2", [128, 128])

            B = {}

            def b(name):
                if name not in B:
                    B[name] = Buf(name)
                return B[name]

            epsc = sb("epsc", [128, 3])
            for v_, i_ in EPSI.items():
                S.op("pool", [], [b("epsc")], lambda e, v_=v_, i_=i_: e.memset(epsc[:, i_:i_ + 1], v_))
            S.barrier(["act"])

            cst = b("const")
            S.dma("pool", "cw5", ident[:], c_ident.ap(), [], [b("ident")])
            S.dma("pool", "cw0", wtm[:], w_tm.ap().rearrange("(k p) n -> p k n", p=128), [], [b("wtm")])
            S.dma("pool", "cw1", wfm[:], w_fm.ap().rearrange("(k p) n -> p k n", p=128), [], [b("wfm")])
            S.dma("pool", "cw2", W1k_[:], w1k.ap().rearrange("(p d) h -> d p h", d=64), [], [b("W1a")])
            S.dma("pool", "cw2", W1v_[:], w1v.ap().rearrange("(p d) h -> d p h", d=64), [], [b("W1b")])
            S.dma("pool", "cw3", w2[:], w2kv.ap(), [], [b("w2")])
            S.dma("pool", "cw3", posTk[:], posT.ap()[0:64, :], [], [b("posT")])
            S.dma("pool", "cw3", posTv[:], posT.ap()[64:128, :], [], [b("posT2")])
            S.dma("pool", "cw3", ovsb[:], c_ov.ap().rearrange("(c p) j -> p c j", p=128), [], [b("ov")])
            S.dma("pool", "cw4", KE[64:128, :], c_e.ap(), [], [b("KEe")])
            for bb_ in ("w2", "posT", "posT2", "ov"):
                b(bb_).w = [("d", "cw3", S.chcnt["cw3"])]
            b("W1a").w = [("d", "cw2", S.chcnt["cw2"])]
            for i, (dst, src) in enumerate([(tri, c_tri), (tri2t, c_tri2t), (gq6, gq6_d), (gkc, gkc_d), (gon, gon_d),
                                            (wa, c_wa), (wb, c_wb), (nmcol, nmcol_d), (lbrow, lbrow_d), (lbc, lbc_d)]):
                S.dma("sp", "cs0", dst[:], src.ap(), [], [cst])
            cst.w = [("d", "cs0", S.chcnt["cs0"])]

            S.op("pool", [], [b("ones")], lambda e: e.memset(ones_c[:], 1.0))
            S.op("pool", [], [b("vsAones")], lambda e: e.memset(vsA[:, :, 64:65], 1.0))
            S.op("pool", [], [b("vwAones")], lambda e: e.memset(vwA[:, :, 64:65], 1.0))
            S.op("pool", [], [b("vcA")], lambda e: e.memset(vcA[:, :, 0:64], 0.0))
            S.op("pool", [], [b("vcAones")], lambda e: e.memset(vcA[:, :, 64:65], 1.0))
            S.op("pool", [], [b("kcT")], lambda e: e.memset(kcT[:], 0.0))
            S.op("pool", [], [b("vcT")], lambda e: e.memset(vcT[:], 0.0))
            S.op("pool", [], [b("Sst")], lambda e: e.memset(Sst[:], 0.0))
            for i in range(2):
                S.op("pool", [], [b(f"Sbf{i}")], lambda e, i=i: e.memset(Sbf[i][:], 0.0))
                S.op("pool", [], [b(f"QpTz{i}")], lambda e, i=i: e.memset(QpTz[i][:], 0.0))
                S.op("pool", [], [b(f"Kppz{i}")], lambda e, i=i: e.memset(Kppz[i][:], 0.0))
            S.op("dve", [cst], [b("lbB")], lambda e: e.tensor_sub(lbB[:], lbrow[:, 0:128], lbrow[:, 128:256]))
            S.op("act", [b("lbB")], [b("lbB")], lambda e: e.activation(lbB[:], lbB[:], AF.Sigmoid))
            S.op("dve", [b("lbB")], [b("omlB")],
                 lambda e: e.tensor_scalar(omlB[:], lbB[:], -1.0, 1.0, ALU.mult, ALU.add))
            S.op("dve", [cst], [b("lbcol")], lambda e: e.tensor_sub(lbcol[:], lbc[:, 0:1], lbc[:, 1:2]))
            S.op("act", [b("lbcol")], [b("lbcol")], lambda e: e.activation(lbcol[:], lbcol[:], AF.Sigmoid))
            S.op("dve", [b("lbcol")], [b("omlc")],
                 lambda e: e.tensor_scalar(omlc[:], lbcol[:], -1.0, 1.0, ALU.mult, ALU.add))
            S.op("dve", [b("lbcol")], [b("nomlc")],
                 lambda e: e.tensor_scalar(nomlc[:], lbcol[:], 1.0, -1.0, ALU.mult, ALU.add))
            pX0 = ps[3][:, 0:16]
            for kv in range(2):
                for p in range(32):
                    S.op("pe", [b("W1a"), b("W1b"), b("posT"), b("posT2")], [b("ps3")],
                         lambda e, p=p, kv=kv: e.matmul(pX0[:, kv:kv + 1], lhsT=W1s[kv][:, p, :],
                                                        rhs=posTs[kv][:, p:p + 1],
                                                        start=(p == 0), stop=(p == 31)))
            S.op("dve", [b("ps3")], [b("cbias")], lambda e: e.tensor_copy(cbias[:], pX0[:, 0:2]))

            def load_x(qb):
                sl = qb % NXS
                S.dma("sp", f"x{sl}", xt[sl][:], x_b.ap()[qb * 128:(qb + 1) * 128, :], [], [b(f"xt{sl}")])

            load_x(0)
            if N_TILES > 1:
                load_x(1)

            psA, psB_, psC = ps[0], ps[1], ps[2]
            F3 = ps[3]
            F3b = psb
            psO = ps[6]
            psSWs = [ps[4][:, 0:256], ps[5][:, 0:256]]
            psSW2 = [ps[4], ps[5]]
            swtok = ["ps4", "ps5"]
            pX = F3[:, 0:16]
            pY = F3[0:8, 16:80]
            pZ = F3[0:64, 80:88]
            pGt = ps[1][:, 384:390]
            pKT = psb[0:64, 768:776]
            pVT = psb[:, 776:840]
            pT1 = psb[:, 840:968]
            pSC = ps[2]

            def impap(r):
                return ps[1][:, r * 128:(r + 1) * 128]

            def imptok(r):
                return b("ps1")

            def ocap(h):
                return F3[:, 128 + h * 64:192 + h * 64]

            def xstage(q):
                yield from xstage_a(q)
                yield from xstage_b(q)

            def xstage_a(q):
                sl = q % NXS
                xs = xt[sl]
                xb = b(f"xt{sl}")
                S.op("pool", [], [b("ssx")], lambda e: e.memset(ssx[:], 0.0))
                S.op("act", [xb, b("ssx")], [b("sqj"), b("ssx")],
                     lambda e: e.activation(sqj[:], xs[:], AF.Square, accum_out=ssx[:]))
                S.op("act", [b("ssx")], [b("rx")],
                     lambda e: e.activation(rx[:], ssx[:], AF.Ln, bias=epsc[:, 0:1]))
                S.op("act", [b("rx")], [b("rx")], lambda e: e.activation(rx[:], rx[:], AF.Exp, scale=-0.5))
                yield
                S.op("dve", [xb, b("rx")], [b("xn")],
                     lambda e: e.tensor_scalar(xn[:], xs[:], rx[:, 0:1], 32.0, ALU.mult, ALU.mult))
                yield

            def xstage_b(q):
                xT = xnT[q % 2]
                xTb = b(f"xnT{q % 2}")
                for k in range(8):
                    S.op("pe", [b("xn"), b("ident")], [b("psb")],
                         lambda e, k=k: e.transpose(F3b[:, k * 128:(k + 1) * 128], xn[:, k * 128:(k + 1) * 128], ident[:]))
                S.op("dve", [b("psb"), cst], [xTb],
                     lambda e: e.tensor_tensor(xT[:], F3b[:, 0:1024].rearrange("p (k t) -> p k t", k=8),
                                               bc(nmcol[:], 2, [128, 8, 128]), ALU.mult))
                yield

            def hoisted(q):
                return 1 <= q <= HOIST_MAX and q < N_TILES

            def frontend(qb):
                par = qb % 2
                t0 = qb * 128
                xT = xnT[qb % 2]
                xTb = b(f"xnT{qb % 2}")
                if qb + 2 < N_TILES:
                    load_x(qb + 2)
                if not hoisted(qb):
                    if qb == 0:
                        yield from xstage_a(qb)
                    yield from xstage_b(qb)
                for k in range(8):
                    S.op("pe", [xTb, b("wtm")], [b("ps0")],
                         lambda e, k=k: e.matmul(psA[:, 0:512], lhsT=xT[:, k, :], rhs=wtm[:, k, 0:512],
                                                 start=(k == 0), stop=(k == 7)))
                    if k % 2 == 1:
                        yield
                for k in range(8):
                    S.op("pe", [xTb, b("wtm")], [b("ps1")],
                         lambda e, k=k: e.matmul(psB_[:, 0:390], lhsT=xT[:, k, :], rhs=wtm[:, k, 512:902],
                                                 start=(k == 0), stop=(k == 7)))
                    if k % 2 == 1:
                        yield
                for c, (m0, m1) in enumerate([(0, 64), (64, 128), (128, 256), (256, 384)]):
                    for k in range(8):
                        S.op("pe", [xTb, b("wfm")], [b("ps2")],
                             lambda e, k=k, c=c, m0=m0, m1=m1: e.matmul(psC[0:m1 - m0, c * 128:(c + 1) * 128],
                                                                        lhsT=wfm[:, k, m0:m1], rhs=xT[:, k, :],
                                                                        start=(k == 0), stop=(k == 7)))
                    yield
                S.op("act", [b("ps0")], [b("sq6")], lambda e: e.activation(sq6[:], psA[:, 0:384], AF.Square))
                S.op("dve", [b("sq6")], [b("ss6")],
                     lambda e: e.tensor_reduce(ss6[:], sq6[:].rearrange("p (h d) -> p h d", h=6), AX.X, ALU.add))
                S.op("act", [b("ss6")], [b("r6")], lambda e: e.activation(r6[:], ss6[:], AF.Ln, bias=epsc[:, 1:2]))
                S.op("act", [b("r6")], [b("r6")], lambda e: e.activation(r6[:], r6[:], AF.Exp, scale=-0.5))
                yield
                S.op("dve", [b("r6")], [b("r6")], lambda e: e.tensor_scalar_mul(r6[:, 4:6], r6[:, 4:6], 8.0))
                S.op("dve", [b("ps0"), b("r6")], [b("t6")],
                     lambda e: e.tensor_tensor(t6[:].rearrange("p (h d) -> p h d", h=6),
                                               psA[:, 0:384].rearrange("p (h d) -> p h d", h=6),
                                               bc(r6[:], 2, [128, 6, 64]), ALU.mult))
                S.op("dve", [b("t6"), cst], [b("qk6")], lambda e: e.tensor_tensor(qk6[:], t6[:], gq6[:], ALU.mult))
                yield
                S.op("act", [b("ps0")], [b(f"vsA{qb}")], lambda e: e.copy(vsA[:, qb, 0:64], psA[:, 384:448]))
                S.op("act", [b("ps0")], [b(f"vwA{qb}")], lambda e: e.copy(vwA[:, qb, 0:64], psA[:, 448:512]))
                S.op("act", [b("ps2")], [b("kvcT")], lambda e: e.copy(kcrT[:, t0:t0 + 128], psC[0:64, 0:128]))
                S.op("act", [b("ps2")], [b("kvcT")], lambda e: e.copy(vcrT[:, t0:t0 + 128], psC[0:64, 128:256]))
                yield
                for h in range(6):
                    S.op("pe", [b("qk6"), b("ident")], [b("psb")],
                         lambda e, h=h: e.transpose(F3b[0:64, h * 128:(h + 1) * 128], qk6[:, h * 64:(h + 1) * 64], ident[:]))
                S.op("act", [b("psb")], [b("qT4")],
                     lambda e: e.copy(qT4[:], F3b[0:64, 0:512].rearrange("p (h t) -> p h t", h=4)))
                for i in range(2):
                    if i == 1 and qb < 32:
                        continue
                    S.op("dve", [b("psb")], [b(f"QBq{par}{i}")],
                         lambda e, i=i: e.tensor_copy(QB[par][i][0:64], F3b[0:64, 0:256].rearrange("p (h t) -> p h t", h=2)))
                S.op("act", [b("psb")], [b(f"KEk{qb}")], lambda e: e.copy(KE[0:64, t0:t0 + 128], F3b[0:64, 512:640]))
                S.op("act", [b("psb")], [b(f"kwT{qb}")], lambda e: e.copy(kwT[:, t0:t0 + 128], F3b[0:64, 640:768]))
                yield
                i0 = max(0, 8 * qb - 1)
                n = 8 * qb + 7 - i0
                for kv in range(2):
                    for p in range(32):
                        st = 16 * i0 + p
                        S.op("pe", [b("W1a"), b("W1b"), b("kvcT")], [b("ps3")],
                             lambda e, p=p, kv=kv, st=st: e.matmul(
                                 pX[:, kv * 8:kv * 8 + n], lhsT=W1s[kv][:, p, :],
                                 rhs=kvr[kv][:, st:st + 16 * (n - 1) + 1:16], start=(p == 0), stop=(p == 31)))
                        if p % 8 == 7:
                            yield
                v3 = lambda t_: t_.rearrange("p (k n) -> p k n", k=2)[:, :, 0:n]
                S.op("dve", [b("ps3"), b("cbias")], [b("hy")],
                     lambda e: e.tensor_tensor(v3(hy[:]), v3(pX), bc(cbias[:], 2, [128, 2, n]), ALU.add))
                S.op("dve", [b("hy")], [b("hy2")], lambda e: e.tensor_tensor(v3(hy2[:]), v3(hy[:]), v3(hy[:]), ALU.mult))
                S.op("dve", [b("hy2")], [b("hy2")],
                     lambda e: e.tensor_scalar(v3(hy2[:]), v3(hy2[:]), 0.044715, 1.0, ALU.mult, ALU.add))
                S.op("dve", [b("hy2"), b("hy")], [b("hy2")],
                     lambda e: e.tensor_tensor(v3(hy2[:]), v3(hy2[:]), v3(hy[:]), ALU.mult))
                yield
                gt = gts[par]

                def sig_fin(t, tok):
                    S.op("dve", [tok], [tok], lambda e: e.tensor_scalar(t, t, 1.0, None, ALU.add))
                    S.op("dve", [tok], [tok], lambda e: e.reciprocal(t, t))

                S.op("act", [b("hy2")], [b("hsg")],
                     lambda e: e.activation(v3(hsg[:]), v3(hy2[:]), AF.Exp, scale=-1.5957691216057308))
                S.op("act", [b("ps1")], [b(f"gts{par}")], lambda e: e.activation(gt[:], pGt, AF.Exp, scale=-1.0))
                S.op("act", [b("ps2")], [b("sgT")], lambda e: e.activation(sgT[:], psC[:, 384:512], AF.Exp, scale=-1.0))
                S.op("act", [b("ps1")], [b("sg")], lambda e: e.activation(sg[:], psB_[:, 0:128], AF.Exp, scale=-1.0))
                S.op("act", [b("ps2")], [b("qTf")], lambda e: e.activation(qTf[:], psC[:, 256:384], AF.Exp, scale=-1.0))
                S.op("act", [b("ps1")], [b("gsil")], lambda e: e.activation(gsil[:], psB_[:, 256:384], AF.Exp, scale=-1.0))
                S.op("act", [b("ps1")], [b("vbf")], lambda e: e.copy(vbf[:], psB_[:, 128:256]))
                yield
                def chain_a():
                    sig_fin(v3(hsg[:]), b("hsg"))
                    sig_fin(gt[:], b(f"gts{par}"))
                    S.op("dve", [b("hsg"), b("hy")], [b("hid")],
                         lambda e: e.tensor_tensor(v3(hid[:]), v3(hy[:]), v3(hsg[:]), ALU.mult))
                    yield
                    S.op("pe", [b("hid"), b("w2")], [b("ps3")],
                         lambda e: e.matmul(pY[0:n, :], lhsT=hid[:, 0:n], rhs=w2[:, 0:64], start=True, stop=True))
                    S.op("pe", [b("hid"), b("w2")], [b("ps3")],
                         lambda e: e.matmul(pZ[:, 0:n], lhsT=w2[:, 64:128], rhs=hid[:, 8:8 + n], start=True, stop=True))
                    S.op("act", [b("ps3")], [b("ksq")], lambda e: e.activation(ksq[0:n], pY[0:n, :], AF.Square))
                    S.op("dve", [b("ksq")], [b("kss")], lambda e: e.tensor_reduce(kss[0:n], ksq[0:n], AX.X, ALU.add))
                    S.op("act", [b("kss")], [b("kr")], lambda e: e.activation(kr[0:n], kss[0:n], AF.Ln, bias=epsc[0:n, 1:2]))
                    S.op("act", [b("kr")], [b("kr")], lambda e: e.activation(kr[0:n], kr[0:n], AF.Exp, scale=-0.5))
                    yield
                    S.op("dve", [b("ps3"), b("kr")], [b("kt1")],
                         lambda e: e.tensor_scalar(kt1[0:n], pY[0:n, :], kr[0:n, 0:1], 8.0, ALU.mult, ALU.mult))
                    S.op("dve", [b("kt1"), cst], [b("kcn")], lambda e: e.tensor_tensor(kcn[0:n], kt1[0:n], gkc[0:n], ALU.mult))
                    S.op("act", [b("ps3")], [b("vcT")], lambda e: e.copy(vcT[:, i0:i0 + n], pZ[:, 0:n]))
                    S.op("pe", [b("kcn"), b("ident")], [b("psb")],
                         lambda e: e.transpose(pKT[:, 0:n], kcn[0:n, :], ident[0:n, 0:n]))
                    S.op("act", [b("psb")], [b("kcT")], lambda e: e.copy(kcT[:, i0:i0 + n], pKT[:, 0:n]))
                    yield
                    for ct in range(i0 // 128, (i0 + n - 1) // 128 + 1):
                        S.op("pe", [b("vcT"), b("ident")], [b("psb")],
                             lambda e, ct=ct: e.transpose(pVT, vcT[:, ct * 128:(ct + 1) * 128], ident[0:64, 0:64]))
                        S.op("act", [b("psb")], [b("vcA")], lambda e, ct=ct: e.copy(vcA[:, ct, 0:64], pVT))
                    yield

                    nct = (8 * qb + 6) // 128 + 1
                    for ct in range(nct):
                        pc = Pc[ct % 2]
                        pcb = b(f"Pc{ct % 2}")
                        S.op("pe", [b("kcT"), b("qT4")], [b("ps2")],
                             lambda e, ct=ct: e.matmul(pSC[:, 0:512], lhsT=kcT[:, ct * 128:(ct + 1) * 128],
                                                       rhs=qT4[:].rearrange("p h t -> p (h t)"), start=True, stop=True))
                        S.op("act", [b("ps2")], [pcb], lambda e, pc=pc: e.activation(pc[:], pSC[:, 0:512], AF.Exp))
                        pcv = pc[:].rearrange("p (h q) -> p h q", h=4)
                        if 16 * (ct * 128 + 127) + 31 > t0:
                            S.op("pool", [pcb], [pcb],
                                 lambda e, pcv=pcv, ct=ct: e.affine_select(out=pcv, in_=pcv, pattern=[[0, 4], [1, 128]],
                                                                           compare_op=ALU.is_ge, fill=0.0,
                                                                           base=t0 - 2048 * ct - 31, channel_multiplier=-16))
                        yield
                        for r in range(4):
                            lw = pc[:, r * 128:(r + 1) * 128]
                            S.op("pe", [pcb, b("ov")], [imptok(r)],
                                 lambda e, r=r, lw=lw, ct=ct: e.matmul(impap(r), lhsT=lw, rhs=ovsb[:, ct, :],
                                                                       start=(ct == 0 and r == 0), stop=(ct == nct - 1),
                                                                       skip_group_check=True))
                            if r < 2:
                                S.op("pe", [pcb, b("vcA")], [b("ps3")],
                                     lambda e, r=r, lw=lw, ct=ct: e.matmul(ocap(r), lhsT=lw,
                                                                           rhs=vcA[:, ct, 0:64], start=(ct == 0 and r == 0),
                                                                           stop=(ct == nct - 1), skip_group_check=True))
                        yield
                    S.op("dve", [b("ps1")], [b("dn4")],
                         lambda e: e.tensor_scalar_max(dn4[:], ps[1][:, 0:512].rearrange("p (r j) -> p r j", r=4)[:, :, 0], 1e-30))
                    S.op("dve", [b("dn4")], [b("rd4")], lambda e: e.reciprocal(rd4[:], dn4[:]))
                    yield
                    S.op("dve", [b("rd4"), b(f"gts{par}")], [b("cfc")],
                         lambda e: e.tensor_tensor(cfc[:], rd4[:, 0:2], gt[:, 0:6:3], ALU.mult))
                    for h in range(2):
                        S.op("dve", [b("ps3"), b("cfc")], [b(f"ocmp{par}")],
                             lambda e, h=h: e.tensor_scalar(ocmp[par][:, h, :], ocap(h),
                                                            cfc[:, h:h + 1], None, ALU.mult))
                    yield
                    S.op("dve", [b("ps1"), b("rd4")], [b("imp")],
                         lambda e: e.tensor_scalar(imp[:], impap(0), rd4[:, 0:1], None, ALU.mult))
                    for r in range(1, 4):
                        S.op("dve", [imptok(r), b("rd4"), b("imp")], [b("imp")],
                             lambda e, r=r: e.scalar_tensor_tensor(imp[:], impap(r), rd4[:, r:r + 1], imp[:],
                                                                   ALU.mult, ALU.add))
                    yield
                    w0 = 128 - 2 * qb
                    S.op("dve", [b("imp"), cst], [b("imp")], lambda e: e.tensor_tensor(imp[:], imp[:], wa[:, w0:w0 + 128], ALU.mult))
                    S.op("dve", [b("imp"), cst], [b("imp")], lambda e: e.tensor_tensor(imp[:], imp[:], wb[:, w0:w0 + 128], ALU.add))
                    S.op("dve", [b("imp")], [b("imp")], lambda e: e.memset(imp[:, 0:1], 1.0e4))
                    yield
                    S.op("dve", [b("imp")], [b("m8a")], lambda e: e.max(m8a[:], imp[:]))
                    S.op("dve", [b("imp"), b("m8a")], [b("imp2")],
                         lambda e: e.match_replace(imp2[:], m8a[:], imp[:], NEG))
                    S.op("dve", [b("imp2")], [b("m8b")], lambda e: e.max(m8b[:], imp2[:]))
                    S.op("dve", [b("m8b")], [b("thr")], lambda e: e.tensor_reduce(thr[:], m8b[:], AX.X, ALU.min))
                    yield
                    S.op("dve", [b("imp"), b("thr")], [b("selsw")],
                         lambda e: e.tensor_scalar(selsw[:, 64:128], imp[:, 0:64], thr[:, 0:1], 1.0, ALU.is_ge, ALU.subtract))
                    S.op("dve", [b("imp"), b("thr")], [b("selsw")],
                         lambda e: e.tensor_scalar(selsw[:, 0:64], imp[:, 64:128], thr[:, 0:1], 1.0, ALU.is_ge, ALU.subtract))
                    S.op("pe", [b("selsw"), b("ident")], [b("psb")], lambda e: e.transpose(pT1, selsw[:], ident[:]))
                    S.op("act", [b("psb")], [b(f"QBb{par}0")],
                         lambda e: e.copy(QB[par][0][64:128], bc(pT1[64:128, :], 1, [64, 2, 128])))
                    yield
                    if qb >= 32:
                        S.op("dve", [b("imp"), b("thr")], [b("selb")],
                             lambda e: e.tensor_scalar(selb[:], imp[:], thr[:, 0:1], 1.0, ALU.is_ge, ALU.subtract))
                        S.op("pe", [b("selb"), b("ident")], [b("psb")], lambda e: e.transpose(pT1, selb[:], ident[:]))
                        S.op("act", [b("psb")], [b(f"QBb{par}1")],
                             lambda e: e.copy(QB[par][1][64:128], bc(pT1[64:128, :], 1, [64, 2, 128])))
                        yield


                def chain_b():
                    sig_fin(sgT[:], b("sgT"))
                    sig_fin(sg[:], b("sg"))
                    yield
                    sig_fin(qTf[:], b("qTf"))
                    S.op("dve", [b("qTf"), b("ps2")], [b("qTf")],
                         lambda e: e.tensor_tensor(qTf[:], psC[:, 256:384], qTf[:], ALU.mult))
                    sig_fin(gsil[:], b("gsil"))
                    S.op("dve", [b("gsil"), b("ps1")], [b("gsil")],
                         lambda e: e.tensor_tensor(gsil[:], psB_[:, 256:384], gsil[:], ALU.mult))
                    yield
                    S.op("dve", [b("sgT"), b("nomlc"), b("omlc")], [b("kTf")],
                         lambda e: e.tensor_scalar(kTf[:], sgT[:], nomlc[:, 0:1], omlc[:, 0:1], ALU.mult, ALU.add))
                    S.op("dve", [b("sg"), b("omlB")], [b("ff")], lambda e: e.tensor_tensor(ff[:], sg[:], omlB[:], ALU.mult))
                    S.op("dve", [b("ff"), b("lbB")], [b("ff")], lambda e: e.tensor_tensor(ff[:], ff[:], lbB[:], ALU.add))
                    yield
                    S.op("act", [b("ff")], [b("logf")], lambda e: e.activation(logf[:], ff[:], AF.Ln))
                    S.op("dve", [b("ff")], [b("ktm")],
                         lambda e: e.tensor_scalar(ktm[:], ff[:], -1.0, 1.0, ALU.mult, ALU.add))
                    yield
                    pH = ps[0]
                    S.op("pe", [b("logf"), cst], [b("ps0")],
                         lambda e: e.matmul(pH[:, 0:128], lhsT=logf[:], rhs=tri[:], start=True, stop=True))
                    S.op("pe", [b("logf"), cst], [b("ps0")],
                         lambda e: e.matmul(pH[:, 128:256], lhsT=tri2t[:], rhs=logf[:], start=True, stop=True))
                    yield
                    S.op("act", [b("ps0")], [b("ebT")], lambda e: e.activation(ebT[:], pH[:, 0:128], AF.Exp))
                    S.op("act", [b("ps0")], [b("enbT")], lambda e: e.activation(enbT[:], pH[:, 0:128], AF.Exp, scale=-1.0))
                    S.op("act", [b("ps0")], [b("eblmb")], lambda e: e.activation(eblmb[:], pH[:, 128:256], AF.Exp))
                    yield
                    S.op("dve", [b("qTf"), b("ebT")], [b("QpT")], lambda e: e.tensor_tensor(QpT[:], qTf[:], ebT[:], ALU.mult))
                    S.op("dve", [b("kTf"), b("enbT")], [b("KpT")], lambda e: e.tensor_tensor(KpT[:], kTf[:], enbT[:], ALU.mult))
                    for c in range(2):
                        r0 = c * 64
                        S.op("dve", [b("ktm"), b("eblmb")], [b(f"Kppz{c}")],
                             lambda e, c=c, r0=r0: e.tensor_tensor(Kppz[c][r0:r0 + 64, :], ktm[r0:r0 + 64, :],
                                                                   eblmb[r0:r0 + 64, :], ALU.mult))
                        S.op("pool", [b("QpT")], [b(f"QpTz{c}")],
                             lambda e, c=c, r0=r0: e.tensor_copy(QpTz[c][:, r0:r0 + 64], QpT[:, r0:r0 + 64]))
                    yield
                    S.op("pe", [b("KpT"), b("QpT")], [b("ps0")],
                         lambda e: e.matmul(pH[:, 256:384], lhsT=KpT[:], rhs=QpT[:], start=True, stop=True))
                    S.op("dve", [b("ps0"), cst], [b("attnT")],
                         lambda e: e.tensor_tensor(attnT[:], pH[:, 256:384], tri[:], ALU.mult))
                    pHo = pH[:, 384:512]
                    pSu = pH[:, 0:128]
                    S.op("pe", [b("Kppz0"), b("vbf")], [b("ps0")],
                         lambda e: e.matmul(pSu, lhsT=Kppz[0][:], rhs=vbf[:], start=True, stop=True))
                    yield
                    S.op("dve", [b("ps0"), b("ebT"), b("Sst")], [b("Sst")],
                         lambda e: e.scalar_tensor_tensor(Sst[:], Sst[:], ebT[:, 63:64], pSu, ALU.mult, ALU.add))
                    S.op("act", [b("Sst")], [b("Sbf1")], lambda e: e.copy(Sbf[1][:], Sst[:]))
                    yield
                    S.op("pe", [b("attnT"), b("vbf")], [b("ps0")],
                         lambda e: e.matmul(pHo, lhsT=attnT[:], rhs=vbf[:], start=True, stop=False))
                    S.op("pe", [b("QpTz0"), b("Sbf0")], [b("ps0")],
                         lambda e: e.matmul(pHo, lhsT=QpTz[0][:], rhs=Sbf[0][:], start=False, stop=False))
                    S.op("pe", [b("QpTz1"), b("Sbf1")], [b("ps0")],
                         lambda e: e.matmul(pHo, lhsT=QpTz[1][:], rhs=Sbf[1][:], start=False, stop=True))
                    yield
                    S.op("pool", [], [b("sso")], lambda e: e.memset(sso[:], 0.0))
                    S.op("act", [b("ps0"), b("sso")], [b("o2"), b("sso")],
                         lambda e: e.activation(o2[:], pHo, AF.Square, accum_out=sso[:]))
                    S.op("act", [b("sso")], [b("ro")], lambda e: e.activation(ro[:], sso[:], AF.Ln, bias=epsc[:, 2:3]))
                    S.op("act", [b("ro")], [b("ro")], lambda e: e.activation(ro[:], ro[:], AF.Exp, scale=-0.5))
                    S.op("dve", [b("ps0"), b("ro")], [b("o1")],
                         lambda e: e.tensor_scalar(o1[:], pHo, ro[:, 0:1], float(np.sqrt(128.0)), ALU.mult, ALU.mult))
                    yield
                    S.op("pe", [b("Kppz1"), b("vbf")], [b("ps0")],
                         lambda e: e.matmul(pSu, lhsT=Kppz[1][:], rhs=vbf[:], start=True, stop=True))
                    S.op("dve", [b("ps0"), b("ebT"), b("Sst")], [b("Sst")],
                         lambda e: e.scalar_tensor_tensor(Sst[:], Sst[:], ebT[:, 127:128], pSu, ALU.mult, ALU.add))
                    S.op("act", [b("Sst")], [b("Sbf0")], lambda e: e.copy(Sbf[0][:], Sst[:]))
                    yield
                    S.op("dve", [b("o1"), cst], [b("o1")], lambda e: e.tensor_tensor(o1[:], o1[:], gon[:], ALU.mult))
                    S.op("dve", [b("o1"), b("gsil")], [b(f"mixh{par}")],
                         lambda e: e.tensor_tensor(mix[par][:, 128:256], o1[:], gsil[:], ALU.mult))
                    yield


                def chain_b2():
                    yield from chain_b()
                    if hoisted(qb + 1):
                        yield from xstage(qb + 1)
                    elif qb + 1 < N_TILES:
                        yield from xstage_a(qb + 1)

                yield from interleave(chain_a(), chain_b2())

            pbctr = [0]

            def backend(qb, gen):
                par = qb % 2
                t0 = qb * 128
                jobs = [("slc", kt) for kt in range(qb + 1)] + [("win", kt) for kt in range(max(0, qb - 4), qb + 1)]
                pairs = [jobs[i_:i_ + 2] for i_ in range(0, len(jobs), 2)]
                nsteps = -(-FE_STEPS // max(1, int(1.0 * len(pairs))))

                def advance(k):
                    if gen is None:
                        return
                    for _ in range(k):
                        try:
                            next(gen)
                        except StopIteration:
                            return

                def score(pair, slot):
                    for j_, (kind, kt) in enumerate(pair):
                        pout = psSW2[slot][:, j_ * 256:(j_ + 1) * 256]
                        if kind == "slc":
                            hf_ = kt // 32
                            S.op("pe", [b(f"KEk{kt}"), b("KEe"), b(f"QBq{par}{hf_}"), b(f"QBb{par}{hf_}")], [b(swtok[slot])],
                                 lambda e, pout=pout, kt=kt, hf_=hf_: e.matmul(
                                     pout, lhsT=KE[:, kt * 128:(kt + 1) * 128],
                                     rhs=QB[par][hf_][:].rearrange("p h t -> p (h t)"), start=True, stop=True))
                        else:
                            S.op("pe", [b(f"kwT{kt}"), b(f"QBq{par}0")], [b(swtok[slot])],
                                 lambda e, pout=pout, kt=kt: e.matmul(
                                     pout, lhsT=kwT[:, kt * 128:(kt + 1) * 128],
                                     rhs=QB[par][0][0:64].rearrange("p h t -> p (h t)"), start=True, stop=True))

                score(pairs[0], 0)
                first_pv = [True]
                for pi, pair in enumerate(pairs):
                    if pi + 1 < len(pairs):
                        score(pairs[pi + 1], (pi + 1) % 2)
                    pslot = pbctr[0] % 3
                    pbctr[0] += 1
                    slot_ps = pi % 2
                    pbt = b(f"Pb{pslot}")
                    pbuf = Pb[pslot]
                    w_ = 256 * len(pair)
                    S.op("act", [b(swtok[slot_ps])], [pbt],
                         lambda e: e.activation(pbuf[:, 0:w_], psSW2[slot_ps][:, 0:w_], AF.Exp))
                    for j_, (kind, kt) in enumerate(pair):
                        pv = pbuf[:, j_ * 256:(j_ + 1) * 256].rearrange("p (h q) -> p h q", h=2)
                        if kt == qb:
                            S.op("pool", [pbt], [pbt],
                                 lambda e, pv=pv: e.affine_select(out=pv, in_=pv, pattern=[[0, 2], [1, 128]],
                                                                  compare_op=ALU.is_ge, fill=0.0, base=0, channel_multiplier=-1))
                        elif kind == "win" and kt == qb - 4:
                            S.op("pool", [pbt], [pbt],
                                 lambda e, pv=pv: e.affine_select(out=pv, in_=pv, pattern=[[0, 2], [-1, 128]],
                                                                  compare_op=ALU.is_ge, fill=0.0, base=-1, channel_multiplier=1))
                    if PIPELINE:
                        advance(nsteps)
                    for j_, (kind, kt) in enumerate(pair):
                        branch = 0 if kind == "slc" else 1
                        vA, vname = (vsA, "vsA") if kind == "slc" else (vwA, "vwA")
                        for h in range(2):
                            c0 = (branch * 2 + h) * 65
                            st_ = first_pv[0]
                            first_pv[0] = False
                            S.op("pe", [pbt, b(f"{vname}{kt}"), b(vname + "ones")], [b("ps6")],
                                 lambda e, h=h, c0=c0, j_=j_, kt=kt, vA=vA, st_=st_: e.matmul(
                                     psO[:, c0:c0 + 65], lhsT=pbuf[:, j_ * 256 + h * 128:j_ * 256 + (h + 1) * 128],
                                     rhs=vA[:, kt, :], start=st_, stop=(kt == qb), skip_group_check=True))

                gt = gts[par]
                pOv = psO[:, 0:260].rearrange("p (x c) -> p x c", c=65)
                S.op("dve", [b("ps6")], [b("dn6")], lambda e: e.tensor_scalar_max(dn6[:, 0:4], pOv[:, :, 64], 1e-30))
                S.op("dve", [b("dn6")], [b("rd6")], lambda e: e.reciprocal(rd6[:, 0:4], dn6[:, 0:4]))
                S.op("dve", [b("rd6"), b(f"gts{par}")], [b("coef")],
                     lambda e: e.tensor_tensor(coef[:, 0:4].rearrange("p (b h) -> p b h", b=2),
                                               rd6[:, 0:4].rearrange("p (b h) -> p b h", b=2),
                                               gt[:].rearrange("p (h b) -> p b h", h=2)[:, 1:3, :], ALU.mult))
                for h in range(2):
                    S.op("dve", [b("ps6"), b("coef"), b(f"ocmp{par}")], [b("oacc")],
                         lambda e, h=h: e.scalar_tensor_tensor(oacc[:], psO[:, h * 65:h * 65 + 64],
                                                               coef[:, h:h + 1], ocmp[par][:, h, :], ALU.mult, ALU.add))
                    S.op("dve", [b("ps6"), b("coef"), b("oacc")], [b(f"mixn{par}")],
                         lambda e, h=h: e.scalar_tensor_tensor(mix[par][:, h * 64:(h + 1) * 64],
                                                               psO[:, (2 + h) * 65:(2 + h) * 65 + 64],
                                                               coef[:, 2 + h:3 + h], oacc[:], ALU.mult, ALU.add))
                pM = psb[:, 0:256]
                for k in range(2):
                    S.op("pe", [b(f"mixn{par}"), b(f"mixh{par}"), b("ident")], [b("psb")],
                         lambda e, k=k: e.transpose(pM[:, k * 128:(k + 1) * 128], mix[par][:, k * 128:(k + 1) * 128], ident[:]))
                S.op("act", [b("psb")], [b("mixT")],
                     lambda e: e.copy(mixT[:], pM.rearrange("p (k t) -> p k t", k=2)))
                cj, cc0 = qb // 16, (qb % 16) * 128
                S.dma("sp", "mx", mixsrc.ap()[cj * 256:(cj + 1) * 256, cc0:cc0 + 128].rearrange("(k p) t -> p k t", p=128),
                      mixT[:], [b("mixT")], [mixsrc_b])
                advance(10 ** 6)

            g0 = frontend(0)
            nst_ = 0
            for _ in g0:
                nst_ += 1
                if nst_ >= STOP_AT:
                    break
            print("front-end steps", nst_)
            if STOP_AT >= 10 ** 8:
                for qb in range(N_TILES):
                    backend(qb, frontend(qb + 1) if qb + 1 < N_TILES else None)

            if DEBUG:
                S.dma("sp", "dbg", dbg.ap()[:, :], mixsrc.ap()[:, :], [mixsrc_b], [Buf("dbg")])
            S.barrier()

        if RUN_CC:
            for cj in range(4):
                S.collective("cc", [mixsrc_b], [mixall_b],
                             lambda e, cj=cj: e.collective_compute(
                                 "AllGather", ALU.bypass, replica_groups=[[0, 1, 2, 3], [4, 5, 6, 7]],
                                 ins=[mixsrc.ap()[cj * 256:(cj + 1) * 256, :].opt()],
                                 outs=[mixall.ap()[cj * 1024:(cj + 1) * 1024, :].opt()]))

        if not RUN_P2:
            S.dma("sp", "y0", y.ap()[0:128, 0:8], x2.ap()[0:128, 0:8], [], [Buf("yd")])
            S.barrier(["sp"])
            return nc
        with ExitStack() as es:
            def sb(name, shape, dt=F32):
                return es.enter_context(nc.sbuf_tensor(name, list(shape), dt))

            B = {}

            def b(name):
                if name not in B:
                    B[name] = Buf(name)
                return B[name]

            epsc = sb("epsc2", [128, 3])
            for v_, i_ in EPSI.items():
                S.op("pool", [], [b("epsc")], lambda e, v_=v_, i_=i_: e.memset(epsc[:, i_:i_ + 1], v_))
            S.barrier(["act"])
            wo = sb("wo", [128, 8, D], BF16)
            wd = sb("wd", [128, NFF, D], BF16)
            ident2 = sb("ident2", [128, 128], BF16)
            nfcol = sb("nfcol_s", [128, 8])
            WCH = 4
            NSC = -(-NFF // WCH)
            wg = [sb(f"wg{i}", [128, 8, 2, WCH * 128], BF16) for i in range(3)]
            mT = sb("mT", [128, 8, 512], BF16)
            x1 = sb("x1", [128, 4, D])
            hn2 = [sb(f"hn{i}", [128, D], BF16) for i in range(2)]
            hT = sb("hT", [128, 8, 512], BF16)
            actT = sb("actT", [128, NFF, 512], BF16)
            sgl = [sb(f"sgl{i}", [128, 512]) for i in range(2)]
            yo = [sb(f"yo{i}", [128, D]) for i in range(2)]
            sqj2 = sb("sqj2", [128, D])
            ss2 = sb("ss2", [128, 1])
            r2 = sb("r2", [128, 1])

            S.dma("pool", "p2w0", wo[:], w_out_p.ap().rearrange("(k p) n -> p k n", p=128), [], [b("wo")])
            S.dma("pool", "p2w1", ident2[:], c_ident.ap(), [], [b("ident2")])
            S.dma("sp", "p2w2", nfcol[:], nfcol_d.ap(), [], [b("nfcol")])
            for j0 in range(0, NFF, 2):
                S.dma("pool", "p2w3", wd[:, j0:j0 + 2, :],
                      w_dn.ap().rearrange("(j p) n -> p j n", p=128)[:, j0:j0 + 2, :], [], [b("wd")])

            pid = nc.sync.partition_id()
            rank = pid % 4
            wgu_v = w_gu.ap().rearrange("(k p) n -> p k n", p=128)
            gctr = [0]

            def load_wg():
                n_ = gctr[0]
                if n_ >= 4 * NSC:
                    return
                gctr[0] += 1
                i, sc = n_ % 3, n_ % NSC
                j0 = sc * WCH * 128
                w_ = min(WCH * 128, DFF - j0)
                S.dma("pool", f"wg{i}", wg[i][:, :, 0, 0:w_], wgu_v[:, :, j0:j0 + w_], [], [b(f"wg{i}")])
                S.dma("pool", f"wg{i}", wg[i][:, :, 1, 0:w_], wgu_v[:, :, DFF + j0:DFF + j0 + w_], [], [b(f"wg{i}")])

            load_wg()
            load_wg()
            wuse = [0]

            pOP = [ps[0], ps[1]]
            pG = [ps[2], ps[3]]
            pU = [ps[4], ps[5]]
            octr = [0]
            yctr = [0]
            for g in range(4):
                S.dma("sp", "mT", mT[:],
                      mixall.ap().rearrange("(jk p) t -> p jk t", p=128)[:, ds(rank * 8, 8), g * 512:(g + 1) * 512],
                      [mixall_b], [b("mT")])
                S.dma("sp", "x1", x1[:], x2.ap()[g * 512:(g + 1) * 512, :].rearrange("(a p) d -> p a d", p=128),
                      [], [b("x1")])
                def stage_a(a):
                    for hf_ in range(2):
                        po = pOP[octr[0] % 2]
                        pob = b(f"pOP{octr[0] % 2}")
                        octr[0] += 1
                        for k in range(8):
                            S.op("pe", [b("mT"), b("wo")], [pob],
                                 lambda e, k=k, po=po, hf_=hf_: e.matmul(po[:, 0:512], lhsT=mT[:, k, a * 128:(a + 1) * 128],
                                                                         rhs=wo[:, k, hf_ * 512:(hf_ + 1) * 512],
                                                                         start=(k == 0), stop=(k == 7)))
                        S.op("dve", [pob, b("x1")], [b("x1")],
                             lambda e, po=po, hf_=hf_: e.tensor_tensor(x1[:, a, hf_ * 512:(hf_ + 1) * 512], po[:, 0:512],
                                                                       x1[:, a, hf_ * 512:(hf_ + 1) * 512], ALU.add))
                    hn_ = hn2[a % 2]
                    S.op("pool", [], [b("ss2")], lambda e: e.memset(ss2[:], 0.0))
                    S.op("act", [b("x1"), b("ss2")], [b("sqj2"), b("ss2")],
                         lambda e: e.activation(sqj2[:], x1[:, a, :], AF.Square, accum_out=ss2[:]))
                    S.op("act", [b("ss2")], [b("r2")],
                         lambda e: e.activation(r2[:], ss2[:], AF.Ln, bias=epsc[:, 0:1]))
                    S.op("act", [b("r2")], [b("r2")],
                         lambda e: e.activation(r2[:], r2[:], AF.Exp, scale=-0.5))
                    S.op("dve", [b("x1"), b("r2")], [b(f"hn{a % 2}")],
                         lambda e: e.tensor_scalar(hn_[:], x1[:, a, :], r2[:, 0:1], 32.0, ALU.mult, ALU.mult))

                def stage_b(a):
                    hn_ = hn2[a % 2]
                    for k in range(8):
                        S.op("pe", [b(f"hn{a % 2}"), b("ident2")], [b("psb")],
                             lambda e, k=k: e.transpose(psb[:, k * 128:(k + 1) * 128], hn_[:, k * 128:(k + 1) * 128], ident2[:]))
                    S.op("dve", [b("psb"), b("nfcol")], [b("hT")],
                         lambda e: e.tensor_tensor(hT[:, :, a * 128:(a + 1) * 128],
                                                   psb[:, 0:1024].rearrange("p (k t) -> p k t", k=8),
                                                   bc(nfcol[:], 2, [128, 8, 128]), ALU.mult))

                stage_a(0)
                for a in range(1, 4):
                    stage_a(a)
                    stage_b(a - 1)
                stage_b(3)
                for j in range(NFF):
                    if j % WCH == 0:
                        i = wuse[0] % 3
                        wuse[0] += 1
                        load_wg()
                    jj = j % WCH
                    pg, pu = pG[j % 2], pU[j % 2]
                    pgb, pub = b(f"pG{j % 2}"), b(f"pU{j % 2}")
                    for k in range(8):
                        S.op("pe", [b("hT"), b(f"wg{i}")], [pgb],
                             lambda e, k=k: e.matmul(pg[:, 0:512], lhsT=wg[i][:, k, 0, jj * 128:(jj + 1) * 128], rhs=hT[:, k, :],
                                                     start=(k == 0), stop=(k == 7)))
                    for k in range(8):
                        S.op("pe", [b("hT"), b(f"wg{i}")], [pub],
                             lambda e, k=k: e.matmul(pu[:, 0:512], lhsT=wg[i][:, k, 1, jj * 128:(jj + 1) * 128], rhs=hT[:, k, :],
                                                     start=(k == 0), stop=(k == 7)))
                    sgt = sgl[j % 2]
                    S.op("act", [pgb], [b(f"sgl{j % 2}")], lambda e: e.activation(sgt[:], pg[:, 0:512], AF.Silu))
                    S.op("dve", [b(f"sgl{j % 2}"), pub], [b("actT")],
                         lambda e: e.tensor_tensor(actT[:, j, :], sgt[:], pu[:, 0:512], ALU.mult))
                for a in range(4):
                    yt = yo[yctr[0] % 2]
                    ytb = b(f"yo{yctr[0] % 2}")
                    ych = f"y{yctr[0] % 2}"
                    yctr[0] += 1
                    for hf_ in range(2):
                        po = pOP[octr[0] % 2]
                        pob = b(f"pOP{octr[0] % 2}")
                        octr[0] += 1
                        for j in range(NFF):
                            S.op("pe", [b("actT"), b("wd")], [pob],
                                 lambda e, j=j, po=po: e.matmul(po[:, 0:512], lhsT=actT[:, j, a * 128:(a + 1) * 128],
                                                                rhs=wd[:, j, hf_ * 512:(hf_ + 1) * 512],
                                                                start=(j == 0), stop=(j == NFF - 1)))
                        S.op("dve", [pob, b("x1")], [ytb],
                             lambda e, po=po: e.tensor_tensor(yt[:, hf_ * 512:(hf_ + 1) * 512], po[:, 0:512],
                                                              x1[:, a, hf_ * 512:(hf_ + 1) * 512], ALU.add))
                    r0 = g * 512 + a * 128
                    S.dma("sp", ych, y.ap()[r0:r0 + 128, :], yt[:], [ytb], [b("ydram")])
            S.barrier(["sp"])
        print("instr counts", S.cnt, "waits", S.nwait, "sems", 5 + len(S.chsem))
    return nc


def _consts():
    ident = np.eye(128, dtype=np.float32)
    s = np.arange(128)
    same = (s[:, None] // 64) == (s[None, :] // 64)
    tri = (same & (s[:, None] <= s[None, :])).astype(np.float32)
    tri2t = (same & (s[:, None] > s[None, :])).astype(np.float32)
    i = np.arange(512)[:, None]
    j = np.arange(128)[None, :]
    ov = ((i * 16 < j * 64 + 64) & (i * 16 + 32 > j * 64)).astype(np.float32)
    ov[:, 0] = 1.0
    q = np.arange(128)[:, None]
    xcol = np.arange(256)[None, :]
    jrel = xcol - 128 - (q >= 64)
    wa = (jrel < -1).astype(np.float32)
    wb = np.where(jrel > 0, -1.0, np.where(jrel >= -1, 1.0e4, 0.0)).astype(np.float32)
    key = np.arange(T)[None, :]
    e = (((key // 64) % 64) == np.arange(64)[:, None]).astype(np.float32) * 30000.0
    return dict(c_ident=ident, c_tri=tri, c_tri2t=tri2t, c_ov=ov, c_wa=wa, c_wb=wb, c_e=e)


def make_in_maps(x, norm_mix, w_in, q_norm, k_norm, cmp_pos_k, cmp_pos_v, cmp_k_w1, cmp_k_w2, cmp_v_w1, cmp_v_w2,
                 hgrn_lb_logits, hgrn_o_norm, w_out, norm_ffn, w_gate_up, w_down):
    f = np.float32
    x = np.asarray(x, f)
    w_in0 = np.asarray(w_in, f)[0]
    consts = _consts()
    offs = np.cumsum([0, 512, 128, 128, 128, 128, 128, 128, 24, 512, 512, 512, 512])
    o_q, o_kc, o_vc, o_ks, o_vs, o_kw, o_vw, o_g, o_hq, o_hf, o_hi, o_hg = [int(v) for v in offs[:12]]
    rep = lambda v, n=128: np.ascontiguousarray(np.broadcast_to(np.asarray(v, f)[None, :], (n, len(v))))
    qn = np.asarray(q_norm, f)[0]
    kn = np.asarray(k_norm, f)[0]
    gq6 = np.concatenate([rep(qn)] * 4 + [rep(kn[1]), rep(kn[2])], axis=1)
    col8 = lambda v: np.ascontiguousarray(np.asarray(v, f).reshape(8, 128).T)
    lbl = np.asarray(hgrn_lb_logits, f)
    w_out0 = np.asarray(w_out, f)[0]
    in_maps = []
    for c in range(8):
        bi, s = c // 4, c % 4
        g = s // 2
        own = [2 * s, 2 * s + 1]
        oth = [h for h in range(4 * g, 4 * g + 4) if h not in own]
        hc = lambda o, h, w: list(range(o + h * w, o + (h + 1) * w))
        cols_tm = []
        for h in own + oth:
            cols_tm += hc(o_q, h, 64)
        cols_tm += hc(o_ks, g, 64) + hc(o_kw, g, 64) + hc(o_vs, g, 64) + hc(o_vw, g, 64)
        cols_tm += hc(o_hf, s, 128) + hc(o_hi, s, 128) + hc(o_hg, s, 128)
        for h in own:
            cols_tm += hc(o_g, h, 3)
        cols_fm = hc(o_kc, g, 64) + hc(o_vc, g, 64) + hc(o_hq, s, 128) + hc(o_hf, s, 128)
        hs = slice(s * 128, (s + 1) * 128)
        rows_out = []
        for s2 in range(4):
            rows_out += list(range(2 * s2 * 64, (2 * s2 + 2) * 64)) + list(range(512 + s2 * 128, 512 + (s2 + 1) * 128))
        m = dict(consts)
        m.update(
            x_b=np.ascontiguousarray(x[bi]),
            x2=np.ascontiguousarray(x[bi, s * 2048:(s + 1) * 2048]),
            w_tm=np.ascontiguousarray(w_in0[:, cols_tm]),
            w_fm=np.ascontiguousarray(w_in0[:, cols_fm]),
            w1k=np.asarray(cmp_k_w1, f)[0], w1v=np.asarray(cmp_v_w1, f)[0],
            w2kv=np.ascontiguousarray(np.concatenate([np.asarray(cmp_k_w2, f)[0], np.asarray(cmp_v_w2, f)[0]], axis=1)),
            posT=np.ascontiguousarray(np.concatenate([np.asarray(cmp_pos_k, f)[0].T, np.asarray(cmp_pos_v, f)[0].T], axis=0)),
            gq6=gq6, gkc=rep(kn[0]), gon=rep(np.asarray(hgrn_o_norm, f)[0]),
            nmcol=col8(np.asarray(norm_mix, f)[0]),
            lbrow=np.concatenate([rep(lbl[0, hs]), rep(lbl[1, hs])], axis=1),
            lbc=np.ascontiguousarray(np.stack([lbl[0, hs], lbl[1, hs]], axis=1)),
            w_out_p=np.ascontiguousarray(w_out0[rows_out]),
            w_gu=np.asarray(w_gate_up, f)[0], w_dn=np.asarray(w_down, f)[0],
            nfcol=col8(np.asarray(norm_ffn, f)[0]),
        )
        in_maps.append(m)
    return in_maps


def kernel(**inputs):
    in_maps = make_in_maps(**inputs)
    nc = build_nc()
    res = run_bass_kernel_spmd(nc, in_maps, core_ids=list(range(8)))
    out = np.empty((2, T, D), np.float32)
    for c in range(8):
        bi, s = c // 4, c % 4
        out[bi, s * 2048:(s + 1) * 2048] = res.results[c]["y"]
    return out
```

```python
import numpy as np
from contextlib import ExitStack

import concourse.bass as bass
import concourse.mybir as mybir
from concourse.bass import ds
from concourse.bass_utils import run_bass_kernel_spmd

F32 = mybir.dt.float32
BF16 = mybir.dt.bfloat16
AF = mybir.ActivationFunctionType
ALU = mybir.AluOpType
AX = mybir.AxisListType

T = 8192
D = 1024
NT = T // 128
DFF = 2816
NFF = DFF // 128
EPS = 1e-6
NEG = -1.0e30
SAME_ENGINE_SYNC = True
EPSI = {float(D * EPS): 0, float(64 * EPS): 1, float(128 * EPS): 2}
N_TILES = NT
FE_STEPS = 56
STOP_AT = 10 ** 9
HOIST_MAX = 63
PIPELINE = True
DEBUG = False
RUN_CC = True
RUN_P2 = True


class Buf:
    __slots__ = ("name", "w", "r", "excl")

    def __init__(self, name):
        self.name = name
        self.w = []
        self.r = {}
        self.excl = name.startswith("ps")


class Sched:
    def __init__(self, nc, es):
        self.nc = nc
        self.es = es
        self.eng = {"pe": nc.tensor, "act": nc.scalar, "dve": nc.vector, "pool": nc.gpsimd, "sp": nc.sync}
        self.sem = {k: es.enter_context(nc.semaphore("sem_" + k)) for k in self.eng}
        self.cnt = {k: 0 for k in self.eng}
        self.known = {k: {} for k in self.eng}
        self.chsem = {}
        self.chcnt = {}
        self.nwait = 0

    def _sem_of(self, ev):
        return self.sem[ev[1]] if ev[0] == "e" else self.chsem[ev[1]]

    def _waits(self, e, reads, writes):
        need = {}
        for b in reads:
            for ev in b.w:
                k = (ev[0], ev[1])
                need[k] = max(need.get(k, 0), ev[2])
            if b.excl:
                for k, v in b.r.items():
                    if not (k[0] == "e" and k[1] == e):
                        need[k] = max(need.get(k, 0), v)
        for b in writes:
            for ev in b.w:
                k = (ev[0], ev[1])
                need[k] = max(need.get(k, 0), ev[2])
            for k, v in b.r.items():
                need[k] = max(need.get(k, 0), v)
        for k, v in need.items():
            if k[0] == "e" and k[1] == e and (e == "pe" or not SAME_ENGINE_SYNC):
                continue
            if self.known[e].get(k, 0) >= v:
                continue
            sem = self.sem[k[1]] if k[0] == "e" else self.chsem[k[1]]
            self.eng[e].wait_ge(sem, v)
            self.known[e][k] = v
            self.nwait += 1

    def _record(self, ev, reads, writes):
        k = (ev[0], ev[1])
        for b in reads:
            b.r[k] = max(b.r.get(k, 0), ev[2])
        for b in writes:
            b.w = [ev]
            b.r = {}

    def op(self, e, reads, writes, fn):
        self._waits(e, reads, writes)
        inst = fn(self.eng[e])
        self.cnt[e] += 1
        inst.then_inc(self.sem[e], 1)
        self._record(("e", e, self.cnt[e]), reads, writes)

    def dma(self, q, ch, out, in_, reads, writes, **kw):
        if ch not in self.chsem:
            self.chsem[ch] = self.es.enter_context(self.nc.semaphore("ch_" + ch))
            self.chcnt[ch] = 0
        self._waits(q, reads, writes)
        inst = self.eng[q].dma_start(out=out, in_=in_, **kw)
        self.chcnt[ch] += 16
        inst.then_inc(self.chsem[ch], 16)
        self._record(("d", ch, self.chcnt[ch]), reads, writes)

    def collective(self, ch, reads, writes, fn):
        if ch not in self.chsem:
            self.chsem[ch] = self.es.enter_context(self.nc.semaphore("ch_" + ch))
            self.chcnt[ch] = 0
        self._waits("pool", reads, writes)
        inst = fn(self.eng["pool"])
        self.chcnt[ch] += 1
        inst.then_inc(self.chsem[ch], 1)
        self._record(("d", ch, self.chcnt[ch]), reads, writes)

    def barrier(self, engines=None):
        engines = engines or list(self.eng)
        for e in engines:
            for e2 in self.eng:
                if e2 == e or self.cnt[e2] == 0:
                    continue
                k = ("e", e2)
                if self.known[e].get(k, 0) < self.cnt[e2]:
                    self.eng[e].wait_ge(self.sem[e2], self.cnt[e2])
                    self.known[e][k] = self.cnt[e2]
            for ch, v in self.chcnt.items():
                k = ("d", ch)
                if v and self.known[e].get(k, 0) < v:
                    self.eng[e].wait_ge(self.chsem[ch], v)
                    self.known[e][k] = v


def interleave(a, b_):
    da = db = False
    while not (da and db):
        if not da:
            try:
                next(a)
                yield
            except StopIteration:
                da = True
        if not db:
            try:
                next(b_)
                yield
            except StopIteration:
                db = True


def bc(ap, axis, shape):
    return ap.unsqueeze(axis).to_broadcast(list(shape))


def build_nc():
    nc = bass.Bass("TRN2", target_bir_lowering=False)

    def din(name, shape, dt=F32):
        return nc.dram_tensor(name, list(shape), dt, kind="ExternalInput")

    x_b = din("x_b", [T, D])
    x2 = din("x2", [2048, D])
    w_tm = din("w_tm", [D, 902])
    w_fm = din("w_fm", [D, 384])
    w1k = din("w1k", [2048, 128])
    w1v = din("w1v", [2048, 128])
    w2kv = din("w2kv", [128, 128])
    posT = din("posT", [128, 32])
    c_ident = din("c_ident", [128, 128])
    c_tri = din("c_tri", [128, 128])
    c_tri2t = din("c_tri2t", [128, 128])
    c_ov = din("c_ov", [512, 128])
    c_wa = din("c_wa", [128, 256])
    c_wb = din("c_wb", [128, 256])
    c_e = din("c_e", [64, T])
    gq6_d = din("gq6", [128, 384])
    gkc_d = din("gkc", [128, 64])
    gon_d = din("gon", [128, 128])
    nmcol_d = din("nmcol", [128, 8])
    lbrow_d = din("lbrow", [128, 256])
    lbc_d = din("lbc", [128, 2])
    w_out_p = din("w_out_p", [D, D])
    w_gu = din("w_gu", [D, 2 * DFF])
    w_dn = din("w_dn", [DFF, D])
    nfcol_d = din("nfcol", [128, 8])
    y = nc.dram_tensor("y", [2048, D], F32, kind="ExternalOutput")

    dbg = nc.dram_tensor("dbg", [1024, 2048], BF16, kind="ExternalOutput") if DEBUG else None
    mixsrc = nc.dram_tensor("mixsrc", [1024, 2048], BF16)
    mixall = nc.dram_tensor("mixall", [4096, 2048], BF16)
    mixsrc_b = Buf("mixsrc")
    mixall_b = Buf("mixall")

    with ExitStack() as es_all:
        S = Sched(nc, es_all)
        ps = [es_all.enter_context(nc.psum_tensor(f"ps{i}", [128, 512], F32)) for i in range(7)]
        psb = es_all.enter_context(nc.psum_tensor("psb", [128, 1024], BF16))

        with ExitStack() as es:
            def sb(name, shape, dt=F32):
                return es.enter_context(nc.sbuf_tensor(name, list(shape), dt))

            wtm = sb("wtm", [128, 8, 902], BF16)
            wfm = sb("wfm", [128, 8, 384], BF16)
            W1k_ = sb("W1k_", [64, 32, 128], BF16)
            W1v_ = sb("W1v_", [64, 32, 128], BF16)
            W1s = [W1k_, W1v_]
            w2 = sb("w2", [128, 128], BF16)
            posTk = sb("posTk", [64, 32], BF16)
            posTv = sb("posTv", [64, 32], BF16)
            posTs = [posTk, posTv]
            ident = sb("ident", [128, 128], BF16)
            tri = sb("tri", [128, 128], F32)
            tri2t = sb("tri2t", [128, 128], F32)
            ovsb = sb("ovsb", [128, 4, 128], BF16)
            ones_c = sb("ones_c", [128, 1], BF16)
            gq6 = sb("gq6s", [128, 384])
            gkc = sb("gkcs", [128, 64])
            gon = sb("gons", [128, 128])
            wa = sb("was", [128, 256])
            wb = sb("wbs", [128, 256])
            nmcol = sb("nmcols", [128, 8])
            lbrow = sb("lbrows", [128, 256])
            lbB = sb("lbB", [128, 128])
            omlB = sb("omlB", [128, 128])
            lbc = sb("lbcs", [128, 2])
            lbcol = sb("lbcol", [128, 1])
            omlc = sb("omlc", [128, 1])
            nomlc = sb("nomlc", [128, 1])
            cbias = sb("cbias", [128, 2])

            KE = sb("KE", [128, T], BF16)
            kwT = sb("kwT", [64, T], BF16)
            kcrT = sb("kcrT", [64, T], BF16)
            vcrT = sb("vcrT", [64, T], BF16)
            kvr = [kcrT, vcrT]
            vsA = sb("vsA", [128, NT, 65], BF16)
            vwA = sb("vwA", [128, NT, 65], BF16)
            kcT = sb("kcT", [64, 512], BF16)
            vcT = sb("vcT", [64, 512], BF16)
            vcA = sb("vcA", [128, 4, 65], BF16)
            Sst = sb("Sst", [128, 128])
            Sbf = [sb(f"Sbf{i}", [128, 128], BF16) for i in range(2)]
            QpTz = [sb(f"QpTz{i}", [128, 128], BF16) for i in range(2)]
            Kppz = [sb(f"Kppz{i}", [128, 128], BF16) for i in range(2)]

            NXS = 3
            xt = [sb(f"xt{i}", [128, D]) for i in range(NXS)]
            sqj = sb("sqj", [128, D])
            ssx = sb("ssx", [128, 1])
            rx = sb("rx", [128, 1])
            xn = sb("xn", [128, D], BF16)
            xnT = [sb(f"xnT{i}", [128, 8, 128], BF16) for i in range(2)]
            sq6 = sb("sq6", [128, 384])
            ss6 = sb("ss6", [128, 6])
            r6 = sb("r6", [128, 6])
            t6 = sb("t6", [128, 384])
            qk6 = sb("qk6", [128, 384], BF16)
            gts = [sb(f"gts{i}", [128, 6]) for i in range(2)]
            qT4 = sb("qT4", [64, 4, 128], BF16)
            QB = [[sb(f"QB{p_}{i}", [128, 2, 128], BF16) for i in range(2)] for p_ in range(2)]
            ocmp = [sb(f"ocmp{i}", [128, 2, 64]) for i in range(2)]
            cfc = sb("cfc", [128, 2])
            hy = sb("hy", [128, 16])
            hy2 = sb("hy2", [128, 16])
            hsg = sb("hsg", [128, 16])
            hid = sb("hid", [128, 16], BF16)
            ksq = sb("ksq", [8, 64])
            kss = sb("kss", [8, 1])
            kr = sb("kr", [8, 1])
            kt1 = sb("kt1", [8, 64])
            kcn = sb("kcn", [8, 64], BF16)
            Pc = [sb(f"Pc{i}", [128, 512], BF16) for i in range(2)]
            Pb = [sb(f"Pb{i}", [128, 512], BF16) for i in range(3)]
            dn4 = sb("dn4", [128, 4])
            rd4 = sb("rd4", [128, 4])
            imp = sb("imp", [128, 128])
            imp2 = sb("imp2", [128, 128])
            m8a = sb("m8a", [128, 8])
            m8b = sb("m8b", [128, 8])
            thr = sb("thr", [128, 1])
            selb = sb("selb", [128, 128], BF16)
            selsw = sb("selsw", [128, 128], BF16)
            dn6 = sb("dn6", [128, 6])
            rd6 = sb("rd6", [128, 6])
            coef = sb("coef", [128, 6])
            oacc = sb("oacc", [128, 64])
            mix = [sb(f"mix{i}", [128, 256], BF16) for i in range(2)]
            mixT = sb("mixT", [128, 2, 128], BF16)
            sgT = sb("sgT", [128, 128])
            qTf = sb("qTf", [128, 128])
            kTf = sb("kTf", [128, 128])
            sg = sb("sg", [128, 128])
            ff = sb("ff", [128, 128])
            logf = sb("logf", [128, 128])
            ktm = sb("ktm", [128, 128])
            gsil = sb("gsil", [128, 128])
            vbf = sb("vbf", [128, 128], BF16)
            ebT = sb("ebT", [128, 128])
            enbT = sb("enbT", [128, 128])
            eblmb = sb("eblmb", [128, 128])
            QpT = sb("QpT", [128, 128], BF16)
            KpT = sb("KpT", [128, 128], BF16)
            Kpp = sb("Kpp", [128, 128], BF16)
            attnT = sb("attnT", [128, 128], BF16)
            sso = sb("sso", [128, 1])
            ro = sb("ro", [128, 1])
            o1 = sb("o1", [128, 128])
            o2 = sb("o2", [128, 128])

            B = {}

            def b(name):
                if name not in B:
                    B[name] = Buf(name)
                return B[name]

            epsc = sb("epsc", [128, 3])
            for v_, i_ in EPSI.items():
                S.op("pool", [], [b("epsc")], lambda e, v_=v_, i_=i_: e.memset(epsc[:, i_:i_ + 1], v_))
            S.barrier(["act"])

            cst = b("const")
            S.dma("pool", "cw5", ident[:], c_ident.ap(), [], [b("ident")])
            S.dma("pool", "cw0", wtm[:], w_tm.ap().rearrange("(k p) n -> p k n", p=128), [], [b("wtm")])
            S.dma("pool", "cw1", wfm[:], w_fm.ap().rearrange("(k p) n -> p k n", p=128), [], [b("wfm")])
            S.dma("pool", "cw2", W1k_[:], w1k.ap().rearrange("(p d) h -> d p h", d=64), [], [b("W1a")])
            S.dma("pool", "cw2", W1v_[:], w1v.ap().rearrange("(p d) h -> d p h", d=64), [], [b("W1b")])
            S.dma("pool", "cw3", w2[:], w2kv.ap(), [], [b("w2")])
            S.dma("pool", "cw3", posTk[:], posT.ap()[0:64, :], [], [b("posT")])
            S.dma("pool", "cw3", posTv[:], posT.ap()[64:128, :], [], [b("posT2")])
            S.dma("pool", "cw3", ovsb[:], c_ov.ap().rearrange("(c p) j -> p c j", p=128), [], [b("ov")])
            S.dma("pool", "cw4", KE[64:128, :], c_e.ap(), [], [b("KEe")])
            for bb_ in ("w2", "posT", "posT2", "ov"):
                b(bb_).w = [("d", "cw3", S.chcnt["cw3"])]
            b("W1a").w = [("d", "cw2", S.chcnt["cw2"])]
            for i, (dst, src) in enumerate([(tri, c_tri), (tri2t, c_tri2t), (gq6, gq6_d), (gkc, gkc_d), (gon, gon_d),
                                            (wa, c_wa), (wb, c_wb), (nmcol, nmcol_d), (lbrow, lbrow_d), (lbc, lbc_d)]):
                S.dma("sp", "cs0", dst[:], src.ap(), [], [cst])
            cst.w = [("d", "cs0", S.chcnt["cs0"])]

            S.op("pool", [], [b("ones")], lambda e: e.memset(ones_c[:], 1.0))
            S.op("pool", [], [b("vsAones")], lambda e: e.memset(vsA[:, :, 64:65], 1.0))
            S.op("pool", [], [b("vwAones")], lambda e: e.memset(vwA[:, :, 64:65], 1.0))
            S.op("pool", [], [b("vcA")], lambda e: e.memset(vcA[:, :, 0:64], 0.0))
            S.op("pool", [], [b("vcAones")], lambda e: e.memset(vcA[:, :, 64:65], 1.0))
            S.op("pool", [], [b("kcT")], lambda e: e.memset(kcT[:], 0.0))
            S.op("pool", [], [b("vcT")], lambda e: e.memset(vcT[:], 0.0))
            S.op("pool", [], [b("Sst")], lambda e: e.memset(Sst[:], 0.0))
            for i in range(2):
                S.op("pool", [], [b(f"Sbf{i}")], lambda e, i=i: e.memset(Sbf[i][:], 0.0))
                S.op("pool", [], [b(f"QpTz{i}")], lambda e, i=i: e.memset(QpTz[i][:], 0.0))
                S.op("pool", [], [b(f"Kppz{i}")], lambda e, i=i: e.memset(Kppz[i][:], 0.0))
            S.op("dve", [cst], [b("lbB")], lambda e: e.tensor_sub(lbB[:], lbrow[:, 0:128], lbrow[:, 128:256]))
            S.op("act", [b("lbB")], [b("lbB")], lambda e: e.activation(lbB[:], lbB[:], AF.Sigmoid))
            S.op("dve", [b("lbB")], [b("omlB")],
                 lambda e: e.tensor_scalar(omlB[:], lbB[:], -1.0, 1.0, ALU.mult, ALU.add))
            S.op("dve", [cst], [b("lbcol")], lambda e: e.tensor_sub(lbcol[:], lbc[:, 0:1], lbc[:, 1:2]))
            S.op("act", [b("lbcol")], [b("lbcol")], lambda e: e.activation(lbcol[:], lbcol[:], AF.Sigmoid))
            S.op("dve", [b("lbcol")], [b("omlc")],
                 lambda e: e.tensor_scalar(omlc[:], lbcol[:], -1.0, 1.0, ALU.mult, ALU.add))
            S.op("dve", [b("lbcol")], [b("nomlc")],
                 lambda e: e.tensor_scalar(nomlc[:], lbcol[:], 1.0, -1.0, ALU.mult, ALU.add))
            pX0 = ps[3][:, 0:16]
            for kv in range(2):
                for p in range(32):
                    S.op("pe", [b("W1a"), b("W1b"), b("posT"), b("posT2")], [b("ps3")],
                         lambda e, p=p, kv=kv: e.matmul(pX0[:, kv:kv + 1], lhsT=W1s[kv][:, p, :],
                                                        rhs=posTs[kv][:, p:p + 1],
                                                        start=(p == 0), stop=(p == 31)))
            S.op("dve", [b("ps3")], [b("cbias")], lambda e: e.tensor_copy(cbias[:], pX0[:, 0:2]))

            def load_x(qb):
                sl = qb % NXS
                S.dma("sp", f"x{sl}", xt[sl][:], x_b.ap()[qb * 128:(qb + 1) * 128, :], [], [b(f"xt{sl}")])

            load_x(0)
            if N_TILES > 1:
                load_x(1)

            psA, psB_, psC = ps[0], ps[1], ps[2]
            F3 = ps[3]
            F3b = psb
            psO = ps[6]
            psSWs = [ps[4][:, 0:256], ps[5][:, 0:256]]
            psSW2 = [ps[4], ps[5]]
            swtok = ["ps4", "ps5"]
            pX = F3[:, 0:16]
            pY = F3[0:8, 16:80]
            pZ = F3[0:64, 80:88]
            pGt = ps[1][:, 384:390]
            pKT = psb[0:64, 768:776]
            pVT = psb[:, 776:840]
            pT1 = psb[:, 840:968]
            pSC = ps[2]

            def impap(r):
                return ps[1][:, r * 128:(r + 1) * 128]

            def imptok(r):
                return b("ps1")

            def ocap(h):
                return F3[:, 128 + h * 64:192 + h * 64]

            def xstage(q):
                yield from xstage_a(q)
                yield from xstage_b(q)

            def xstage_a(q):
                sl = q % NXS
                xs = xt[sl]
                xb = b(f"xt{sl}")
                S.op("pool", [], [b("ssx")], lambda e: e.memset(ssx[:], 0.0))
                S.op("act", [xb, b("ssx")], [b("sqj"), b("ssx")],
                     lambda e: e.activation(sqj[:], xs[:], AF.Square, accum_out=ssx[:]))
                S.op("act", [b("ssx")], [b("rx")],
                     lambda e: e.activation(rx[:], ssx[:], AF.Ln, bias=epsc[:, 0:1]))
                S.op("act", [b("rx")], [b("rx")], lambda e: e.activation(rx[:], rx[:], AF.Exp, scale=-0.5))
                yield
                S.op("dve", [xb, b("rx")], [b("xn")],
                     lambda e: e.tensor_scalar(xn[:], xs[:], rx[:, 0:1], 32.0, ALU.mult, ALU.mult))
                yield

            def xstage_b(q):
                xT = xnT[q % 2]
                xTb = b(f"xnT{q % 2}")
                for k in range(8):
                    S.op("pe", [b("xn"), b("ident")], [b("psb")],
                         lambda e, k=k: e.transpose(F3b[:, k * 128:(k + 1) * 128], xn[:, k * 128:(k + 1) * 128], ident[:]))
                S.op("dve", [b("psb"), cst], [xTb],
                     lambda e: e.tensor_tensor(xT[:], F3b[:, 0:1024].rearrange("p (k t) -> p k t", k=8),
                                               bc(nmcol[:], 2, [128, 8, 128]), ALU.mult))
                yield

            def hoisted(q):
                return 1 <= q <= HOIST_MAX and q < N_TILES

            def frontend(qb):
                par = qb % 2
                t0 = qb * 128
                xT = xnT[qb % 2]
                xTb = b(f"xnT{qb % 2}")
                if qb + 2 < N_TILES:
                    load_x(qb + 2)
                if not hoisted(qb):
                    if qb == 0:
                        yield from xstage_a(qb)
                    yield from xstage_b(qb)
                for k in range(8):
                    S.op("pe", [xTb, b("wtm")], [b("ps0")],
                         lambda e, k=k: e.matmul(psA[:, 0:512], lhsT=xT[:, k, :], rhs=wtm[:, k, 0:512],
                                                 start=(k == 0), stop=(k == 7)))
                    if k % 2 == 1:
                        yield
                for k in range(8):
                    S.op("pe", [xTb, b("wtm")], [b("ps1")],
                         lambda e, k=k: e.matmul(psB_[:, 0:390], lhsT=xT[:, k, :], rhs=wtm[:, k, 512:902],
                                                 start=(k == 0), stop=(k == 7)))
                    if k % 2 == 1:
                        yield
                for c, (m0, m1) in enumerate([(0, 64), (64, 128), (128, 256), (256, 384)]):
                    for k in range(8):
                        S.op("pe", [xTb, b("wfm")], [b("ps2")],
                             lambda e, k=k, c=c, m0=m0, m1=m1: e.matmul(psC[0:m1 - m0, c * 128:(c + 1) * 128],
                                                                        lhsT=wfm[:, k, m0:m1], rhs=xT[:, k, :],
                                                                        start=(k == 0), stop=(k == 7)))
                    yield
                S.op("act", [b("ps0")], [b("sq6")], lambda e: e.activation(sq6[:], psA[:, 0:384], AF.Square))
                S.op("dve", [b("sq6")], [b("ss6")],
                     lambda e: e.tensor_reduce(ss6[:], sq6[:].rearrange("p (h d) -> p h d", h=6), AX.X, ALU.add))
                S.op("act", [b("ss6")], [b("r6")], lambda e: e.activation(r6[:], ss6[:], AF.Ln, bias=epsc[:, 1:2]))
                S.op("act", [b("r6")], [b("r6")], lambda e: e.activation(r6[:], r6[:], AF.Exp, scale=-0.5))
                yield
                S.op("dve", [b("r6")], [b("r6")], lambda e: e.tensor_scalar_mul(r6[:, 4:6], r6[:, 4:6], 8.0))
                S.op("dve", [b("ps0"), b("r6")], [b("t6")],
                     lambda e: e.tensor_tensor(t6[:].rearrange("p (h d) -> p h d", h=6),
                                               psA[:, 0:384].rearrange("p (h d) -> p h d", h=6),
                                               bc(r6[:], 2, [128, 6, 64]), ALU.mult))
                S.op("dve", [b("t6"), cst], [b("qk6")], lambda e: e.tensor_tensor(qk6[:], t6[:], gq6[:], ALU.mult))
                yield
                S.op("act", [b("ps0")], [b(f"vsA{qb}")], lambda e: e.copy(vsA[:, qb, 0:64], psA[:, 384:448]))
                S.op("act", [b("ps0")], [b(f"vwA{qb}")], lambda e: e.copy(vwA[:, qb, 0:64], psA[:, 448:512]))
                S.op("act", [b("ps2")], [b("kvcT")], lambda e: e.copy(kcrT[:, t0:t0 + 128], psC[0:64, 0:128]))
                S.op("act", [b("ps2")], [b("kvcT")], lambda e: e.copy(vcrT[:, t0:t0 + 128], psC[0:64, 128:256]))
                yield
                for h in range(6):
                    S.op("pe", [b("qk6"), b("ident")], [b("psb")],
                         lambda e, h=h: e.transpose(F3b[0:64, h * 128:(h + 1) * 128], qk6[:, h * 64:(h + 1) * 64], ident[:]))
                S.op("act", [b("psb")], [b("qT4")],
                     lambda e: e.copy(qT4[:], F3b[0:64, 0:512].rearrange("p (h t) -> p h t", h=4)))
                for i in range(2):
                    if i == 1 and qb < 32:
                        continue
                    S.op("dve", [b("psb")], [b(f"QBq{par}{i}")],
                         lambda e, i=i: e.tensor_copy(QB[par][i][0:64], F3b[0:64, 0:256].rearrange("p (h t) -> p h t", h=2)))
                S.op("act", [b("psb")], [b(f"KEk{qb}")], lambda e: e.copy(KE[0:64, t0:t0 + 128], F3b[0:64, 512:640]))
                S.op("act", [b("psb")], [b(f"kwT{qb}")], lambda e: e.copy(kwT[:, t0:t0 + 128], F3b[0:64, 640:768]))
                yield
                i0 = max(0, 8 * qb - 1)
                n = 8 * qb + 7 - i0
                for kv in range(2):
                    for p in range(32):
                        st = 16 * i0 + p
                        S.op("pe", [b("W1a"), b("W1b"), b("kvcT")], [b("ps3")],
                             lambda e, p=p, kv=kv, st=st: e.matmul(
                                 pX[:, kv * 8:kv * 8 + n], lhsT=W1s[kv][:, p, :],
                                 rhs=kvr[kv][:, st:st + 16 * (n - 1) + 1:16], start=(p == 0), stop=(p == 31)))
                        if p % 8 == 7:
                            yield
                v3 = lambda t_: t_.rearrange("p (k n) -> p k n", k=2)[:, :, 0:n]
                S.op("dve", [b("ps3"), b("cbias")], [b("hy")],
                     lambda e: e.tensor_tensor(v3(hy[:]), v3(pX), bc(cbias[:], 2, [128, 2, n]), ALU.add))
                S.op("dve", [b("hy")], [b("hy2")], lambda e: e.tensor_tensor(v3(hy2[:]), v3(hy[:]), v3(hy[:]), ALU.mult))
                S.op("dve", [b("hy2")], [b("hy2")],
                     lambda e: e.tensor_scalar(v3(hy2[:]), v3(hy2[:]), 0.044715, 1.0, ALU.mult, ALU.add))
                S.op("dve", [b("hy2"), b("hy")], [b("hy2")],
                     lambda e: e.tensor_tensor(v3(hy2[:]), v3(hy2[:]), v3(hy[:]), ALU.mult))
                yield
                gt = gts[par]

                def sig_fin(t, tok):
                    S.op("dve", [tok], [tok], lambda e: e.tensor_scalar(t, t, 1.0, None, ALU.add))
                    S.op("dve", [tok], [tok], lambda e: e.reciprocal(t, t))

                S.op("act", [b("hy2")], [b("hsg")],
                     lambda e: e.activation(v3(hsg[:]), v3(hy2[:]), AF.Exp, scale=-1.5957691216057308))
                S.op("act", [b("ps1")], [b(f"gts{par}")], lambda e: e.activation(gt[:], pGt, AF.Exp, scale=-1.0))
                S.op("act", [b("ps2")], [b("sgT")], lambda e: e.activation(sgT[:], psC[:, 384:512], AF.Exp, scale=-1.0))
                S.op("act", [b("ps1")], [b("sg")], lambda e: e.activation(sg[:], psB_[:, 0:128], AF.Exp, scale=-1.0))
                S.op("act", [b("ps2")], [b("qTf")], lambda e: e.activation(qTf[:], psC[:, 256:384], AF.Exp, scale=-1.0))
                S.op("act", [b("ps1")], [b("gsil")], lambda e: e.activation(gsil[:], psB_[:, 256:384], AF.Exp, scale=-1.0))
                S.op("act", [b("ps1")], [b("vbf")], lambda e: e.copy(vbf[:], psB_[:, 128:256]))
                yield
                def chain_a():
                    sig_fin(v3(hsg[:]), b("hsg"))
                    sig_fin(gt[:], b(f"gts{par}"))
                    S.op("dve", [b("hsg"), b("hy")], [b("hid")],
                         lambda e: e.tensor_tensor(v3(hid[:]), v3(hy[:]), v3(hsg[:]), ALU.mult))
                    yield
                    S.op("pe", [b("hid"), b("w2")], [b("ps3")],
                         lambda e: e.matmul(pY[0:n, :], lhsT=hid[:, 0:n], rhs=w2[:, 0:64], start=True, stop=True))
                    S.op("pe", [b("hid"), b("w2")], [b("ps3")],
                         lambda e: e.matmul(pZ[:, 0:n], lhsT=w2[:, 64:128], rhs=hid[:, 8:8 + n], start=True, stop=True))
                    S.op("act", [b("ps3")], [b("ksq")], lambda e: e.activation(ksq[0:n], pY[0:n, :], AF.Square))
                    S.op("dve", [b("ksq")], [b("kss")], lambda e: e.tensor_reduce(kss[0:n], ksq[0:n], AX.X, ALU.add))
                    S.op("act", [b("kss")], [b("kr")], lambda e: e.activation(kr[0:n], kss[0:n], AF.Ln, bias=epsc[0:n, 1:2]))
                    S.op("act", [b("kr")], [b("kr")], lambda e: e.activation(kr[0:n], kr[0:n], AF.Exp, scale=-0.5))
                    yield
                    S.op("dve", [b("ps3"), b("kr")], [b("kt1")],
                         lambda e: e.tensor_scalar(kt1[0:n], pY[0:n, :], kr[0:n, 0:1], 8.0, ALU.mult, ALU.mult))
                    S.op("dve", [b("kt1"), cst], [b("kcn")], lambda e: e.tensor_tensor(kcn[0:n], kt1[0:n], gkc[0:n], ALU.mult))
                    S.op("act", [b("ps3")], [b("vcT")], lambda e: e.copy(vcT[:, i0:i0 + n], pZ[:, 0:n]))
                    S.op("pe", [b("kcn"), b("ident")], [b("psb")],
                         lambda e: e.transpose(pKT[:, 0:n], kcn[0:n, :], ident[0:n, 0:n]))
                    S.op("act", [b("psb")], [b("kcT")], lambda e: e.copy(kcT[:, i0:i0 + n], pKT[:, 0:n]))
                    yield
                    for ct in range(i0 // 128, (i0 + n - 1) // 128 + 1):
                        S.op("pe", [b("vcT"), b("ident")], [b("psb")],
                             lambda e, ct=ct: e.transpose(pVT, vcT[:, ct * 128:(ct + 1) * 128], ident[0:64, 0:64]))
                        S.op("act", [b("psb")], [b("vcA")], lambda e, ct=ct: e.copy(vcA[:, ct, 0:64], pVT))
                    yield

                    nct = (8 * qb + 6) // 128 + 1
                    for ct in range(nct):
                        pc = Pc[ct % 2]
                        pcb = b(f"Pc{ct % 2}")
                        S.op("pe", [b("kcT"), b("qT4")], [b("ps2")],
                             lambda e, ct=ct: e.matmul(pSC[:, 0:512], lhsT=kcT[:, ct * 128:(ct + 1) * 128],
                                                       rhs=qT4[:].rearrange("p h t -> p (h t)"), start=True, stop=True))
                        S.op("act", [b("ps2")], [pcb], lambda e, pc=pc: e.activation(pc[:], pSC[:, 0:512], AF.Exp))
                        pcv = pc[:].rearrange("p (h q) -> p h q", h=4)
                        if 16 * (ct * 128 + 127) + 31 > t0:
                            S.op("pool", [pcb], [pcb],
                                 lambda e, pcv=pcv, ct=ct: e.affine_select(out=pcv, in_=pcv, pattern=[[0, 4], [1, 128]],
                                                                           compare_op=ALU.is_ge, fill=0.0,
                                                                           base=t0 - 2048 * ct - 31, channel_multiplier=-16))
                        yield
                        for r in range(4):
                            lw = pc[:, r * 128:(r + 1) * 128]
                            S.op("pe", [pcb, b("ov")], [imptok(r)],
                                 lambda e, r=r, lw=lw, ct=ct: e.matmul(impap(r), lhsT=lw, rhs=ovsb[:, ct, :],
                                                                       start=(ct == 0 and r == 0), stop=(ct == nct - 1),
                                                                       skip_group_check=True))
                            if r < 2:
                                S.op("pe", [pcb, b("vcA")], [b("ps3")],
                                     lambda e, r=r, lw=lw, ct=ct: e.matmul(ocap(r), lhsT=lw,
                                                                           rhs=vcA[:, ct, 0:64], start=(ct == 0 and r == 0),
                                                                           stop=(ct == nct - 1), skip_group_check=True))
                        yield
                    S.op("dve", [b("ps1")], [b("dn4")],
                         lambda e: e.tensor_scalar_max(dn4[:], ps[1][:, 0:512].rearrange("p (r j) -> p r j", r=4)[:, :, 0], 1e-30))
                    S.op("dve", [b("dn4")], [b("rd4")], lambda e: e.reciprocal(rd4[:], dn4[:]))
                    yield
                    S.op("dve", [b("rd4"), b(f"gts{par}")], [b("cfc")],
                         lambda e: e.tensor_tensor(cfc[:], rd4[:, 0:2], gt[:, 0:6:3], ALU.mult))
                    for h in range(2):
                        S.op("dve", [b("ps3"), b("cfc")], [b(f"ocmp{par}")],
                             lambda e, h=h: e.tensor_scalar(ocmp[par][:, h, :], ocap(h),
                                                            cfc[:, h:h + 1], None, ALU.mult))
                    yield
                    S.op("dve", [b("ps1"), b("rd4")], [b("imp")],
                         lambda e: e.tensor_scalar(imp[:], impap(0), rd4[:, 0:1], None, ALU.mult))
                    for r in range(1, 4):
                        S.op("dve", [imptok(r), b("rd4"), b("imp")], [b("imp")],
                             lambda e, r=r: e.scalar_tensor_tensor(imp[:], impap(r), rd4[:, r:r + 1], imp[:],
                                                                   ALU.mult, ALU.add))
                    yield
                    w0 = 128 - 2 * qb
                    S.op("dve", [b("imp"), cst], [b("imp")], lambda e: e.tensor_tensor(imp[:], imp[:], wa[:, w0:w0 + 128], ALU.mult))
                    S.op("dve", [b("imp"), cst], [b("imp")], lambda e: e.tensor_tensor(imp[:], imp[:], wb[:, w0:w0 + 128], ALU.add))
                    S.op("dve", [b("imp")], [b("imp")], lambda e: e.memset(imp[:, 0:1], 1.0e4))
                    yield
                    S.op("dve", [b("imp")], [b("m8a")], lambda e: e.max(m8a[:], imp[:]))
                    S.op("dve", [b("imp"), b("m8a")], [b("imp2")],
                         lambda e: e.match_replace(imp2[:], m8a[:], imp[:], NEG))
                    S.op("dve", [b("imp2")], [b("m8b")], lambda e: e.max(m8b[:], imp2[:]))
                    S.op("dve", [b("m8b")], [b("thr")], lambda e: e.tensor_reduce(thr[:], m8b[:], AX.X, ALU.min))
                    yield
                    S.op("dve", [b("imp"), b("thr")], [b("selsw")],
                         lambda e: e.tensor_scalar(selsw[:, 64:128], imp[:, 0:64], thr[:, 0:1], 1.0, ALU.is_ge, ALU.subtract))
                    S.op("dve", [b("imp"), b("thr")], [b("selsw")],
                         lambda e: e.tensor_scalar(selsw[:, 0:64], imp[:, 64:128], thr[:, 0:1], 1.0, ALU.is_ge, ALU.subtract))
                    S.op("pe", [b("selsw"), b("ident")], [b("psb")], lambda e: e.transpose(pT1, selsw[:], ident[:]))
                    S.op("act", [b("psb")], [b(f"QBb{par}0")],
                         lambda e: e.copy(QB[par][0][64:128], bc(pT1[64:128, :], 1, [64, 2, 128])))
                    yield
                    if qb >= 32:
                        S.op("dve", [b("imp"), b("thr")], [b("selb")],
                             lambda e: e.tensor_scalar(selb[:], imp[:], thr[:, 0:1], 1.0, ALU.is_ge, ALU.subtract))
                        S.op("pe", [b("selb"), b("ident")], [b("psb")], lambda e: e.transpose(pT1, selb[:], ident[:]))
                        S.op("act", [b("psb")], [b(f"QBb{par}1")],
                             lambda e: e.copy(QB[par][1][64:128], bc(pT1[64:128, :], 1, [64, 2, 128])))
                        yield


                def chain_b():
                    sig_fin(sgT[:], b("sgT"))
                    sig_fin(sg[:], b("sg"))
                    yield
                    sig_fin(qTf[:], b("qTf"))
                    S.op("dve", [b("qTf"), b("ps2")], [b("qTf")],
                         lambda e: e.tensor_tensor(qTf[:], psC[:, 256:384], qTf[:], ALU.mult))
                    sig_fin(gsil[:], b("gsil"))
                    S.op("dve", [b("gsil"), b("ps1")], [b("gsil")],
                         lambda e: e.tensor_tensor(gsil[:], psB_[:, 256:384], gsil[:], ALU.mult))
                    yield
                    S.op("dve", [b("sgT"), b("nomlc"), b("omlc")], [b("kTf")],
                         lambda e: e.tensor_scalar(kTf[:], sgT[:], nomlc[:, 0:1], omlc[:, 0:1], ALU.mult, ALU.add))
                    S.op("dve", [b("sg"), b("omlB")], [b("ff")], lambda e: e.tensor_tensor(ff[:], sg[:], omlB[:], ALU.mult))
                    S.op("dve", [b("ff"), b("lbB")], [b("ff")], lambda e: e.tensor_tensor(ff[:], ff[:], lbB[:], ALU.add))
                    yield
                    S.op("act", [b("ff")], [b("logf")], lambda e: e.activation(logf[:], ff[:], AF.Ln))
                    S.op("dve", [b("ff")], [b("ktm")],
                         lambda e: e.tensor_scalar(ktm[:], ff[:], -1.0, 1.0, ALU.mult, ALU.add))
                    yield
                    pH = ps[0]
                    S.op("pe", [b("logf"), cst], [b("ps0")],
                         lambda e: e.matmul(pH[:, 0:128], lhsT=logf[:], rhs=tri[:], start=True, stop=True))
                    S.op("pe", [b("logf"), cst], [b("ps0")],
                         lambda e: e.matmul(pH[:, 128:256], lhsT=tri2t[:], rhs=logf[:], start=True, stop=True))
                    yield
                    S.op("act", [b("ps0")], [b("ebT")], lambda e: e.activation(ebT[:], pH[:, 0:128], AF.Exp))
                    S.op("act", [b("ps0")], [b("enbT")], lambda e: e.activation(enbT[:], pH[:, 0:128], AF.Exp, scale=-1.0))
                    S.op("act", [b("ps0")], [b("eblmb")], lambda e: e.activation(eblmb[:], pH[:, 128:256], AF.Exp))
                    yield
                    S.op("dve", [b("qTf"), b("ebT")], [b("QpT")], lambda e: e.tensor_tensor(QpT[:], qTf[:], ebT[:], ALU.mult))
                    S.op("dve", [b("kTf"), b("enbT")], [b("KpT")], lambda e: e.tensor_tensor(KpT[:], kTf[:], enbT[:], ALU.mult))
                    for c in range(2):
                        r0 = c * 64
                        S.op("dve", [b("ktm"), b("eblmb")], [b(f"Kppz{c}")],
                             lambda e, c=c, r0=r0: e.tensor_tensor(Kppz[c][r0:r0 + 64, :], ktm[r0:r0 + 64, :],
                                                                   eblmb[r0:r0 + 64, :], ALU.mult))
                        S.op("pool", [b("QpT")], [b(f"QpTz{c}")],
                             lambda e, c=c, r0=r0: e.tensor_copy(QpTz[c][:, r0:r0 + 64], QpT[:, r0:r0 + 64]))
                    yield
                    S.op("pe", [b("KpT"), b("QpT")], [b("ps0")],
                         lambda e: e.matmul(pH[:, 256:384], lhsT=KpT[:], rhs=QpT[:], start=True, stop=True))
                    S.op("dve", [b("ps0"), cst], [b("attnT")],
                         lambda e: e.tensor_tensor(attnT[:], pH[:, 256:384], tri[:], ALU.mult))
                    pHo = pH[:, 384:512]
                    pSu = pH[:, 0:128]
                    S.op("pe", [b("Kppz0"), b("vbf")], [b("ps0")],
                         lambda e: e.matmul(pSu, lhsT=Kppz[0][:], rhs=vbf[:], start=True, stop=True))
                    yield
                    S.op("dve", [b("ps0"), b("ebT"), b("Sst")], [b("Sst")],
                         lambda e: e.scalar_tensor_tensor(Sst[:], Sst[:], ebT[:, 63:64], pSu, ALU.mult, ALU.add))
                    S.op("act", [b("Sst")], [b("Sbf1")], lambda e: e.copy(Sbf[1][:], Sst[:]))
                    yield
                    S.op("pe", [b("attnT"), b("vbf")], [b("ps0")],
                         lambda e: e.matmul(pHo, lhsT=attnT[:], rhs=vbf[:], start=True, stop=False))
                    S.op("pe", [b("QpTz0"), b("Sbf0")], [b("ps0")],
                         lambda e: e.matmul(pHo, lhsT=QpTz[0][:], rhs=Sbf[0][:], start=False, stop=False))
                    S.op("pe", [b("QpTz1"), b("Sbf1")], [b("ps0")],
                         lambda e: e.matmul(pHo, lhsT=QpTz[1][:], rhs=Sbf[1][:], start=False, stop=True))
                    yield
                    S.op("pool", [], [b("sso")], lambda e: e.memset(sso[:], 0.0))
                    S.op("act", [b("ps0"), b("sso")], [b("o2"), b("sso")],
                         lambda e: e.activation(o2[:], pHo, AF.Square, accum_out=sso[:]))
                    S.op("act", [b("sso")], [b("ro")], lambda e: e.activation(ro[:], sso[:], AF.Ln, bias=epsc[:, 2:3]))
                    S.op("act", [b("ro")], [b("ro")], lambda e: e.activation(ro[:], ro[:], AF.Exp, scale=-0.5))
                    S.op("dve", [b("ps0"), b("ro")], [b("o1")],
                         lambda e: e.tensor_scalar(o1[:], pHo, ro[:, 0:1], float(np.sqrt(128.0)), ALU.mult, ALU.mult))
                    yield
                    S.op("pe", [b("Kppz1"), b("vbf")], [b("ps0")],
                         lambda e: e.matmul(pSu, lhsT=Kppz[1][:], rhs=vbf[:], start=True, stop=True))
                    S.op("dve", [b("ps0"), b("ebT"), b("Sst")], [b("Sst")],
                         lambda e: e.scalar_tensor_tensor(Sst[:], Sst[:], ebT[:, 127:128], pSu, ALU.mult, ALU.add))
                    S.op("act", [b("Sst")], [b("Sbf0")], lambda e: e.copy(Sbf[0][:], Sst[:]))
                    yield
                    S.op("dve", [b("o1"), cst], [b("o1")], lambda e: e.tensor_tensor(o1[:], o1[:], gon[:], ALU.mult))
                    S.op("dve", [b("o1"), b("gsil")], [b(f"mixh{par}")],
                         lambda e: e.tensor_tensor(mix[par][:, 128:256], o1[:], gsil[:], ALU.mult))
                    yield


                def chain_b2():
                    yield from chain_b()
                    if hoisted(qb + 1):
                        yield from xstage(qb + 1)
                    elif qb + 1 < N_TILES:
                        yield from xstage_a(qb + 1)

                yield from interleave(chain_a(), chain_b2())

            pbctr = [0]

            def backend(qb, gen):
                par = qb % 2
                t0 = qb * 128
                jobs = [("slc", kt) for kt in range(qb + 1)] + [("win", kt) for kt in range(max(0, qb - 4), qb + 1)]
                pairs = [jobs[i_:i_ + 2] for i_ in range(0, len(jobs), 2)]
                nsteps = max(1, int(FE_STEPS / max(1, len(pairs)) + 0.5))

                def advance(k):
                    if gen is None:
                        return
                    for _ in range(k):
                        try:
                            next(gen)
                        except StopIteration:
                            return

                def score(pair, slot):
                    for j_, (kind, kt) in enumerate(pair):
                        pout = psSW2[slot][:, j_ * 256:(j_ + 1) * 256]
                        if kind == "slc":
                            hf_ = kt // 32
                            S.op("pe", [b(f"KEk{kt}"), b("KEe"), b(f"QBq{par}{hf_}"), b(f"QBb{par}{hf_}")], [b(swtok[slot])],
                                 lambda e, pout=pout, kt=kt, hf_=hf_: e.matmul(
                                     pout, lhsT=KE[:, kt * 128:(kt + 1) * 128],
                                     rhs=QB[par][hf_][:].rearrange("p h t -> p (h t)"), start=True, stop=True))
                        else:
                            S.op("pe", [b(f"kwT{kt}"), b(f"QBq{par}0")], [b(swtok[slot])],
                                 lambda e, pout=pout, kt=kt: e.matmul(
                                     pout, lhsT=kwT[:, kt * 128:(kt + 1) * 128],
                                     rhs=QB[par][0][0:64].rearrange("p h t -> p (h t)"), start=True, stop=True))

                score(pairs[0], 0)
                first_pv = [True]
                for pi, pair in enumerate(pairs):
                    if pi + 1 < len(pairs):
                        score(pairs[pi + 1], (pi + 1) % 2)
                    pslot = pbctr[0] % 3
                    pbctr[0] += 1
                    slot_ps = pi % 2
                    pbt = b(f"Pb{pslot}")
                    pbuf = Pb[pslot]
                    w_ = 256 * len(pair)
                    S.op("act", [b(swtok[slot_ps])], [pbt],
                         lambda e: e.activation(pbuf[:, 0:w_], psSW2[slot_ps][:, 0:w_], AF.Exp))
                    for j_, (kind, kt) in enumerate(pair):
                        pv = pbuf[:, j_ * 256:(j_ + 1) * 256].rearrange("p (h q) -> p h q", h=2)
                        if kt == qb:
                            S.op("pool", [pbt], [pbt],
                                 lambda e, pv=pv: e.affine_select(out=pv, in_=pv, pattern=[[0, 2], [1, 128]],
                                                                  compare_op=ALU.is_ge, fill=0.0, base=0, channel_multiplier=-1))
                        elif kind == "win" and kt == qb - 4:
                            S.op("pool", [pbt], [pbt],
                                 lambda e, pv=pv: e.affine_select(out=pv, in_=pv, pattern=[[0, 2], [-1, 128]],
                                                                  compare_op=ALU.is_ge, fill=0.0, base=-1, channel_multiplier=1))
                    if PIPELINE:
                        advance(nsteps)
                    for j_, (kind, kt) in enumerate(pair):
                        branch = 0 if kind == "slc" else 1
                        vA, vname = (vsA, "vsA") if kind == "slc" else (vwA, "vwA")
                        for h in range(2):
                            c0 = (branch * 2 + h) * 65
                            st_ = first_pv[0]
                            first_pv[0] = False
                            S.op("pe", [pbt, b(f"{vname}{kt}"), b(vname + "ones")], [b("ps6")],
                                 lambda e, h=h, c0=c0, j_=j_, kt=kt, vA=vA, st_=st_: e.matmul(
                                     psO[:, c0:c0 + 65], lhsT=pbuf[:, j_ * 256 + h * 128:j_ * 256 + (h + 1) * 128],
                                     rhs=vA[:, kt, :], start=st_, stop=(kt == qb), skip_group_check=True))

                gt = gts[par]
                pOv = psO[:, 0:260].rearrange("p (x c) -> p x c", c=65)
                S.op("dve", [b("ps6")], [b("dn6")], lambda e: e.tensor_scalar_max(dn6[:, 0:4], pOv[:, :, 64], 1e-30))
                S.op("dve", [b("dn6")], [b("rd6")], lambda e: e.reciprocal(rd6[:, 0:4], dn6[:, 0:4]))
                S.op("dve", [b("rd6"), b(f"gts{par}")], [b("coef")],
                     lambda e: e.tensor_tensor(coef[:, 0:4].rearrange("p (b h) -> p b h", b=2),
                                               rd6[:, 0:4].rearrange("p (b h) -> p b h", b=2),
                                               gt[:].rearrange("p (h b) -> p b h", h=2)[:, 1:3, :], ALU.mult))
                for h in range(2):
                    S.op("dve", [b("ps6"), b("coef"), b(f"ocmp{par}")], [b("oacc")],
                         lambda e, h=h: e.scalar_tensor_tensor(oacc[:], psO[:, h * 65:h * 65 + 64],
                                                               coef[:, h:h + 1], ocmp[par][:, h, :], ALU.mult, ALU.add))
                    S.op("dve", [b("ps6"), b("coef"), b("oacc")], [b(f"mixn{par}")],
                         lambda e, h=h: e.scalar_tensor_tensor(mix[par][:, h * 64:(h + 1) * 64],
                                                               psO[:, (2 + h) * 65:(2 + h) * 65 + 64],
                                                               coef[:, 2 + h:3 + h], oacc[:], ALU.mult, ALU.add))
                pM = psb[:, 0:256]
                for k in range(2):
                    S.op("pe", [b(f"mixn{par}"), b(f"mixh{par}"), b("ident")], [b("psb")],
                         lambda e, k=k: e.transpose(pM[:, k * 128:(k + 1) * 128], mix[par][:, k * 128:(k + 1) * 128], ident[:]))
                S.op("act", [b("psb")], [b("mixT")],
                     lambda e: e.copy(mixT[:], pM.rearrange("p (k t) -> p k t", k=2)))
                cj, cc0 = qb // 16, (qb % 16) * 128
                S.dma("sp", "mx", mixsrc.ap()[cj * 256:(cj + 1) * 256, cc0:cc0 + 128].rearrange("(k p) t -> p k t", p=128),
                      mixT[:], [b("mixT")], [mixsrc_b])
                advance(10 ** 6)

            g0 = frontend(0)
            nst_ = 0
            for _ in g0:
                nst_ += 1
                if nst_ >= STOP_AT:
                    break
            print("front-end steps", nst_)
            if STOP_AT >= 10 ** 8:
                for qb in range(N_TILES):
                    backend(qb, frontend(qb + 1) if qb + 1 < N_TILES else None)

            if DEBUG:
                S.dma("sp", "dbg", dbg.ap()[:, :], mixsrc.ap()[:, :], [mixsrc_b], [Buf("dbg")])
            S.barrier()

        if RUN_CC:
            for cj in range(4):
                S.collective("cc", [mixsrc_b], [mixall_b],
                             lambda e, cj=cj: e.collective_compute(
                                 "AllGather", ALU.bypass, replica_groups=[[0, 1, 2, 3], [4, 5, 6, 7]],
                                 ins=[mixsrc.ap()[cj * 256:(cj + 1) * 256, :].opt()],
                                 outs=[mixall.ap()[cj * 1024:(cj + 1) * 1024, :].opt()]))

        if not RUN_P2:
            S.dma("sp", "y0", y.ap()[0:128, 0:8], x2.ap()[0:128, 0:8], [], [Buf("yd")])
            S.barrier(["sp"])
            return nc
        with ExitStack() as es:
            def sb(name, shape, dt=F32):
                return es.enter_context(nc.sbuf_tensor(name, list(shape), dt))

            B = {}

            def b(name):
                if name not in B:
                    B[name] = Buf(name)
                return B[name]

            epsc = sb("epsc2", [128, 3])
            for v_, i_ in EPSI.items():
                S.op("pool", [], [b("epsc")], lambda e, v_=v_, i_=i_: e.memset(epsc[:, i_:i_ + 1], v_))
            S.barrier(["act"])
            wo = sb("wo", [128, 8, D], BF16)
            wd = sb("wd", [128, NFF, D], BF16)
            ident2 = sb("ident2", [128, 128], BF16)
            nfcol = sb("nfcol_s", [128, 8])
            WCH = 4
            NSC = -(-NFF // WCH)
            wg = [sb(f"wg{i}", [128, 8, 2, WCH * 128], BF16) for i in range(3)]
            mT = sb("mT", [128, 8, 512], BF16)
            x1 = sb("x1", [128, 4, D])
            hn2 = [sb(f"hn{i}", [128, D], BF16) for i in range(2)]
            hT = sb("hT", [128, 8, 512], BF16)
            actT = sb("actT", [128, NFF, 512], BF16)
            sgl = [sb(f"sgl{i}", [128, 512]) for i in range(2)]
            yo = [sb(f"yo{i}", [128, D]) for i in range(2)]
            sqj2 = sb("sqj2", [128, D])
            ss2 = sb("ss2", [128, 1])
            r2 = sb("r2", [128, 1])

            S.dma("pool", "p2w0", wo[:], w_out_p.ap().rearrange("(k p) n -> p k n", p=128), [], [b("wo")])
            S.dma("pool", "p2w1", ident2[:], c_ident.ap(), [], [b("ident2")])
            S.dma("sp", "p2w2", nfcol[:], nfcol_d.ap(), [], [b("nfcol")])
            for j0 in range(0, NFF, 2):
                S.dma("pool", "p2w3", wd[:, j0:j0 + 2, :],
                      w_dn.ap().rearrange("(j p) n -> p j n", p=128)[:, j0:j0 + 2, :], [], [b("wd")])

            pid = nc.sync.partition_id()
            rank = pid % 4
            wgu_v = w_gu.ap().rearrange("(k p) n -> p k n", p=128)
            gctr = [0]

            def load_wg():
                n_ = gctr[0]
                if n_ >= 4 * NSC:
                    return
                gctr[0] += 1
                i, sc = n_ % 3, n_ % NSC
                j0 = sc * WCH * 128
                w_ = min(WCH * 128, DFF - j0)
                S.dma("pool", f"wg{i}", wg[i][:, :, 0, 0:w_], wgu_v[:, :, j0:j0 + w_], [], [b(f"wg{i}")])
                S.dma("pool", f"wg{i}", wg[i][:, :, 1, 0:w_], wgu_v[:, :, DFF + j0:DFF + j0 + w_], [], [b(f"wg{i}")])

            load_wg()
            load_wg()
            wuse = [0]

            pOP = [ps[0], ps[1]]
            pG = [ps[2], ps[3]]
            pU = [ps[4], ps[5]]
            octr = [0]
            yctr = [0]
            for g in range(4):
                S.dma("sp", "mT", mT[:],
                      mixall.ap().rearrange("(jk p) t -> p jk t", p=128)[:, ds(rank * 8, 8), g * 512:(g + 1) * 512],
                      [mixall_b], [b("mT")])
                S.dma("sp", "x1", x1[:], x2.ap()[g * 512:(g + 1) * 512, :].rearrange("(a p) d -> p a d", p=128),
                      [], [b("x1")])
                def stage_a(a):
                    for hf_ in range(2):
                        po = pOP[octr[0] % 2]
                        pob = b(f"pOP{octr[0] % 2}")
                        octr[0] += 1
                        for k in range(8):
                            S.op("pe", [b("mT"), b("wo")], [pob],
                                 lambda e, k=k, po=po, hf_=hf_: e.matmul(po[:, 0:512], lhsT=mT[:, k, a * 128:(a + 1) * 128],
                                                                         rhs=wo[:, k, hf_ * 512:(hf_ + 1) * 512],
                                                                         start=(k == 0), stop=(k == 7)))
                        S.op("dve", [pob, b("x1")], [b("x1")],
                             lambda e, po=po, hf_=hf_: e.tensor_tensor(x1[:, a, hf_ * 512:(hf_ + 1) * 512], po[:, 0:512],
                                                                       x1[:, a, hf_ * 512:(hf_ + 1) * 512], ALU.add))
                    hn_ = hn2[a % 2]
                    S.op("pool", [], [b("ss2")], lambda e: e.memset(ss2[:], 0.0))
                    S.op("act", [b("x1"), b("ss2")], [b("sqj2"), b("ss2")],
                         lambda e: e.activation(sqj2[:], x1[:, a, :], AF.Square, accum_out=ss2[:]))
                    S.op("act", [b("ss2")], [b("r2")],
                         lambda e: e.activation(r2[:], ss2[:], AF.Ln, bias=epsc[:, 0:1]))
                    S.op("act", [b("r2")], [b("r2")],
                         lambda e: e.activation(r2[:], r2[:], AF.Exp, scale=-0.5))
                    S.op("dve", [b("x1"), b("r2")], [b(f"hn{a % 2}")],
                         lambda e: e.tensor_scalar(hn_[:], x1[:, a, :], r2[:, 0:1], 32.0, ALU.mult, ALU.mult))

                def stage_b(a):
                    hn_ = hn2[a % 2]
                    for k in range(8):
                        S.op("pe", [b(f"hn{a % 2}"), b("ident2")], [b("psb")],
                             lambda e, k=k: e.transpose(psb[:, k * 128:(k + 1) * 128], hn_[:, k * 128:(k + 1) * 128], ident2[:]))
                    S.op("dve", [b("psb"), b("nfcol")], [b("hT")],
                         lambda e: e.tensor_tensor(hT[:, :, a * 128:(a + 1) * 128],
                                                   psb[:, 0:1024].rearrange("p (k t) -> p k t", k=8),
                                                   bc(nfcol[:], 2, [128, 8, 128]), ALU.mult))

                stage_a(0)
                for a in range(1, 4):
                    stage_a(a)
                    stage_b(a - 1)
                stage_b(3)
                for j in range(NFF):
                    if j % WCH == 0:
                        i = wuse[0] % 3
                        wuse[0] += 1
                        load_wg()
                    jj = j % WCH
                    pg, pu = pG[j % 2], pU[j % 2]
                    pgb, pub = b(f"pG{j % 2}"), b(f"pU{j % 2}")
                    for k in range(8):
                        S.op("pe", [b("hT"), b(f"wg{i}")], [pgb],
                             lambda e, k=k: e.matmul(pg[:, 0:512], lhsT=wg[i][:, k, 0, jj * 128:(jj + 1) * 128], rhs=hT[:, k, :],
                                                     start=(k == 0), stop=(k == 7)))
                    for k in range(8):
                        S.op("pe", [b("hT"), b(f"wg{i}")], [pub],
                             lambda e, k=k: e.matmul(pu[:, 0:512], lhsT=wg[i][:, k, 1, jj * 128:(jj + 1) * 128], rhs=hT[:, k, :],
                                                     start=(k == 0), stop=(k == 7)))
                    sgt = sgl[j % 2]
                    S.op("act", [pgb], [b(f"sgl{j % 2}")], lambda e: e.activation(sgt[:], pg[:, 0:512], AF.Silu))
                    S.op("dve", [b(f"sgl{j % 2}"), pub], [b("actT")],
                         lambda e: e.tensor_tensor(actT[:, j, :], sgt[:], pu[:, 0:512], ALU.mult))
                for a in range(4):
                    yt = yo[yctr[0] % 2]
                    ytb = b(f"yo{yctr[0] % 2}")
                    ych = f"y{yctr[0] % 2}"
                    yctr[0] += 1
                    for hf_ in range(2):
                        po = pOP[octr[0] % 2]
                        pob = b(f"pOP{octr[0] % 2}")
                        octr[0] += 1
                        for j in range(NFF):
                            S.op("pe", [b("actT"), b("wd")], [pob],
                                 lambda e, j=j, po=po: e.matmul(po[:, 0:512], lhsT=actT[:, j, a * 128:(a + 1) * 128],
                                                                rhs=wd[:, j, hf_ * 512:(hf_ + 1) * 512],
                                                                start=(j == 0), stop=(j == NFF - 1)))
                        S.op("dve", [pob, b("x1")], [ytb],
                             lambda e, po=po: e.tensor_tensor(yt[:, hf_ * 512:(hf_ + 1) * 512], po[:, 0:512],
                                                              x1[:, a, hf_ * 512:(hf_ + 1) * 512], ALU.add))
                    r0 = g * 512 + a * 128
                    S.dma("sp", ych, y.ap()[r0:r0 + 128, :], yt[:], [ytb], [b("ydram")])
            S.barrier(["sp"])
        print("instr counts", S.cnt, "waits", S.nwait, "sems", 5 + len(S.chsem))
    return nc


def _consts():
    ident = np.eye(128, dtype=np.float32)
    s = np.arange(128)
    same = (s[:, None] // 64) == (s[None, :] // 64)
    tri = (same & (s[:, None] <= s[None, :])).astype(np.float32)
    tri2t = (same & (s[:, None] > s[None, :])).astype(np.float32)
    i = np.arange(512)[:, None]
    j = np.arange(128)[None, :]
    ov = ((i * 16 < j * 64 + 64) & (i * 16 + 32 > j * 64)).astype(np.float32)
    ov[:, 0] = 1.0
    q = np.arange(128)[:, None]
    xcol = np.arange(256)[None, :]
    jrel = xcol - 128 - (q >= 64)
    wa = (jrel < -1).astype(np.float32)
    wb = np.where(jrel > 0, -1.0, np.where(jrel >= -1, 1.0e4, 0.0)).astype(np.float32)
    key = np.arange(T)[None, :]
    e = (((key // 64) % 64) == np.arange(64)[:, None]).astype(np.float32) * 30000.0
    return dict(c_ident=ident, c_tri=tri, c_tri2t=tri2t, c_ov=ov, c_wa=wa, c_wb=wb, c_e=e)


def make_in_maps(x, norm_mix, w_in, q_norm, k_norm, cmp_pos_k, cmp_pos_v, cmp_k_w1, cmp_k_w2, cmp_v_w1, cmp_v_w2,
                 hgrn_lb_logits, hgrn_o_norm, w_out, norm_ffn, w_gate_up, w_down):
    f = np.float32
    x = np.asarray(x, f)
    w_in0 = np.asarray(w_in, f)[0]
    consts = _consts()
    offs = np.cumsum([0, 512, 128, 128, 128, 128, 128, 128, 24, 512, 512, 512, 512])
    o_q, o_kc, o_vc, o_ks, o_vs, o_kw, o_vw, o_g, o_hq, o_hf, o_hi, o_hg = [int(v) for v in offs[:12]]
    rep = lambda v, n=128: np.ascontiguousarray(np.broadcast_to(np.asarray(v, f)[None, :], (n, len(v))))
    qn = np.asarray(q_norm, f)[0]
    kn = np.asarray(k_norm, f)[0]
    gq6 = np.concatenate([rep(qn)] * 4 + [rep(kn[1]), rep(kn[2])], axis=1)
    col8 = lambda v: np.ascontiguousarray(np.asarray(v, f).reshape(8, 128).T)
    lbl = np.asarray(hgrn_lb_logits, f)
    w_out0 = np.asarray(w_out, f)[0]
    in_maps = []
    for c in range(8):
        bi, s = c // 4, c % 4
        g = s // 2
        own = [2 * s, 2 * s + 1]
        oth = [h for h in range(4 * g, 4 * g + 4) if h not in own]
        hc = lambda o, h, w: list(range(o + h * w, o + (h + 1) * w))
        cols_tm = []
        for h in own + oth:
            cols_tm += hc(o_q, h, 64)
        cols_tm += hc(o_ks, g, 64) + hc(o_kw, g, 64) + hc(o_vs, g, 64) + hc(o_vw, g, 64)
        cols_tm += hc(o_hf, s, 128) + hc(o_hi, s, 128) + hc(o_hg, s, 128)
        for h in own:
            cols_tm += hc(o_g, h, 3)
        cols_fm = hc(o_kc, g, 64) + hc(o_vc, g, 64) + hc(o_hq, s, 128) + hc(o_hf, s, 128)
        hs = slice(s * 128, (s + 1) * 128)
        rows_out = []
        for s2 in range(4):
            rows_out += list(range(2 * s2 * 64, (2 * s2 + 2) * 64)) + list(range(512 + s2 * 128, 512 + (s2 + 1) * 128))
        m = dict(consts)
        m.update(
            x_b=np.ascontiguousarray(x[bi]),
            x2=np.ascontiguousarray(x[bi, s * 2048:(s + 1) * 2048]),
            w_tm=np.ascontiguousarray(w_in0[:, cols_tm]),
            w_fm=np.ascontiguousarray(w_in0[:, cols_fm]),
            w1k=np.asarray(cmp_k_w1, f)[0], w1v=np.asarray(cmp_v_w1, f)[0],
            w2kv=np.ascontiguousarray(np.concatenate([np.asarray(cmp_k_w2, f)[0], np.asarray(cmp_v_w2, f)[0]], axis=1)),
            posT=np.ascontiguousarray(np.concatenate([np.asarray(cmp_pos_k, f)[0].T, np.asarray(cmp_pos_v, f)[0].T], axis=0)),
            gq6=gq6, gkc=rep(kn[0]), gon=rep(np.asarray(hgrn_o_norm, f)[0]),
            nmcol=col8(np.asarray(norm_mix, f)[0]),
            lbrow=np.concatenate([rep(lbl[0, hs]), rep(lbl[1, hs])], axis=1),
            lbc=np.ascontiguousarray(np.stack([lbl[0, hs], lbl[1, hs]], axis=1)),
            w_out_p=np.ascontiguousarray(w_out0[rows_out]),
            w_gu=np.asarray(w_gate_up, f)[0], w_dn=np.asarray(w_down, f)[0],
            nfcol=col8(np.asarray(norm_ffn, f)[0]),
        )
        in_maps.append(m)
    return in_maps


def kernel(**inputs):
    in_maps = make_in_maps(**inputs)
    nc = build_nc()
    res = run_bass_kernel_spmd(nc, in_maps, core_ids=list(range(8)))
    out = np.empty((2, T, D), np.float32)
    for c in range(8):
        bi, s = c // 4, c % 4
        out[bi, s * 2048:(s + 1) * 2048] = res.results[c]["y"]
    return out
```
